# Optimizing a Trainium2 kernel written in Bass

```python
import math
import jax, jax.numpy as jnp
from jax import lax
import numpy as np


D_MODEL = 1024
BATCH = 8
SEQ = 2048
DEPTH = 2
DEC_BATCH = 128
DEC_SEQ = 1
PAST_LEN = 16384
PAGE_SIZE = 128

MIX_WIDTH = D_MODEL
W_A = MIX_WIDTH // 4
W_B = MIX_WIDTH // 4
W_C = MIX_WIDTH // 4
W_D = MIX_WIDTH - W_A - W_B - W_C
A_HEADS = 4
A_HEAD_DIM = W_A // A_HEADS
CHUNK = 128
POOL_WINDOWS = (2, 4, 8, 16)
POOL_GROUP = W_B // len(POOL_WINDOWS)
POOL_BUF = max(POOL_WINDOWS) - 1
CONV_W = 3
D_HEAD_DIM = 64
D_HEADS = W_D // D_HEAD_DIM
R_DECAY = 32
R_AAA = 32
R_GATE = 64
D_PROJ = 3 * W_D + R_DECAY + R_AAA + R_GATE
PROJ = 2 * W_A + W_B + 3 * W_C + D_PROJ
MEM_LEN = 256
X_HEADS = 4
X_HEAD_DIM = D_MODEL // X_HEADS
D_FF = int(math.ceil(8 * D_MODEL / 3 / 256)) * 256
ALPHA = (2 * DEPTH) ** 0.25
BETA = (8 * DEPTH) ** -0.25
LN_EPS = 1e-5
GN_EPS = 64e-5

kernel_name = 'hymba_style_chunkmlp_pool_conv_rwkv7_decoder_step'


def layer_norm(x, g, b, eps=LN_EPS):
    xf = x.astype(jnp.float32)
    mu = jnp.mean(xf, axis=-1, keepdims=True)
    var = jnp.mean(jnp.square(xf - mu), axis=-1, keepdims=True)
    return ((xf - mu) * lax.rsqrt(var + eps)).astype(x.dtype) * g + b


def chunk_spatial_gate(v, ws, bs):
    bn, t, _ = v.shape
    n_chunks = -(-t // CHUNK)
    vp = jnp.pad(v, ((0, 0), (0, n_chunks * CHUNK - t), (0, 0)))
    vp = vp.reshape(bn, n_chunks, CHUNK, A_HEADS, A_HEAD_DIM)
    ws_causal = ws * jnp.tril(jnp.ones((CHUNK, CHUNK), ws.dtype))
    z = jnp.einsum('hts,bcshd->bcthd', ws_causal, vp) + jnp.swapaxes(bs, 0, 1)[None, None, :, :, None]
    return z.reshape(bn, n_chunks * CHUNK, W_A)[:, :t]


def multiscale_pool(xb, buf, pos0, w_pool, scale):
    t = xb.shape[1]
    x_ext = jnp.concatenate([buf, xb], axis=1)
    cs = jnp.cumsum(x_ext.astype(jnp.float32), axis=1)
    cs = jnp.pad(cs, ((0, 0), (1, 0), (0, 0)))
    end = cs[:, POOL_BUF + 1:]
    pos = pos0 + jnp.arange(t)
    outs = []
    for gi, win in enumerate(POOL_WINDOWS):
        sl = slice(gi * POOL_GROUP, (gi + 1) * POOL_GROUP)
        start = cs[:, POOL_BUF + 1 - win:POOL_BUF + 1 - win + t, sl]
        cnt = jnp.minimum(win, pos + 1).astype(jnp.float32)[None, :, None]
        d = ((end[..., sl] - start) / cnt).astype(xb.dtype) - xb[..., sl]
        outs.append(d @ w_pool[gi])
    y = jnp.concatenate(outs, axis=-1) * scale
    return y, x_ext[:, -POOL_BUF:]


def short_conv(xc, buf, conv_w):
    bg, cg, xin = jnp.split(xc, 3, axis=-1)
    z = cg * xin
    t = z.shape[1]
    z_ext = jnp.concatenate([buf, z], axis=1)
    y = conv_w[0] * z_ext[:, 0:t]
    for j in range(1, CONV_W):
        y = y + conv_w[j] * z_ext[:, j:j + t]
    return bg * y, z_ext[:, -(CONV_W - 1):]


def wkv_scan(s0, r, w, k, v, kk, a):
    def step(s, inp):
        r_t, w_t, k_t, v_t, kk_t, a_t = inp
        sa = jnp.einsum('bhvk,bhk->bhv', s, -kk_t)
        s = s * w_t[:, :, None, :] + sa[..., None] * (kk_t * a_t)[:, :, None, :] + v_t[..., None] * k_t[:, :, None, :]
        return s, jnp.einsum('bhvk,bhk->bhv', s, r_t)
    tm = lambda z: jnp.swapaxes(z, 0, 1)
    s, o = lax.scan(step, s0, (tm(r), tm(w), tm(k), tm(v), tm(kk), tm(a)))
    return tm(o), s


def rwkv7_mix(pd, shift_buf, s0, p):
    bn, t, _ = pd.shape
    prev = jnp.concatenate([shift_buf, pd[:, :-1]], axis=1)
    xs = pd + (prev - pd) * p['mu_d']
    r, k, v, dw, da, dg = jnp.split(xs, (W_D, 2 * W_D, 3 * W_D, 3 * W_D + R_DECAY, 3 * W_D + R_DECAY + R_AAA), axis=-1)
    w_log = -jax.nn.softplus(-(p['rwkv_w0'] + jnp.tanh(dw) @ p['rwkv_w2']).astype(jnp.float32)) - 0.5
    decay = jnp.exp(-jnp.exp(w_log))
    a = jax.nn.sigmoid(p['rwkv_a0'] + da @ p['rwkv_a2'])
    g = jax.nn.sigmoid(dg) @ p['rwkv_g2']
    heads = lambda z: z.reshape(bn, t, D_HEADS, D_HEAD_DIM).astype(jnp.float32)
    kk = heads(k * p['rwkv_k_k'])
    kk = kk * lax.rsqrt(jnp.maximum(jnp.sum(kk * kk, axis=-1, keepdims=True), 1e-12))
    k = k * (1 + (a - 1) * p['rwkv_k_a'])
    rh, kh, vh, ah, wh = heads(r), heads(k), heads(v), heads(a), heads(decay)
    o, s = wkv_scan(s0.astype(jnp.float32), rh, wh, kh, vh, kk, ah)
    mu = jnp.mean(o, axis=-1, keepdims=True)
    var = jnp.mean(jnp.square(o - mu), axis=-1, keepdims=True)
    o = ((o - mu) * lax.rsqrt(var + GN_EPS)).reshape(bn, t, W_D) * p['rwkv_lnx_g'] + p['rwkv_lnx_b']
    bonus = jnp.sum(rh * kh * p['rwkv_r_k'], axis=-1, keepdims=True) * vh
    o = (o + bonus.reshape(bn, t, W_D)) * g
    return o.astype(pd.dtype), pd[:, -1:], s


def hybrid_mixer(h, pos0, pool_buf, conv_buf, shift_buf, wkv_state, p):
    proj = h @ p['w_in']
    pa, pb, pc, pd = jnp.split(proj, (2 * W_A, 2 * W_A + W_B, 2 * W_A + W_B + 3 * W_C), axis=-1)
    u, v = jnp.split(jax.nn.gelu(pa), 2, axis=-1)
    v = layer_norm(v, p['ln_v_g'], p['ln_v_b'])
    ya = u * chunk_spatial_gate(v, p['ws_chunk'], p['b_chunk'])
    yb, pool_buf = multiscale_pool(pb, pool_buf, pos0, p['w_pool'], p['pool_scale'])
    yc, conv_buf = short_conv(pc, conv_buf, p['conv_w'])
    yd, shift_buf, wkv_state = rwkv7_mix(pd, shift_buf, wkv_state, p)
    y = jnp.concatenate([ya, yb, yc, yd], axis=-1) @ p['w_out']
    return y, v, pool_buf, conv_buf, shift_buf, wkv_state


def cross_attention(h, mk, mv, wq, wo):
    bn, t, _ = h.shape
    q = (h @ wq).reshape(bn, t, X_HEADS, X_HEAD_DIM)
    s = jnp.einsum('bthd,bmhd->bhtm', q, mk).astype(jnp.float32) * (X_HEAD_DIM ** -0.5)
    pr = jax.nn.softmax(s, axis=-1).astype(h.dtype)
    o = jnp.einsum('bhtm,bmhd->bthd', pr, mv).reshape(bn, t, X_HEADS * X_HEAD_DIM)
    return o @ wo


def swiglu(h, w1, w3, w2):
    return (jax.nn.silu(h @ w1) * (h @ w3)) @ w2


def decoder_layer(h, pos0, mem_k, mem_v, pool_buf, conv_buf, shift_buf, wkv_state, p):
    y, v_rows, pool_buf, conv_buf, shift_buf, wkv_state = hybrid_mixer(h, pos0, pool_buf, conv_buf, shift_buf, wkv_state, p)
    h = layer_norm(ALPHA * h + y, p['ln1_g'], p['ln1_b'])
    h = layer_norm(ALPHA * h + cross_attention(h, mem_k, mem_v, p['w_xq'], p['w_xo']), p['ln2_g'], p['ln2_b'])
    h = layer_norm(ALPHA * h + swiglu(h, p['ffn_w1'], p['ffn_w3'], p['ffn_w2']), p['ln3_g'], p['ln3_b'])
    return h, v_rows, pool_buf, conv_buf, shift_buf, wkv_state


def setup_inputs(seed: int = 0) -> dict:
    key = jax.random.key(seed)
    ks = iter(jax.random.split(key, 64))
    f32 = jnp.float32
    L = DEPTH
    nrm = lambda shape, scale: jax.random.normal(next(ks), shape, f32) * scale
    gain = lambda shape: 1.0 + 0.05 * jax.random.normal(next(ks), shape, f32)
    return {
        'x_prompt': nrm((BATCH, SEQ, D_MODEL), 1.0),
        'x_sample': nrm((DEC_BATCH, DEC_SEQ, D_MODEL), 1.0),
        'mem_prompt': nrm((BATCH, MEM_LEN, D_MODEL), 1.0),
        'cache_mem_k': nrm((L, DEC_BATCH, MEM_LEN, X_HEADS, X_HEAD_DIM), 1.0),
        'cache_mem_v': nrm((L, DEC_BATCH, MEM_LEN, X_HEADS, X_HEAD_DIM), 1.0),
        'state_pool': nrm((L, DEC_BATCH, POOL_BUF, W_B), 1.0),
        'state_conv': nrm((L, DEC_BATCH, CONV_W - 1, W_C), 1.0),
        'state_shift': nrm((L, DEC_BATCH, 1, D_PROJ), 1.0),
        'state_wkv': nrm((L, DEC_BATCH, D_HEADS, D_HEAD_DIM, D_HEAD_DIM), 0.3),
        'w_in': nrm((L, D_MODEL, PROJ), D_MODEL ** -0.5),
        'mu_d': jax.random.uniform(next(ks), (L, D_PROJ), f32),
        'ln_v_g': gain((L, W_A)),
        'ln_v_b': nrm((L, W_A), 0.02),
        'ws_chunk': nrm((L, A_HEADS, CHUNK, CHUNK), CHUNK ** -0.5),
        'b_chunk': gain((L, A_HEADS, CHUNK)),
        'w_pool': nrm((L, len(POOL_WINDOWS), POOL_GROUP, POOL_GROUP), POOL_GROUP ** -0.5),
        'pool_scale': gain((L, W_B)),
        'conv_w': nrm((L, CONV_W, W_C), 0.5),
        'rwkv_w0': jax.random.uniform(next(ks), (L, W_D), f32, -6.0, 0.5),
        'rwkv_w2': nrm((L, R_DECAY, W_D), 0.1),
        'rwkv_a0': nrm((L, W_D), 0.1),
        'rwkv_a2': nrm((L, R_AAA, W_D), 0.1),
        'rwkv_g2': nrm((L, R_GATE, W_D), R_GATE ** -0.5),
        'rwkv_k_k': 0.85 + 0.05 * jax.random.normal(next(ks), (L, W_D), f32),
        'rwkv_k_a': gain((L, W_D)),
        'rwkv_r_k': nrm((L, D_HEADS, D_HEAD_DIM), 0.1),
        'rwkv_lnx_g': gain((L, W_D)),
        'rwkv_lnx_b': nrm((L, W_D), 0.02),
        'w_out': nrm((L, MIX_WIDTH, D_MODEL), MIX_WIDTH ** -0.5 * BETA),
        'ln1_g': gain((L, D_MODEL)),
        'ln1_b': nrm((L, D_MODEL), 0.02),
        'w_xq': nrm((L, D_MODEL, X_HEADS * X_HEAD_DIM), D_MODEL ** -0.5),
        'w_xk': nrm((L, D_MODEL, X_HEADS * X_HEAD_DIM), D_MODEL ** -0.5),
        'w_xv': nrm((L, D_MODEL, X_HEADS * X_HEAD_DIM), D_MODEL ** -0.5),
        'w_xo': nrm((L, X_HEADS * X_HEAD_DIM, D_MODEL), (X_HEADS * X_HEAD_DIM) ** -0.5 * BETA),
        'ln2_g': gain((L, D_MODEL)),
        'ln2_b': nrm((L, D_MODEL), 0.02),
        'ffn_w1': nrm((L, D_MODEL, D_FF), D_MODEL ** -0.5),
        'ffn_w3': nrm((L, D_MODEL, D_FF), D_MODEL ** -0.5),
        'ffn_w2': nrm((L, D_FF, D_MODEL), D_FF ** -0.5 * BETA),
        'ln3_g': gain((L, D_MODEL)),
        'ln3_b': nrm((L, D_MODEL), 0.02),
    }


def reference(x_prompt, x_sample, mem_prompt, cache_mem_k, cache_mem_v, state_pool, state_conv, state_shift, state_wkv,
              w_in, mu_d, ln_v_g, ln_v_b, ws_chunk, b_chunk, w_pool, pool_scale, conv_w,
              rwkv_w0, rwkv_w2, rwkv_a0, rwkv_a2, rwkv_g2, rwkv_k_k, rwkv_k_a, rwkv_r_k, rwkv_lnx_g, rwkv_lnx_b,
              w_out, ln1_g, ln1_b, w_xq, w_xk, w_xv, w_xo, ln2_g, ln2_b, ffn_w1, ffn_w3, ffn_w2, ln3_g, ln3_b):
    bp, t_p, _ = x_prompt.shape
    last_chunk_start = ((t_p - 1) // CHUNK) * CHUNK
    dt = x_prompt.dtype
    hp, hs = x_prompt, x_sample
    pv, ppool, pconv, pshift, pwkv, pmk, pmv = [], [], [], [], [], [], []
    sv, spool, sconv, sshift, swkv = [], [], [], [], []
    for l in range(DEPTH):
        p = dict(w_in=w_in[l], mu_d=mu_d[l], ln_v_g=ln_v_g[l], ln_v_b=ln_v_b[l], ws_chunk=ws_chunk[l],
                 b_chunk=b_chunk[l], w_pool=w_pool[l], pool_scale=pool_scale[l], conv_w=conv_w[l],
                 rwkv_w0=rwkv_w0[l], rwkv_w2=rwkv_w2[l], rwkv_a0=rwkv_a0[l], rwkv_a2=rwkv_a2[l],
                 rwkv_g2=rwkv_g2[l], rwkv_k_k=rwkv_k_k[l], rwkv_k_a=rwkv_k_a[l], rwkv_r_k=rwkv_r_k[l],
                 rwkv_lnx_g=rwkv_lnx_g[l], rwkv_lnx_b=rwkv_lnx_b[l], w_out=w_out[l],
                 ln1_g=ln1_g[l], ln1_b=ln1_b[l], w_xq=w_xq[l], w_xo=w_xo[l], ln2_g=ln2_g[l], ln2_b=ln2_b[l],
                 ffn_w1=ffn_w1[l], ffn_w3=ffn_w3[l], ffn_w2=ffn_w2[l], ln3_g=ln3_g[l], ln3_b=ln3_b[l])
        mk_p = (mem_prompt @ w_xk[l]).reshape(bp, MEM_LEN, X_HEADS, X_HEAD_DIM)
        mv_p = (mem_prompt @ w_xv[l]).reshape(bp, MEM_LEN, X_HEADS, X_HEAD_DIM)
        hp, v_rows, pb, cb, sb, wk = decoder_layer(
            hp, 0, mk_p, mv_p,
            jnp.zeros((bp, POOL_BUF, W_B), dt), jnp.zeros((bp, CONV_W - 1, W_C), dt),
            jnp.zeros((bp, 1, D_PROJ), dt), jnp.zeros((bp, D_HEADS, D_HEAD_DIM, D_HEAD_DIM), jnp.float32), p)
        pv.append(v_rows[:, last_chunk_start:])
        ppool.append(pb)
        pconv.append(cb)
        pshift.append(sb)
        pwkv.append(wk)
        pmk.append(mk_p)
        pmv.append(mv_p)
        hs, v_rows_s, pb_s, cb_s, sb_s, wk_s = decoder_layer(
            hs, PAST_LEN, cache_mem_k[l], cache_mem_v[l],
            state_pool[l], state_conv[l], state_shift[l], state_wkv[l], p)
        sv.append(v_rows_s)
        spool.append(pb_s)
        sconv.append(cb_s)
        sshift.append(sb_s)
        swkv.append(wk_s)
    return (hp, hs,
            jnp.stack(pv), jnp.stack(ppool), jnp.stack(pconv), jnp.stack(pshift), jnp.stack(pwkv),
            jnp.stack(pmk), jnp.stack(pmv),
            jnp.stack(sv), jnp.stack(spool), jnp.stack(sconv), jnp.stack(sshift), jnp.stack(swkv))
```

```python
import contextlib
import numpy as np
import concourse.bass as bass
import concourse.mybir as mybir
from concourse.bass_utils import run_bass_kernel_spmd

F32 = mybir.dt.float32
BF16 = mybir.dt.bfloat16
ALU = mybir.AluOpType
AF = mybir.ActivationFunctionType
AX = mybir.AxisListType

ENGS = ['pe', 'act', 'dve', 'pool', 'sp']
SAME_ENG_SYNC = True

D = 1024
SEQ = 2048
NT = 512
NTILE = SEQ // NT
L = 2
PROJ = 2432
DFF = 2816
NS = 16
NTX = NT + NS
ALPHA = float(4 ** 0.25)
LN_EPS = 1e-5
GN_EPS = 64e-5
NVEC = 88


class _Stop(Exception):
    pass


class Res:
    __slots__ = ('name', 'w', 'r')

    def __init__(self, name):
        self.name = name
        self.w = None
        self.r = {}


class Sched:
    def __init__(self, nc, stack, n_dma_sems=None):
        self.nc = nc
        self.ops = {e: [] for e in ENGS}
        self.cnt = {e: 0 for e in ENGS}
        self.waited = {e: {} for e in ENGS}
        self.sem = {}
        for e in ENGS:
            self.sem[e] = stack.enter_context(nc.semaphore('s_' + e))
        n_dma_sems = n_dma_sems or {'sp': 24, 'pool': 8}
        self.dsem = {}
        self.dsem_cnt = {}
        self.dsem_rr = {}
        for q, n in n_dma_sems.items():
            self.dsem[q] = []
            for i in range(n):
                k = 'd_%s_%d' % (q, i)
                self.sem[k] = stack.enter_context(nc.semaphore(k))
                self.dsem[q].append(k)
                self.dsem_cnt[k] = 0
            self.dsem_rr[q] = 0
        self.res = {}

    def R(self, *key):
        r = self.res.get(key)
        if r is None:
            r = Res(key)
            self.res[key] = r
        return r

    def add(self, eng, fn, reads=(), writes=(), dma=False, inc=1):
        def _flat(xs):
            out = []
            for x in xs:
                if isinstance(x, (tuple, list)):
                    out.extend(_flat(x))
                else:
                    out.append(x)
            return out
        reads = _flat(reads)
        writes = _flat(writes)
        writes = writes + [r for r in reads if r.name[0] == 'ps']
        reads = [r for r in reads if r.name[0] != 'ps']
        deps = {}

        def need(d, same_ok):
            if d is None:
                return
            k, v = d
            if k == eng and not same_ok:
                return
            if deps.get(k, 0) < v:
                deps[k] = v

        same_raw = SAME_ENG_SYNC and eng != 'pe'
        for r in reads:
            need(r.w, same_raw)
        for w in writes:
            need(w.w, same_raw)
            for k, v in w.r.items():
                need((k, v), same_raw)
        if dma:
            q = eng
            sk = self.dsem[q][self.dsem_rr[q] % len(self.dsem[q])]
            self.dsem_rr[q] += 1
            if self.dsem_cnt[sk] > 0:
                need((sk, self.dsem_cnt[sk]), True)
            self.dsem_cnt[sk] += 16
            done = (sk, self.dsem_cnt[sk])
        else:
            self.cnt[eng] += inc
            done = (eng, self.cnt[eng])
        waits = []
        wd = self.waited[eng]
        for k, v in deps.items():
            if wd.get(k, 0) >= v:
                continue
            wd[k] = v
            waits.append((k, v))
        self.ops[eng].append((waits, fn, done, dma, inc))
        for r in reads:
            if r.r.get(done[0], 0) < done[1]:
                r.r[done[0]] = done[1]
        for w in writes:
            w.w = done
            w.r = {}
        return done

    def dma(self, q, out, in_, reads=(), writes=(), **kw):
        return self.add(q, lambda e: e.dma_start(out=out, in_=in_, **kw), reads, writes, dma=True)

    def barrier(self):
        part = ['pe', 'act', 'dve', 'sp']
        snap = {e: self.cnt[e] for e in ('pe', 'act', 'dve')}
        dsn = {k: self.dsem_cnt[k] for k in self.dsem['sp'] if self.dsem_cnt[k] > 0}
        for e in part:
            waits = []
            wd = self.waited[e]
            for k, v in list(snap.items()) + list(dsn.items()):
                if k == e or v == 0:
                    continue
                if wd.get(k, 0) >= v:
                    continue
                wd[k] = v
                waits.append((k, v))
            if waits:
                self.ops[e].append((waits, None, None, False, 0))

    def emit(self):
        nc = self.nc
        hmap = {'pe': 'tensor', 'act': 'scalar', 'dve': 'vector', 'pool': 'gpsimd', 'sp': 'sync'}
        final = [(k, v) for k, v in self.dsem_cnt.items() if v > 0]
        with nc.Block() as block:
            for eng in ENGS:
                def body(e, eng=eng):
                    for waits, fn, done, dma, inc in self.ops[eng]:
                        for k, v in waits:
                            e.wait_ge(self.sem[k], v)
                        if fn is None:
                            continue
                        ins = fn(e)
                        ins.then_inc(self.sem[done[0]], 16 if dma else inc)
                    if eng == 'sp':
                        for k, v in final:
                            e.wait_ge(self.sem[k], v)
                        for en2 in ENGS:
                            if en2 != 'sp' and self.cnt[en2] > 0:
                                e.wait_ge(self.sem[en2], self.cnt[en2])
                getattr(block, hmap[eng])(body)


class Prog:
    def __init__(self):
        self.nc = bass.Bass("TRN2", target_bir_lowering=False)
        self.dram = {}

    def din(self, name, shape):
        t = self.nc.dram_tensor(name, list(shape), F32, kind="ExternalInput").ap()
        self.dram[name] = t
        return t

    def dout(self, name, shape):
        t = self.nc.dram_tensor(name, list(shape), F32, kind="ExternalOutput").ap()
        self.dram[name] = t
        return t


OUT_SHAPES = {
    'y_prompt': [SEQ, D], 'y_sample': [NS, D],
    'chunk_v_prompt': [L, 128, 256], 'pool_prompt': [L, 15, 256], 'conv_prompt': [L, 2, 256],
    'shift_prompt': [L, 896], 'wkv_prompt': [L, 4, 64, 64],
    'mem_k_prompt': [L, 256, D], 'mem_v_prompt': [L, 256, D],
    'chunk_v_sample': [L, NS, 256], 'pool_sample': [L, NS, 15, 256], 'conv_sample': [L, NS, 2, 256],
    'shift_sample': [L, NS, 896], 'wkv_sample': [L, NS, 4, 64, 64],
}
IN_SHAPES = {
    'x_prompt': [SEQ, D], 'x_sample': [NS, D], 'mem_prompt': [256, D],
    'cache_mem_k': [L, NS, 256, D], 'cache_mem_v': [L, NS, 256, D],
    'state_pool': [L, NS, 15, 256], 'state_conv': [L, NS, 2, 256], 'state_shift': [L, NS, 896],
    'state_wkv': [L, NS, 4, 64, 64],
    'w_in': [L, D, PROJ], 'w_out': [L, D, D], 'w_xq': [L, D, D], 'w_xk': [L, D, D], 'w_xv': [L, D, D],
    'w_xo': [L, D, D], 'ffn_w1': [L, D, DFF], 'ffn_w3': [L, D, DFF], 'ffn_w2': [L, DFF, D],
    'wsT': [L, 4, 128, 128], 'wpbd': [L, 2, 128, 128], 'lora_w': [L, 128, 256],
    'vec': [128, L * NVEC], 'bvec': [128, L * 768], 'cmask': [128, 4 * 128], 'cmat': [128, 2 * 128],
    'invcnt0': [128, 32], 'esel': [16, 16 * 128],
}


def build_program(do_sample=True, ntile=NTILE, nlayer=L, stop=None):
    P = Prog()
    nc = P.nc
    di = {k: P.din(k, v) for k, v in IN_SHAPES.items()}
    do = {k: P.dout(k, v) for k, v in OUT_SHAPES.items()}
    st = contextlib.ExitStack()
    with st:
        S = Sched(nc, st)
        R = S.R

        def sb(name, shape, dt=F32):
            return st.enter_context(nc.sbuf_tensor('sb_' + name, list(shape), dt))

        h32 = sb('h32', [128, 8, NTX])
        hb = sb('hb', [128, 8, NTX], BF16)
        pre32 = sb('pre32', [128, 8, NTX])
        lnt = sb('lnt', [128, 3, 2, NTX], BF16)
        pjS = sb('pjS', [128, 19, NS])
        accS = sb('accS', [128, 2, NS])
        NW = 5
        wsl = [sb('wsl%d' % i, [128, 4096], BF16) for i in range(NW)]
        cmask = sb('cmask', [128, 4, 128])
        cmat = sb('cmat', [128, 2, 128])
        onesD = sb('onesD', [128, 128], BF16)
        identb = sb('identb', [128, 128], BF16)
        vec = sb('vec', [128, L * NVEC])
        omka = sb('omka', [128, L, 2])
        bvec = sb('bvec', [128, L, 768])
        invcnt0 = sb('invcnt0', [128, 2, 16])
        wsTb = sb('wsTb', [128, L, 4, 128], BF16)
        wpbd = sb('wpbd', [128, L, 2, 128], BF16)
        lorab = sb('lorab', [128, L, 256], BF16)
        KT = sb('KT', [128, L, 8, 256], BF16)
        Vm = sb('Vm', [128, L, 2, D], BF16)
        hpool = sb('hpool', [128, L, 2, 15])
        hconv = sb('hconv', [128, L, 2, 2])
        hpd = sb('hpd', [128, L, 7])
        Gst = sb('Gst', [128, L, 2, 64])
        lnm = sb('lnm', [128, NTX])
        lnr = sb('lnr', [128, NTX])
        lnn = sb('lnn', [128, NTX])
        SCRN = 20760
        scr = sb('scr', [128, SCRN])

        ps = [st.enter_context(nc.psum_tensor('ps%d' % i, [128, 512], F32)) for i in range(8)]
        ps_rr = [0]

        def psum():
            i = ps_rr[0] % 6
            ps_rr[0] += 1
            return ps[i], R('ps', i)

        scr_off = [0]
        scr_ph = [0]
        attn_mark = [0]
        scr_max = [0]

        def phase():
            S.barrier()
            scr_off[0] = 0
            scr_ph[0] += 1

        def salloc(shape, dt=F32):
            n = int(np.prod(shape))
            words = n if dt == F32 else (n + 1) // 2
            words = (words + 1) // 2 * 2
            o = scr_off[0]
            assert o + words <= SCRN, ('scratch overflow', o, words)
            scr_off[0] += words
            scr_max[0] = max(scr_max[0], scr_off[0])
            v = scr[:, o:o + words]
            if dt != F32:
                v = v.bitcast(dt)
            v = v[:, 0:n]
            if len(shape) == 2:
                v = v.rearrange("p (a b) -> p a b", b=shape[1])
            elif len(shape) == 3:
                v = v.rearrange("p (a b c) -> p a b c", b=shape[1], c=shape[2])
            return v

        def mm(out, lhsT, rhs, start, stop, reads, writes):
            S.add('pe', lambda e: e.matmul(out, lhsT, rhs, start=start, stop=stop), reads, writes)

        def mmr(out, lhsT, rhs, start, stop, reads, writes):
            F32R = mybir.dt.float32r
            S.add('pe', lambda e: e.matmul(out, lhsT.bitcast(F32R), rhs.bitcast(F32R), start=start, stop=stop), reads, writes)

        def tr(out, in_, reads, writes):
            n = in_.shape[0]
            S.add('pe', lambda e: e.transpose(out, in_, cmask[0:n, 0, 0:n]), list(reads) + [R('const')], writes)

        def trb(out, in_, reads, writes):
            n = in_.shape[0]
            S.add('pe', lambda e: e.transpose(out, in_, identb[0:n, 0:n]), list(reads) + [R('const')], writes)

        def act(out, in_, func, reads, writes, bias=None, scale=None, accum_out=None):
            kw = {}
            if bias is not None:
                kw['bias'] = bias
            if scale is not None:
                kw['scale'] = scale
            if accum_out is not None:
                kw['accum_out'] = accum_out
            S.add('act', lambda e: e.activation(out, in_, func, **kw), reads, writes)

        def tt(out, a, b, op, reads, writes, eng='dve'):
            S.add(eng, lambda e: e.tensor_tensor(out, a, b, op), reads, writes)

        def ts(out, a, s1, s2, op0, op1, reads, writes, eng='dve'):
            if s2 is None:
                S.add(eng, lambda e: e.tensor_scalar(out, a, s1, None, op0), reads, writes)
            else:
                S.add(eng, lambda e: e.tensor_scalar(out, a, s1, s2, op0, op1), reads, writes)

        def stt(out, in0, scalar, in1, op0, op1, reads, writes, eng='dve'):
            S.add(eng, lambda e: e.scalar_tensor_tensor(out, in0, scalar, in1, op0, op1), reads, writes)

        def cp(out, in_, reads, writes, eng='dve'):
            if eng == 'act':
                S.add('act', lambda e: e.activation(out, in_, AF.Copy), reads, writes)
            else:
                S.add(eng, lambda e: e.tensor_copy(out, in_), reads, writes)

        def memset(ap, val, writes, eng='dve'):
            S.add(eng, lambda e: e.memset(ap, val), (), writes)

        w_rr = [0]

        def wload(src, kc, cols):
            i = w_rr[0] % NW
            w_rr[0] += 1
            v = wsl[i][:, 0:kc * cols].rearrange("p (k c) -> p k c", c=cols)
            S.dma('pool', v, src, writes=[R('wsl', i)])
            return v, R('wsl', i)

        def wsrc(name, l, c0, cols, kc=8):
            return di[name][l, :, c0:c0 + cols].rearrange("(k p) c -> p k c", p=128)

        RC = R('const')

        S.dma('sp', cmask[:], di['cmask'].rearrange("p (a b) -> p a b", b=128), writes=[RC])
        S.dma('sp', cmat[:], di['cmat'].rearrange("p (a b) -> p a b", b=128), writes=[RC])
        S.dma('sp', vec[:], di['vec'], writes=[RC])
        S.dma('sp', bvec[:], di['bvec'].rearrange("p (l n) -> p l n", n=768), writes=[RC])
        S.dma('sp', invcnt0[:], di['invcnt0'].rearrange("p (a b) -> p a b", b=16), writes=[RC])
        cp(onesD[:], cmat[:, 0, :], [RC], [RC], eng='act')
        cp(identb[:], cmask[:, 0, :], [RC], [RC], eng='act')
        S.dma('pool', wpbd[:], di['wpbd'].rearrange("l c p n -> p l c n"), writes=[RC])
        S.dma('pool', lorab[:], di['lora_w'].rearrange("l p n -> p l n"), writes=[RC])
        for l in range(L):
            vb = l * NVEC
            ts(omka[:, l, :], vec[:, vb + 21:vb + 23], -1.0, 1.0, ALU.mult, ALU.add, [RC], [RC])

        def V_(l, off, n=1):
            return vec[:, l * NVEC + off:l * NVEC + off + n]

        ln_i = [0]

        def ln_accum(c, n, ext=False):
            nx = n + NS if ext else n
            k = ln_i[0] % 3
            ln_i[0] += 1
            Rt = R('lnt', k)
            act(lnt[:, k, 0, 0:nx], pre32[:, c, 0:nx], AF.Copy, [R('pre', c)], [Rt])
            act(lnt[:, k, 1, 0:nx], pre32[:, c, 0:nx], AF.Square, [R('pre', c)], [Rt])
            def pe_part():
                mm(ps[6][:, 0:n], onesD[:], lnt[:, k, 0, 0:n], c == 0, c == 7, [Rt, RC], [R('ps', 6)])
                mm(ps[7][:, 0:n], onesD[:], lnt[:, k, 1, 0:n], c == 0, c == 7, [Rt, RC], [R('ps', 7)])
                if ext:
                    pS, RpS = psum()
                    mm(pS[:, 0:NS], onesD[:], lnt[:, k, 0, n:nx], True, True, [Rt, RC], [RpS])
                    mm(pS[:, NS:2 * NS], onesD[:], lnt[:, k, 1, n:nx], True, True, [Rt, RC], [RpS])
                    av_ = accS[:, :, :].rearrange("p a b -> p (a b)")
                    if c == 0:
                        cp(av_, pS[:, 0:2 * NS], [RpS], [R('accS')])
                    else:
                        tt(av_, av_, pS[:, 0:2 * NS], ALU.add, [RpS, R('accS')], [R('accS')])
            return pe_part

        def ln_finish(l, goff, boff, n, ext=False):
            nx = n + NS if ext else n
            Rl = R('lnsm')
            cp(lnm[:, 0:n], ps[6][:, 0:n], [R('ps', 6)], [Rl], eng='act')
            if ext:
                cp(lnm[:, n:nx], accS[:, 0, :], [R('accS')], [Rl])
            tt(lnr[:, 0:nx], lnm[:, 0:nx], lnm[:, 0:nx], ALU.mult, [Rl], [Rl])
            if ext:
                tt(lnr[:, n:nx], accS[:, 1, :], lnr[:, n:nx], ALU.subtract, [R('accS'), Rl], [Rl])
            tt(lnr[:, 0:n], ps[7][:, 0:n], lnr[:, 0:n], ALU.subtract, [R('ps', 7), Rl], [Rl])
            act(lnr[:, 0:nx], lnr[:, 0:nx], AF.Sqrt, [Rl], [Rl], bias=LN_EPS, scale=1.0)
            S.add('dve', lambda e: e.reciprocal(lnr[:, 0:nx], lnr[:, 0:nx]), [Rl], [Rl])
            stt(lnn[:, 0:nx], lnm[:, 0:nx], -1.0, lnr[:, 0:nx], ALU.mult, ALU.mult, [Rl], [Rl])
            for c in range(8):
                tt(pre32[:, c, 0:nx], pre32[:, c, 0:nx], lnr[:, 0:nx], ALU.mult, [R('pre', c), Rl], [R('pre', c)])
                tt(pre32[:, c, 0:nx], pre32[:, c, 0:nx], lnn[:, 0:nx], ALU.add, [R('pre', c), Rl], [R('pre', c)])
                act(hb[:, c, 0:nx], pre32[:, c, 0:nx], AF.Identity, [R('pre', c), RC], [R('hb', c)],
                    bias=V_(l, boff + c), scale=V_(l, goff + c))
                act(h32[:, c, 0:nx], pre32[:, c, 0:nx], AF.Identity, [R('pre', c), RC], [R('h32', c)],
                    bias=V_(l, boff + c), scale=V_(l, goff + c))

        def dense_res_ln(l, wname, src, src_res, nk, n, goff, boff, ext=False):
            nx = n + NS if ext else n
            pend = []
            if nk == 8:
                groups = [(g * 512, 512) for g in range(2)]
            else:
                groups = [(g * 128, 128) for g in range(8)]
            for (c0, cols) in groups:
                wv, Rw = wload(di[wname][l, :, c0:c0 + cols].rearrange("(k p) c -> p k c", p=128), nk, cols)
                for j in range(cols // 128):
                    c = (c0 + j * 128) // 128
                    pb, Rp = psum()
                    for k in range(nk):
                        mm(pb[:, 0:n], wv[:, k, j * 128:(j + 1) * 128], src[:, k, 0:n], k == 0, k == nk - 1,
                           [Rw, src_res(k)], [Rp])
                    if ext:
                        pS, RpS = psum()
                        for k in range(nk):
                            mm(pS[:, 0:NS], wv[:, k, j * 128:(j + 1) * 128], src[:, k, n:nx], k == 0, k == nk - 1,
                               [Rw, src_res(k)], [RpS])
                    stt(pre32[:, c, 0:n], h32[:, c, 0:n], ALPHA, pb[:, 0:n], ALU.mult, ALU.add,
                        [R('h32', c), Rp], [R('pre', c)])
                    if ext:
                        stt(pre32[:, c, n:nx], h32[:, c, n:nx], ALPHA, pS[:, 0:NS], ALU.mult, ALU.add,
                            [R('h32', c), RpS], [R('pre', c)])
                    pend.append(ln_accum(c, n, ext))
                    if len(pend) > 2:
                        pend.pop(0)()
            for f_ in pend:
                f_()
            ln_finish(l, goff, boff, n, ext)

        def load_tokens(src_rows, n, col0=0):
            nsub = (n + 127) // 128
            xt = salloc([4, D])
            for s_ in range(nsub):
                r = min(128, n - s_ * 128)
                S.dma('sp', xt[0:r, s_, :], src_rows[s_ * 128:s_ * 128 + r, :], writes=[R('xt', s_)])
            for c in range(8):
                pb, Rp = psum()
                for s_ in range(nsub):
                    r = min(128, n - s_ * 128)
                    tr(pb[:, s_ * 128:s_ * 128 + r], xt[0:r, s_, c * 128:(c + 1) * 128], [R('xt', s_)], [Rp])
                cp(h32[:, c, col0:col0 + n], pb[:, 0:n], [Rp], [R('h32', c)], eng='act')
                cp(hb[:, c, col0:col0 + n], pb[:, 0:n], [Rp], [R('hb', c)], eng='act')

        def store_tokens(dst_rows, n, col0=0):
            nsub = (n + 127) // 128
            yt = salloc([4, D])
            for s_ in range(nsub):
                r = min(128, n - s_ * 128)
                for g in range(2):
                    pb, Rp = psum()
                    for j in range(4):
                        c = g * 4 + j
                        tr(pb[0:r, j * 128:(j + 1) * 128], h32[:, c, col0 + s_ * 128:col0 + s_ * 128 + r], [R('h32', c)], [Rp])
                    cp(yt[0:r, s_, g * 512:(g + 1) * 512], pb[0:r, :], [Rp], [R('yt', s_)], eng=('act' if g else 'dve'))
                S.dma('sp', dst_rows[s_ * 128:s_ * 128 + r, :], yt[0:r, s_, :], reads=[R('yt', s_)])

        def setup_layers():
            memT = salloc([8, 256], BF16)
            mt = salloc([2, D])
            for s_ in range(2):
                S.dma('sp', mt[:, s_, :], di['mem_prompt'][s_ * 128:(s_ + 1) * 128, :], writes=[R('mt', s_)])
            for c in range(8):
                pb, Rp = psum()
                for s_ in range(2):
                    tr(pb[:, s_ * 128:(s_ + 1) * 128], mt[:, s_, c * 128:(c + 1) * 128], [R('mt', s_)], [Rp])
                cp(memT[:, c, :], pb[:, 0:256], [Rp], [R('memT')], eng=('act' if c % 2 else 'dve'))
            wst = salloc([4, 128])
            kvo = salloc([2, 512])
            if stop == 'su1':
                return
            for l in range(L):
                S.dma('sp', wst[:], di['wsT'][l].rearrange("h s t -> s h t"), writes=[R('wst')])
                tt(wsTb[:, l, :, :], wst[:], cmask[:, 2:3, :].broadcast_to([128, 4, 128]), ALU.mult,
                   [R('wst'), RC], [RC])
                if stop == 'su2':
                    continue
                for wi, wname in enumerate(['w_xk', 'w_xv']):
                    oname = 'mem_k_prompt' if wi == 0 else 'mem_v_prompt'
                    for g in range(2):
                        wv, Rw = wload(wsrc(wname, l, g * 512, 512), 8, 512)
                        if stop == 'su3':
                            continue
                        if wi == 0:
                            for j in range(4):
                                pb, Rp = psum()
                                for k in range(8):
                                    mm(pb[:, 0:256], wv[:, k, j * 128:(j + 1) * 128], memT[:, k, :], k == 0, k == 7,
                                       [Rw, R('memT')], [Rp])
                                cp(KT[:, l, g * 4 + j, :], pb[:, 0:256], [Rp], [R('KT', l)], eng='act')
                        if stop == 'su4':
                            continue
                        for s_ in range(2):
                            pb, Rp = psum()
                            for k in range(8):
                                mm(pb[:, :], memT[:, k, s_ * 128:(s_ + 1) * 128], wv[:, k, :], k == 0, k == 7,
                                   [Rw, R('memT')], [Rp])
                            if stop == 'su7':
                                continue
                            if stop != 'su9':
                                cp(kvo[:, s_, :], pb[:, :], [Rp], [R('kvo', s_)], eng='act')
                            if stop == 'su8':
                                continue
                            if wi == 1:
                                cp(Vm[:, l, s_, g * 512:(g + 1) * 512], pb[:, :], [Rp], [R('Vm', l)],
                                   eng=('act' if stop == 'su10' else 'dve'))
                            if stop in ('su9', 'su10'):
                                continue
                            if stop != 'su6':
                                S.dma('sp', do[oname][l, s_ * 128:(s_ + 1) * 128, g * 512:(g + 1) * 512], kvo[:, s_, :],
                                      reads=[R('kvo', s_)])
            for t_, nm in ((hpool, 'hpool'), (hconv, 'hconv'), (hpd, 'hpd')):
                memset(t_[:], 0.0, [R(nm)])
            memset(Gst[:], 0.0, [R('Gh', h_) for h_ in range(4)])

        def mixer_prompt(l, ti, ext=False):
            n = NT
            last = (ti == ntile - 1)
            mixb = salloc([8, NTX], BF16)

            def sproj(wv, Rw, col_lo, chunk):
                if not ext:
                    return
                pS, RpS = psum()
                for k in range(8):
                    mm(pS[:, 0:NS], wv[:, k, col_lo:col_lo + 128], hb[:, k, NT:NTX], k == 0, k == 7, [Rw, R('hb', k)], [RpS])
                cp(pjS[:, chunk, :], pS[:, 0:NS], [RpS], [R('pjS')], eng='act')

            Rmix = lambda k: R('mixb', k)
            G = [salloc([2, 528]) for _ in range(8)]
            RG = [(R('G', i, 0), R('G', i, 1)) for i in range(8)]
            X2 = [salloc([2, NT]) for _ in range(2)]
            RX2 = [(R('X2', i, 0), R('X2', i, 1)) for i in range(2)]
            pre = dict(KR=salloc([2, 4, 256], BF16), kt=salloc([2, NT], BF16), bt=salloc([2, NT], BF16), gC=salloc([2, 4]))
            pre['mark'] = scr_off[0]
            wv, Rw = wload(wsrc('w_in', l, 0, 512), 8, 512)
            ug = G[0]
            for c in range(2):
                pb, Rp = psum()
                for k in range(8):
                    mm(pb[:, :], wv[:, k, c * 128:(c + 1) * 128], hb[:, k, 0:NT], k == 0, k == 7, [Rw, R('hb', k)], [Rp])
                act(ug[:, c, 0:NT], pb[:, :], AF.Gelu_apprx_tanh, [Rp], [RG[0]])
            for cs_ in range(4):
                sproj(wv, Rw, cs_ * 128, cs_)
            va = salloc([4, 256])
            vs6 = salloc([4, 6])
            vmv = salloc([4, 2])
            vlnb = salloc([4, 256], BF16)
            bsb = bvec[:, l, 512:768].rearrange("p (c t) -> p c t", t=128)
            for s_ in range(4):
                q = s_
                Rv = R('va', q)
                pb, Rp = psum()
                for k in range(8):
                    mm(pb[:, 0:256], hb[:, k, s_ * 128:(s_ + 1) * 128], wv[:, k, 256:512], k == 0, k == 7,
                       [Rw, R('hb', k)], [Rp])
                act(va[:, q, :], pb[:, 0:256], AF.Gelu_apprx_tanh, [Rp], [Rv])
            for q in range(4):
                Rv = R('va', q)
                S.add('dve', lambda e, q=q: e.bn_stats(vs6[:, q, :], va[:, q, :]), [Rv], [Rv])
                S.add('dve', lambda e, q=q: e.bn_aggr(vmv[:, q, :], vs6[:, q, :]), [Rv], [Rv])
            for q in range(4):
                Rv = R('va', q)
                act(vmv[:, q, 1:2], vmv[:, q, 1:2], AF.Sqrt, [Rv], [Rv], bias=LN_EPS, scale=1.0)
            for q in range(4):
                Rv = R('va', q)
                S.add('dve', lambda e, q=q: e.reciprocal(vmv[:, q, 1:2], vmv[:, q, 1:2]), [Rv], [Rv])
                ts(va[:, q, :], va[:, q, :], vmv[:, q, 0:1], vmv[:, q, 1:2], ALU.subtract, ALU.mult, [Rv], [Rv])
                tt(va[:, q, :], va[:, q, :], bvec[:, l, 0:256], ALU.mult, [Rv, RC], [Rv])
                tt(va[:, q, :], va[:, q, :], bvec[:, l, 256:512], ALU.add, [Rv, RC], [Rv])
                if last and q == 3:
                    S.dma('sp', do['chunk_v_prompt'][l], va[:, q, :], reads=[Rv])
                cp(vlnb[:, q, :], va[:, q, :], [Rv], [R('vlnb', q)], eng='act')
            zt = salloc([2, NT])
            for s_ in range(4):
                q = s_
                zps = []
                for c in range(2):
                    pb2, Rp2 = psum()
                    for hh in range(2):
                        h_ = 2 * c + hh
                        mm(pb2[hh * 64:(hh + 1) * 64, 0:128], vlnb[:, q, h_ * 64:(h_ + 1) * 64], wsTb[:, l, h_, :],
                           True, True, [R('vlnb', q), RC], [Rp2])
                    zps.append((pb2, Rp2))
                for c in range(2):
                    pb2, Rp2 = zps[c]
                    tt(zt[:, c, s_ * 128:(s_ + 1) * 128], pb2[:, 0:128], bsb[:, c, :], ALU.add, [Rp2, RC], [R('zt', s_, c)])
                    tt(mixb[:, c, s_ * 128:(s_ + 1) * 128], zt[:, c, s_ * 128:(s_ + 1) * 128],
                       ug[:, c, s_ * 128:(s_ + 1) * 128], ALU.mult, [R('zt', s_, c), RG[0]], [Rmix(c)])
            if stop == 'A':
                raise _Stop()
            wv, Rw = wload(wsrc('w_in', l, 512, 512), 8, 512)
            xp, s2, s4, s8 = G[1], G[2], G[3], G[4]
            for c in range(2):
                pb, Rp = psum()
                for k in range(8):
                    mm(pb[:, :], wv[:, k, c * 128:(c + 1) * 128], hb[:, k, 0:NT], k == 0, k == 7, [Rw, R('hb', k)], [Rp])
                cp(xp[:, c, 15:15 + NT], pb[:, :], [Rp], [RG[1]], eng='act')
            for cs_ in range(4):
                sproj(wv, Rw, cs_ * 128, 4 + cs_)
            cp(xp[:, :, 0:15], hpool[:, l, :, :], [R('hpool')], [RG[1]])
            cp(hpool[:, l, :, :], xp[:, :, NT:NT + 15], [RG[1]], [R('hpool')])
            W_ = 15 + NT
            tt(s2[:, :, 1:W_], xp[:, :, 1:W_], xp[:, :, 0:W_ - 1], ALU.add, [RG[1]], [RG[2]])
            tt(s4[:, :, 3:W_], s2[:, :, 3:W_], s2[:, :, 1:W_ - 2], ALU.add, [RG[2]], [RG[3]])
            tt(s8[:, 0, 7:W_], s4[:, 1, 7:W_], s4[:, 1, 3:W_ - 4], ALU.add, [RG[3]], [RG[4]])
            tt(s8[:, 1, 15:W_], s8[:, 0, 15:W_], s8[:, 0, 7:W_ - 8], ALU.add, [RG[4]], [RG[4]])
            mean = G[5]
            srcs = [(0, 0, s2[0:64, 0, 15:W_]), (0, 64, s4[64:128, 0, 15:W_]),
                    (1, 0, s8[0:64, 0, 15:W_]), (1, 64, s8[64:128, 1, 15:W_])]
            for (c, p0, sv) in srcs:
                ts(mean[p0:p0 + 64, c, 0:NT], sv, V_(l, 77 + c)[p0:p0 + 64, :], None, ALU.mult, None,
                   [RG[2], RG[3], RG[4], RC], [RG[5]])
                if ti == 0:
                    tt(mean[p0:p0 + 64, c, 0:16], sv[:, 0:16], invcnt0[p0:p0 + 64, c, :], ALU.mult,
                       [RG[2], RG[3], RG[4], RC], [RG[5]])
            dpl = salloc([2, NT], BF16)
            tt(dpl[:, :, :], mean[:, :, 0:NT], xp[:, :, 15:W_], ALU.subtract, [RG[5], RG[1]], [R('dpl')])
            if last:
                pb, Rp = psum()
                for c in range(2):
                    tr(pb[0:15, c * 128:(c + 1) * 128], xp[:, c, NT:NT + 15], [RG[1]], [Rp])
                ot = salloc([1, 256])
                cp(ot[0:15, 0, :], pb[0:15, 0:256], [Rp], [R('ot_pool')])
                S.dma('sp', do['pool_prompt'][l], ot[0:15, 0, :], reads=[R('ot_pool')])
            bgs = G[6]
            for c in range(2):
                pb, Rp = psum()
                for k in range(8):
                    mm(pb[:, :], wv[:, k, 256 + c * 128:256 + (c + 1) * 128], hb[:, k, 0:NT], k == 0, k == 7,
                       [Rw, R('hb', k)], [Rp])
                cp(bgs[:, c, 0:NT], pb[:, :], [Rp], [RG[6]], eng='act')
            if stop == 'B':
                raise _Stop()
            wv, Rw = wload(wsrc('w_in', l, 1024, 512), 8, 512)
            cgs, zc, yc = G[7], G[2], G[3]
            for c in range(2):
                pb, Rp = psum()
                for k in range(8):
                    mm(pb[:, :], wv[:, k, c * 128:(c + 1) * 128], hb[:, k, 0:NT], k == 0, k == 7, [Rw, R('hb', k)], [Rp])
                cp(cgs[:, c, 0:NT], pb[:, :], [Rp], [RG[7]], eng='act')
            for c in range(2):
                pb, Rp = psum()
                for k in range(8):
                    mm(pb[:, :], wv[:, k, 256 + c * 128:256 + (c + 1) * 128], hb[:, k, 0:NT], k == 0, k == 7,
                       [Rw, R('hb', k)], [Rp])
                tt(zc[:, c, 2:2 + NT], pb[:, :], cgs[:, c, 0:NT], ALU.mult, [Rp, RG[7]], [RG[2]])
            for cs_ in range(4):
                sproj(wv, Rw, cs_ * 128, 8 + cs_)
            for c in range(2):
                pb, Rp = psum()
                mm(pb[:, :], wpbd[:, l, c, :], dpl[:, c, :], True, True, [RC, R('dpl')], [Rp])
                act(mixb[:, 2 + c, 0:NT], pb[:, :], AF.Identity, [Rp, RC], [Rmix(2 + c)], scale=V_(l, 13 + c))
            cp(zc[:, :, 0:2], hconv[:, l, :, :], [R('hconv')], [RG[2]])
            cp(hconv[:, l, :, :], zc[:, :, NT:NT + 2], [RG[2]], [R('hconv')])
            for c in range(2):
                ts(yc[:, c, 0:NT], zc[:, c, 0:NT], V_(l, 7 + 0 * 2 + c), None, ALU.mult, None, [RG[2], RC], [RG[3]])
                stt(yc[:, c, 0:NT], zc[:, c, 1:1 + NT], V_(l, 7 + 1 * 2 + c), yc[:, c, 0:NT], ALU.mult, ALU.add,
                    [RG[2], RG[3], RC], [RG[3]])
                stt(yc[:, c, 0:NT], zc[:, c, 2:2 + NT], V_(l, 7 + 2 * 2 + c), yc[:, c, 0:NT], ALU.mult, ALU.add,
                    [RG[2], RG[3], RC], [RG[3]])
                tt(mixb[:, 4 + c, 0:NT], yc[:, c, 0:NT], bgs[:, c, 0:NT], ALU.mult, [RG[3], RG[6]], [Rmix(4 + c)])
            if last:
                pb, Rp = psum()
                for c in range(2):
                    tr(pb[0:2, c * 128:(c + 1) * 128], zc[:, c, NT:NT + 2], [RG[2]], [Rp])
                ot2 = salloc([1, 256])
                cp(ot2[0:2, 0, :], pb[0:2, 0:256], [Rp], [R('ot_conv')])
                S.dma('sp', do['conv_prompt'][l], ot2[0:2, 0, :], reads=[R('ot_conv')])
            if stop == 'C':
                raise _Stop()
            xr, xk, xv = G[1], G[4], G[5]
            xl = salloc([1, NT])
            pdc = salloc([2, NT + 1])
            dif = salloc([1, NT])
            if last:
                shf = salloc([1, 8])
            dsts = [(xr, 0, RG[1]), (xr, 1, RG[1]), (xk, 0, RG[4]), (xk, 1, RG[4]), (xv, 0, RG[5]), (xv, 1, RG[5]),
                    (xl, 0, R('xl'))]
            for gi, (c0, ncol) in enumerate([(1536, 512), (2048, 384)]):
                wv, Rw = wload(wsrc('w_in', l, c0, ncol), 8, ncol)
                for j in range(ncol // 128):
                    cc = gi * 4 + j
                    q = cc % 2
                    Rq = R('pdc', q)
                    pb, Rp = psum()
                    for k in range(8):
                        mm(pb[:, :], wv[:, k, j * 128:(j + 1) * 128], hb[:, k, 0:NT], k == 0, k == 7, [Rw, R('hb', k)], [Rp])
                    cp(pdc[:, q, 1:NT + 1], pb[:, :], [Rp], [Rq], eng='act')
                    sproj(wv, Rw, j * 128, 12 + cc)
                    cp(pdc[:, q, 0:1], hpd[:, l, cc:cc + 1], [R('hpd')], [Rq])
                    cp(hpd[:, l, cc:cc + 1], pdc[:, q, NT:NT + 1], [Rq], [R('hpd')])
                    if last:
                        cp(shf[:, 0, cc:cc + 1], pdc[:, q, NT:NT + 1], [Rq], [R('shf')])
                    tt(dif[:, 0, :], pdc[:, q, 0:NT], pdc[:, q, 1:NT + 1], ALU.subtract, [Rq], [R('dif')])
                    dt_, dc_, dR = dsts[cc]
                    stt(dt_[:, dc_, 0:NT], dif[:, 0, :], V_(l, cc), pdc[:, q, 1:NT + 1], ALU.mult, ALU.add,
                        [R('dif'), Rq, RC], [dR])
            if last:
                pb, Rp = psum()
                tr(pb[0:7, 0:128], shf[:, 0, 0:7], [R('shf')], [Rp])
                ot3 = salloc([1, 128])
                cp(ot3[0:7, 0, :], pb[0:7, 0:128], [Rp], [R('ot_sh')])
                S.dma('sp', do['shift_prompt'][l].rearrange("(c p) -> c p", p=128), ot3[0:7, 0, :], reads=[R('ot_sh')])
            if stop == 'Dproj':
                raise _Stop()
            wkv_prep_and_scan(l, n, xr, xk, xv, xl, RG[1], RG[4], RG[5], R('xl'), G, RG, X2, RX2, mixb, Rmix, last, pre)
            return mixb, Rmix

        def wkv_prep_and_scan(l, n, xr, xk, xv, xl, Rr, Rk, Rv, Rl, G, RG, X2, RX2, mixb, Rmix, last, pre=None):
            lact = salloc([1, n], BF16)
            act(lact[0:32, 0, 0:n], xl[0:32, 0, 0:n], AF.Tanh, [Rl], [R('lact')])
            cp(lact[32:64, 0, 0:n], xl[32:64, 0, 0:n], [Rl], [R('lact')])
            act(lact[64:128, 0, 0:n], xl[64:128, 0, 0:n], AF.Sigmoid, [Rl], [R('lact')])
            lw, av, gv = G[0], G[2], G[3]
            for c in range(2):
                pb, Rp = psum()
                mm(pb[:, 0:n], lorab[0:32, l, c * 128:(c + 1) * 128], lact[0:32, 0, 0:n], True, True, [RC, R('lact')], [Rp])
                act(lw[:, c, 0:n], pb[:, 0:n], AF.Sigmoid, [Rp, RC], [RG[0]], bias=V_(l, 15 + c), scale=1.0)
                pb, Rp = psum()
                mm(pb[:, 0:n], lorab[32:64, l, c * 128:(c + 1) * 128], lact[32:64, 0, 0:n], True, True, [RC, R('lact')], [Rp])
                act(av[:, c, 0:n], pb[:, 0:n], AF.Sigmoid, [Rp, RC], [RG[2]], bias=V_(l, 17 + c), scale=1.0)
                pb, Rp = psum()
                mm(pb[:, 0:n], lorab[64:128, l, c * 128:(c + 1) * 128], lact[64:128, 0, 0:n], True, True, [RC, R('lact')], [Rp])
                cp(gv[:, c, 0:n], pb[:, 0:n], [Rp], [RG[3]], eng='act')
            ts(lw[:, :, 0:n], lw[:, :, 0:n], -0.6065306597126334, None, ALU.mult, None, [RG[0]], [RG[0]])
            kk, tmp = G[6], G[7]
            C2 = range(2)
            pc_ = lambda Rx, c: (Rx[c] if isinstance(Rx, tuple) else Rx)
            EN = lambda c: 'dve'
            for c in C2:
                ts(kk[:, c, 0:n], xk[:, c, 0:n], V_(l, 19 + c), None, ALU.mult, None, [pc_(Rk, c), RC], [RG[6][c]], eng=EN(c))
            for c in C2:
                tt(tmp[:, c, 0:n], kk[:, c, 0:n], kk[:, c, 0:n], ALU.mult, [RG[6][c]], [RG[7][c]], eng=EN(c))
            pbs = []
            for c in C2:
                pb, Rp = psum()
                mm(pb[:, 0:n], cmat[:, 1, :], tmp[:, c, 0:n], True, True, [RC, RG[7][c]], [Rp])
                pbs.append((pb, Rp))
            for c in C2:
                pb, Rp = pbs[c]
                ts(tmp[:, c, 0:n], pb[:, 0:n], 1e-12, None, ALU.max, None, [Rp], [RG[7][c]])
            for c in C2:
                act(tmp[:, c, 0:n], tmp[:, c, 0:n], AF.Sqrt, [RG[7][c]], [RG[7][c]])
            for c in C2:
                S.add('dve', lambda e, c=c: e.reciprocal(tmp[:, c, 0:n], tmp[:, c, 0:n]), [RG[7][c]], [RG[7][c]])
            for c in C2:
                tt(kk[:, c, 0:n], kk[:, c, 0:n], tmp[:, c, 0:n], ALU.mult, [RG[6][c], RG[7][c]], [RG[6][c]], eng=EN(c))
            for c in C2:
                ts(tmp[:, c, 0:n], av[:, c, 0:n], V_(l, 21 + c), omka[:, l, c:c + 1], ALU.mult, ALU.add,
                   [RG[2][c], RC], [RG[7][c]], eng=EN(c))
            for c in C2:
                tt(xk[:, c, 0:n], xk[:, c, 0:n], tmp[:, c, 0:n], ALU.mult, [pc_(Rk, c), RG[7][c]], [pc_(Rk, c)], eng=EN(c))
            for c in C2:
                tt(av[:, c, 0:n], av[:, c, 0:n], kk[:, c, 0:n], ALU.mult, [RG[2][c], RG[6][c]], [RG[2][c]], eng=EN(c))
            for c in C2:
                stt(tmp[:, c, 0:n], xr[:, c, 0:n], V_(l, 23 + c), xk[:, c, 0:n], ALU.mult, ALU.mult,
                    [pc_(Rr, c), pc_(Rk, c), RC], [RG[7][c]])
            pbs = []
            for c in C2:
                pb, Rp = psum()
                mm(pb[:, 0:n], cmat[:, 1, :], tmp[:, c, 0:n], True, True, [RC, RG[7][c]], [Rp])
                pbs.append((pb, Rp))
            for c in C2:
                pb, Rp = pbs[c]
                tt(tmp[:, c, 0:n], pb[:, 0:n], xv[:, c, 0:n], ALU.mult, [Rp, pc_(Rv, c)], [RG[7][c]])
            bv, bonus = av, tmp
            Rb, Rbonus = RG[2], RG[7]
            if X2 is None:
                return dict(lw=lw, kk=kk, b=bv, bonus=bonus, g=gv, Rlw=RG[0], Rkk=RG[6], Rb=Rb, Rbonus=Rbonus, Rg=RG[3])
            Lw = X2[0]
            ones = X2[1]
            memset(ones[:, 0, 0:128], 1.0, [RX2[1]])
            for c in range(2):
                for j in range(4):
                    sl = slice(j * 128, (j + 1) * 128)
                    S.add('dve', lambda e, c=c, sl=sl: e.tensor_tensor_scan(Lw[:, c, sl], ones[:, 0, 0:128], lw[:, c, sl],
                                                                            0.0, ALU.mult, ALU.add),
                          [RG[0][c], RX2[1]], [RX2[0][c]])
            KR, kt, bt, gC = pre['KR'], pre['kt'], pre['bt'], pre['gC']
            ex = X2[1]
            J = lambda ap: ap.rearrange("p (j t) -> p j t", t=128)
            RKR = (R('KR', 0), R('KR', 1))
            Rkt = (R('kt', 0), R('kt', 1))
            Rbt = (R('bt', 0), R('bt', 1))
            RgC = (R('gC', 0), R('gC', 1))
            for c in C2:
                act(ex[:, c, 0:NT], Lw[:, c, 0:NT], AF.Exp, [RX2[0][c]], [RX2[1][c]])
            for c in C2:
                tt(KR[:, c, :, 128:256], J(xr[:, c, 0:NT]), J(ex[:, c, 0:NT]), ALU.mult, [Rr[c], RX2[1][c]], [RKR[c]], eng=EN(c))
                cp(gC[:, c, :], J(ex[:, c, 0:NT])[:, :, 127], [RX2[1][c]], [RgC[c]], eng=EN(c))
            for c in C2:
                tt(ex[:, c, 0:NT], Lw[:, c, 0:NT], lw[:, c, 0:NT], ALU.subtract, [RX2[0][c], RG[0][c]], [RX2[1][c]], eng=EN(c))
            for c in C2:
                act(ex[:, c, 0:NT], ex[:, c, 0:NT], AF.Exp, [RX2[1][c]], [RX2[1][c]])
            for c in C2:
                tt(KR[:, c, :, 0:128], J(kk[:, c, 0:NT]), J(ex[:, c, 0:NT]), ALU.mult, [RG[6][c], RX2[1][c]], [RKR[c]], eng=EN(c))
            for c in C2:
                act(ex[:, c, 0:NT], Lw[:, c, 0:NT], AF.Exp, [RX2[0][c], RX2[1][c]], [RX2[1][c]], scale=-1.0)
            for c in C2:
                tt(kt[:, c, :], xk[:, c, 0:NT], ex[:, c, 0:NT], ALU.mult, [Rk[c], RX2[1][c]], [Rkt[c]], eng=EN(c))
                tt(bt[:, c, :], bv[:, c, 0:NT], ex[:, c, 0:NT], ALU.mult, [Rb[c], RX2[1][c]], [Rbt[c]], eng=EN(c))
            if stop == 'Dprep':
                raise _Stop()
            S.barrier()
            scr_off[0] = pre['mark']
            tm = salloc([4, 256], BF16)
            NMh = [salloc([6, 128]) for _ in range(4)]
            A3h = [salloc([3, 128], BF16) for _ in range(4)]
            Ttb = salloc([4, 128], BF16)
            WT = salloc([2, 128], BF16)
            AkV = salloc([4, 64], BF16)
            Un = salloc([4, 64], BF16)
            Gb = salloc([2, 64], BF16)
            Ob = salloc([2, 256])
            gst = salloc([1, 8])
            od = G[0]
            Rtm = R('tm')
            cp(Gb[:, :, :], Gst[:, l, :, :], [R('Gh', h_) for h_ in range(4)], [R('Gbh', h_) for h_ in range(4)])
            for j in range(4):
                sl = slice(j * 128, (j + 1) * 128)
                srcs = [xv, None, kt, bt]
                for qi in range(4):
                    for c in range(2):
                        pb, Rp = psum()
                        if qi == 0:
                            tr(pb[:, 0:128], xv[:, c, sl], [Rv], [Rp])
                            cp(tm[:, qi, c * 128:(c + 1) * 128], pb[:, 0:128], [Rp], [Rtm], eng=('act' if c else 'dve'))
                        else:
                            if qi == 1:
                                src = KR[:, c, j, 0:128]
                                rs = [RKR]
                            else:
                                src = srcs[qi][:, c, sl]
                                rs = [Rkt, Rbt]
                            pbv = pb[:, 0:64].bitcast(BF16)
                            trb(pbv, src, rs, [Rp])
                            cp(tm[:, qi, c * 128:(c + 1) * 128], pbv, [Rp], [Rtm], eng=('act' if c else 'dve'))
                pO, RpO = ps[7], R('ps', 7)
                hd = []
                for h_ in range(4):
                    c, hh = h_ // 2, h_ % 2
                    P0 = hh * 64
                    Ps = slice(P0, P0 + 64)
                    hd.append(dict(c=c, hh=hh, P0=P0, Ps=Ps, kr=KR[Ps, c, j, :], ktj=kt[Ps, c, sl], btj=bt[Ps, c, sl],
                                   Vt=tm[:, 0, h_ * 64:(h_ + 1) * 64], kat=tm[:, 1, h_ * 64:(h_ + 1) * 64],
                                   ktt=tm[:, 2, h_ * 64:(h_ + 1) * 64], btt=tm[:, 3, h_ * 64:(h_ + 1) * 64],
                                   NM=NMh[h_], A3=A3h[h_], RNM=R('NM', h_), RA3=R('A3', h_), ni=0, mi=2, ri=4))
                for h_, d_ in enumerate(hd):
                    NM, A3, RNM, RA3 = d_['NM'], d_['A3'], d_['RNM'], d_['RA3']
                    p1, Rp1 = psum()
                    mm(p1[:, 0:256], d_['btj'], d_['kr'], True, True, [Rbt, RKR], [Rp1])
                    mm(p1[:, 256:384], d_['kr'][:, 0:128], d_['btj'], True, True, [RKR, Rbt], [Rp1])
                    p2, Rp2 = psum()
                    mm(p2[:, 0:256], d_['ktj'], d_['kr'], True, True, [Rkt, RKR], [Rp2])
                    tt(NM[:, 0, :], p1[:, 0:128], cmask[:, 1, :], ALU.mult, [Rp1, RC], [RNM])
                    tt(NM[:, 2, :], p1[:, 256:384], cmask[:, 3, :], ALU.mult, [Rp1, RC], [RNM])
                    tt(A3[:, 2, :], p1[:, 128:256], cmask[:, 2, :], ALU.mult, [Rp1, RC], [RA3])
                    tt(NM[:, 4, :], cmask[:, 0, :], NM[:, 0, :], ALU.subtract, [RNM, RC], [RNM])
                    tt(A3[:, 0:2, :], p2[:, 0:256].rearrange("p (a b) -> p a b", b=128), cmask[:, 1:3, :], ALU.mult,
                       [Rp2, RC], [RA3])
                for lev in range(6):
                    lastlev = (lev == 5)
                    pms = []
                    for d_ in hd:
                        NM, RNM = d_['NM'], d_['RNM']
                        ni, mi = d_['ni'], d_['mi']
                        pm, Rpm = psum()
                        mm(pm[:, 0:128], NM[:, ni, :], NM[:, mi, :], True, True, [RNM], [Rpm])
                        if not lastlev:
                            mm(pm[:, 128:256], NM[:, mi, :], NM[:, ni, :], True, True, [RNM], [Rpm])
                        pms.append((pm, Rpm))
                    for d_, (pm, Rpm) in zip(hd, pms):
                        NM, RNM = d_['NM'], d_['RNM']
                        no, mo = 1 - d_['ni'], 5 - d_['mi']
                        cp(NM[:, mo, :], pm[:, 0:128], [Rpm], [RNM], eng='act')
                        if not lastlev:
                            cp(NM[:, no, :], pm[:, 128:256], [Rpm], [RNM], eng='act')
                        d_['ni'], d_['mi'] = no, mo
                    prs = []
                    for d_ in hd:
                        NM, RNM = d_['NM'], d_['RNM']
                        pr, Rpr = psum()
                        mm(pr[:, 0:128], NM[:, d_['mi'], :], NM[:, d_['ri'], :], True, True, [RNM], [Rpr])
                        prs.append((pr, Rpr))
                    for h_, (d_, (pr, Rpr)) in enumerate(zip(hd, prs)):
                        NM, RNM = d_['NM'], d_['RNM']
                        ro = 9 - d_['ri']
                        if lastlev:
                            tt(Ttb[:, h_, :], pr[:, 0:128], NM[:, d_['ri'], :], ALU.add, [Rpr, RNM], [R('Ttb', h_)])
                        else:
                            tt(NM[:, ro, :], pr[:, 0:128], NM[:, d_['ri'], :], ALU.add, [Rpr, RNM], [RNM])
                        d_['ri'] = ro
                pws = []
                for h_, d_ in enumerate(hd):
                    Tt = Ttb[:, h_, :]
                    P0, Ps, hh = d_['P0'], d_['Ps'], d_['hh']
                    pw, Rpw = psum()
                    mm(pw[P0:P0 + 64, 0:128], d_['kat'], Tt, True, True, [Rtm, R('Ttb', h_)], [Rpw])
                    mm(pw[:, 128:192], d_['A3'][:, 0, :], d_['Vt'], True, True, [d_['RA3'], Rtm], [Rpw])
                    pws.append((pw, Rpw))
                for h_, (d_, (pw, Rpw)) in enumerate(zip(hd, pws)):
                    Ps, hh, c = d_['Ps'], d_['hh'], d_['c']
                    cp(WT[Ps, c, :], pw[Ps, 0:128], [Rpw], [R('WT', h_)], eng='act')
                    cp(AkV[:, h_, :], pw[:, 128:192], [Rpw], [R('AkV', h_)], eng='act')
                pus = []
                for h_, d_ in enumerate(hd):
                    Ps, c = d_['Ps'], d_['c']
                    pu, Rpu = psum()
                    mm(pu[:, 0:64], Ttb[:, h_, :], AkV[:, h_, :], True, False, [R('Ttb', h_), R('AkV', h_)], [Rpu])
                    mm(pu[:, 0:64], WT[Ps, c, :], Gb[Ps, c, :], False, True, [R('WT', h_), R('Gbh', h_)], [Rpu])
                    pus.append((pu, Rpu))
                for h_, (d_, (pu, Rpu)) in enumerate(zip(hd, pus)):
                    ts(Un[:, h_, :], pu[:, 0:64], -1.0, None, ALU.mult, None, [Rpu], [R('Un', h_)])
                pgs = []
                for h_, d_ in enumerate(hd):
                    Ps, c, P0 = d_['Ps'], d_['c'], d_['P0']
                    oo = pO[:, h_ * 64:(h_ + 1) * 64]
                    mm(oo, d_['kr'][:, 128:256], Gb[Ps, c, :], True, False, [RKR, R('Gbh', h_)], [RpO])
                    mm(oo, d_['A3'][:, 1, :], d_['Vt'], False, False, [d_['RA3'], Rtm], [RpO])
                    mm(oo, d_['A3'][:, 2, :], Un[:, h_, :], False, True, [d_['RA3'], R('Un', h_)], [RpO])
                    pg, Rpg = psum()
                    mm(pg[P0:P0 + 64, 0:64], d_['ktt'], d_['Vt'], True, False, [Rtm], [Rpg])
                    mm(pg[P0:P0 + 64, 0:64], d_['btt'], Un[:, h_, :], False, True, [Rtm, R('Un', h_)], [Rpg])
                    pgs.append((pg, Rpg))
                for h_, (d_, (pg, Rpg)) in enumerate(zip(hd, pgs)):
                    Ps, c = d_['Ps'], d_['c']
                    tt(Gst[Ps, l, c, :], pg[Ps, 0:64], Gst[Ps, l, c, :], ALU.add, [Rpg, R('Gh', h_)], [R('Gh', h_)])
                    ts(Gst[Ps, l, c, :], Gst[Ps, l, c, :], gC[Ps, c, j:j + 1], None, ALU.mult, None,
                       [R('Gh', h_), RgC], [R('Gh', h_)])
                    cp(Gb[Ps, c, :], Gst[Ps, l, c, :], [R('Gh', h_)], [R('Gbh', h_)], eng='act')
                ob = Ob[:, 0, :]
                on = Ob[:, 1, :]
                Rob = R('Ob')
                cp(ob, pO[:, 0:256], [RpO], [Rob], eng='act')
                sm = gst[:, 0, 0:4]
                sq = gst[:, 0, 4:8]
                ob3 = ob.rearrange("p (h v) -> p h v", v=64)
                on3 = on.rearrange("p (h v) -> p h v", v=64)
                S.add('dve', lambda e, sm=sm, ob3=ob3: e.tensor_reduce(sm, ob3, AX.X, ALU.add), [Rob], [R('gst')])
                ts(sm, sm, 1.0 / 64, None, ALU.mult, None, [R('gst')], [R('gst')])
                tt(on3, ob3, sm.unsqueeze(2).broadcast_to([128, 4, 64]), ALU.subtract, [Rob, R('gst')], [Rob])
                tt(ob3, on3, on3, ALU.mult, [Rob], [Rob])
                S.add('dve', lambda e, sq=sq, ob3=ob3: e.tensor_reduce(sq, ob3, AX.X, ALU.add), [Rob], [R('gst')])
                ts(sq, sq, 1.0 / 64, GN_EPS, ALU.mult, ALU.add, [R('gst')], [R('gst')])
                act(sq, sq, AF.Sqrt, [R('gst')], [R('gst')])
                S.add('dve', lambda e, sq=sq: e.reciprocal(sq, sq), [R('gst')], [R('gst')])
                tt(on3, on3, sq.unsqueeze(2).broadcast_to([128, 4, 64]), ALU.mult, [Rob, R('gst')], [Rob])
                for c in range(2):
                    pb, Rp = psum()
                    tr(pb[:, 0:128], on[:, c * 128:(c + 1) * 128], [Rob], [Rp])
                    act(od[:, c, sl], pb[:, 0:128], AF.Identity, [Rp, RC], [RG[0]], bias=V_(l, 27 + c), scale=V_(l, 25 + c))
            for c in range(2):
                tt(od[:, c, 0:NT], od[:, c, 0:NT], bonus[:, c, 0:NT], ALU.add, [RG[0], Rbonus], [RG[0]])
                tt(mixb[:, 6 + c, 0:NT], od[:, c, 0:NT], G[3][:, c, 0:NT], ALU.mult, [RG[0], RG[3]], [Rmix(6 + c)])
            if last:
                wo = salloc([2, 128])
                for c in range(2):
                    pb, Rp = psum()
                    tr(pb[0:64, 0:128], Gst[:, l, c, :], [R('Gh', 2 * c), R('Gh', 2 * c + 1)], [Rp])
                    cp(wo[0:64, c, :], pb[0:64, 0:128], [Rp], [R('wo')])
                    S.dma('sp', do['wkv_prompt'][l, 2 * c:2 * c + 2].rearrange("h v k -> v h k"),
                          wo[0:64, c, :].rearrange("v (h k) -> v h k", k=64), reads=[R('wo')])

        def attn_prompt(l, n, ext=False):
            ob_ = salloc([8, NTX], BF16)
            attn_mark[0] = scr_off[0]
            qT = salloc([8, NT], BF16)
            qTS = salloc([8, NS]) if ext else None
            for g in range(2):
                wv, Rw = wload(wsrc('w_xq', l, g * 512, 512), 8, 512)
                for j in range(4):
                    c = g * 4 + j
                    pb, Rp = psum()
                    for k in range(8):
                        mm(pb[:, :], wv[:, k, j * 128:(j + 1) * 128], hb[:, k, 0:NT], k == 0, k == 7, [Rw, R('hb', k)], [Rp])
                    act(qT[:, c, :], pb[:, :], AF.Copy, [Rp], [R('qT', c)], scale=1.0 / 16)
                    if ext:
                        pS, RpS = psum()
                        for k in range(8):
                            mm(pS[:, 0:NS], wv[:, k, j * 128:(j + 1) * 128], hb[:, k, NT:NTX], k == 0, k == 7, [Rw, R('hb', k)], [RpS])
                        act(qTS[:, c, :], pS[:, 0:NS], AF.Copy, [RpS], [R('qTS')], scale=1.0 / 16)
            Pf = salloc([2, 4, 256])
            PT = salloc([2, 4, NT], BF16)
            mx = salloc([2, 8])
            for s_ in range(4):
                q = s_ % 2
                sl = slice(s_ * 128, (s_ + 1) * 128)
                RP = R('Pf', q)
                pbs = []
                for half in range(2):
                    pb, Rp = psum()
                    pbs.append((pb, Rp))
                    for hh in range(2):
                        h_ = half * 2 + hh
                        for i in range(2):
                            mm(pb[:, hh * 256:(hh + 1) * 256], qT[:, 2 * h_ + i, sl], KT[:, l, 2 * h_ + i, :], i == 0, i == 1,
                               [R('qT', 2 * h_ + i), R('KT', l)], [Rp])
                for half in range(2):
                    pb, Rp = pbs[half]
                    S.add('dve', lambda e, q=q, half=half, pb=pb: e.tensor_reduce(
                        mx[:, q, half * 2:half * 2 + 2], pb[:, :].rearrange("p (h m) -> p h m", m=256), AX.X, ALU.max),
                        [Rp], [R('mx', q)])
                ts(mx[:, q, 0:4], mx[:, q, 0:4], -1.0, None, ALU.mult, None, [R('mx', q)], [R('mx', q)])
                memset(mx[:, q, 4:8], 0.0, [R('mx', q)])
                for h_ in range(4):
                    pb, Rp = pbs[h_ // 2]
                    act(Pf[:, q, h_, :], pb[:, (h_ % 2) * 256:(h_ % 2 + 1) * 256], AF.Exp, [Rp, R('mx', q)], [RP, R('mx', q)],
                        bias=mx[:, q, h_:h_ + 1], scale=1.0, accum_out=mx[:, q, 4 + h_:5 + h_])
                S.add('dve', lambda e, q=q: e.reciprocal(mx[:, q, 4:8], mx[:, q, 4:8]), [R('mx', q)], [R('mx', q)])
                tt(Pf[:, q, :, :], Pf[:, q, :, :], mx[:, q, 4:8].unsqueeze(2).broadcast_to([128, 4, 256]), ALU.mult,
                   [RP, R('mx', q)], [RP])
                for h_ in range(4):
                    pb, Rp = psum()
                    for ms in range(2):
                        tr(pb[:, ms * 128:(ms + 1) * 128], Pf[:, q, h_, ms * 128:(ms + 1) * 128], [RP], [Rp])
                    cp(PT[:, :, h_, sl], pb[:, 0:256].rearrange("p (a b) -> p a b", b=128), [Rp], [R('PT', h_)],
                       eng=('act' if h_ % 2 else 'dve'))
            for c in range(8):
                h_ = c // 2
                pb, Rp = psum()
                for ms in range(2):
                    mm(pb[:, :], Vm[:, l, ms, c * 128:(c + 1) * 128], PT[:, ms, h_, :], ms == 0, ms == 1,
                       [R('Vm', l), R('PT', h_)], [Rp])
                cp(ob_[:, c, 0:NT], pb[:, :], [Rp], [R('ob', c)], eng=('act' if c % 2 else 'dve'))
            return ob_, qTS

        def ffn(l, n, ext=False):
            nx = n + NS if ext else n
            hid = salloc([22, NTX], BF16)
            sg = salloc([2, NTX])
            for (c0, cols) in [(g * 256, 256) for g in range(11)]:
                i_ = w_rr[0] % NW
                w_rr[0] += 1
                wv_ = wsl[i_][:, 0:4096].rearrange("p (k c) -> p k c", c=512)
                Rw1 = Rw3 = R('wsl', i_)
                S.dma('pool', wv_[:, :, 0:256], wsrc('ffn_w1', l, c0, cols), writes=[Rw1])
                S.dma('pool', wv_[:, :, 256:512], wsrc('ffn_w3', l, c0, cols), writes=[Rw1])
                wv1 = wv_[:, :, 0:256]
                wv3 = wv_[:, :, 256:512]
                for j in range(cols // 128):
                    c = (c0 + j * 128) // 128
                    q = c % 2
                    pa, Rpa = psum()
                    for k in range(8):
                        mm(pa[:, 0:n], wv1[:, k, j * 128:(j + 1) * 128], hb[:, k, 0:n], k == 0, k == 7, [Rw1, R('hb', k)], [Rpa])
                    pb, Rpb = psum()
                    for k in range(8):
                        mm(pb[:, 0:n], wv3[:, k, j * 128:(j + 1) * 128], hb[:, k, 0:n], k == 0, k == 7, [Rw3, R('hb', k)], [Rpb])
                    act(sg[:, q, 0:n], pa[:, 0:n], AF.Silu, [Rpa], [R('sg', q)])
                    tt(hid[:, c, 0:n], sg[:, q, 0:n], pb[:, 0:n], ALU.mult, [R('sg', q), Rpb], [R('hid', c)])
                    if ext:
                        pS, RpS = psum()
                        for k in range(8):
                            mm(pS[:, 0:NS], wv1[:, k, j * 128:(j + 1) * 128], hb[:, k, n:nx], k == 0, k == 7, [Rw1, R('hb', k)], [RpS])
                        for k in range(8):
                            mm(pS[:, NS:2 * NS], wv3[:, k, j * 128:(j + 1) * 128], hb[:, k, n:nx], k == 0, k == 7, [Rw3, R('hb', k)], [RpS])
                        act(sg[:, q, n:nx], pS[:, 0:NS], AF.Silu, [RpS], [R('sg', q)])
                        tt(hid[:, c, n:nx], sg[:, q, n:nx], pS[:, NS:2 * NS], ALU.mult, [R('sg', q), RpS], [R('hid', c)])
            return hid

        def rows_out(chunks, rchunks, dst, n, tag):
            pb, Rp = psum()
            for i, ch in enumerate(chunks):
                tr(pb[0:n, i * 128:(i + 1) * 128], ch, rchunks, [Rp])
            w = len(chunks) * 128
            ot = salloc([1, w])
            cp(ot[0:n, 0, :], pb[0:n, 0:w], [Rp], [R('rows', tag)])
            S.dma('sp', dst, ot[0:n, 0, :], reads=[R('rows', tag)])

        def mixer_sample_body(l, mixb_full, Rmix):
            n = NS
            mixb = mixb_full[:, :, NT:NTX]
            pj = pjS
            Rpj = R('pjS')
            uv = salloc([4, n])
            Ruv = R('uvS')
            act(uv[:, :, :], pj[:, 0:4, :], AF.Gelu_apprx_tanh, [Rpj], [Ruv])
            sqv = salloc([2, n])
            tt(sqv[:, :, :], uv[:, 2:4, :], uv[:, 2:4, :], ALU.mult, [Ruv], [R('sqvS')])
            pm, Rpm = psum()
            pq, Rpq = psum()
            for c in range(2):
                mm(pm[:, 0:n], cmat[:, 0, :], uv[:, 2 + c, :], c == 0, c == 1, [RC, Ruv], [Rpm])
                mm(pq[:, 0:n], cmat[:, 0, :], sqv[:, c, :], c == 0, c == 1, [RC, R('sqvS')], [Rpq])
            stv = salloc([3, n])
            Rst = R('stvS')
            ts(stv[:, 0, :], pm[:, 0:n], 4.0, None, ALU.mult, None, [Rpm], [Rst])
            ts(stv[:, 1, :], pq[:, 0:n], 4.0, None, ALU.mult, None, [Rpq], [Rst])
            tt(stv[:, 2, :], stv[:, 0, :], stv[:, 0, :], ALU.mult, [Rst], [Rst])
            tt(stv[:, 1, :], stv[:, 1, :], stv[:, 2, :], ALU.subtract, [Rst], [Rst])
            act(stv[:, 1, :], stv[:, 1, :], AF.Sqrt, [Rst], [Rst], bias=LN_EPS, scale=1.0)
            S.add('dve', lambda e: e.reciprocal(stv[:, 1, :], stv[:, 1, :]), [Rst], [Rst])
            vln = salloc([2, n])
            Rvl = R('vlnS')
            ztmp = salloc([2, n])
            for c in range(2):
                tt(vln[:, c, :], uv[:, 2 + c, :], stv[:, 0, :], ALU.subtract, [Ruv, Rst], [Rvl])
                tt(vln[:, c, :], vln[:, c, :], stv[:, 1, :], ALU.mult, [Rvl, Rst], [Rvl])
                act(vln[:, c, :], vln[:, c, :], AF.Identity, [Rvl, RC], [Rvl], bias=V_(l, 82 + c), scale=V_(l, 80 + c))
            rows_out([vln[:, 0, :], vln[:, 1, :]], [Rvl], do['chunk_v_sample'][l], n, 'cv')
            for c in range(2):
                ts(ztmp[:, c, :], vln[:, c, :], V_(l, 84 + c), V_(l, 86 + c), ALU.mult, ALU.add, [Rvl, RC], [R('ztS')])
                tt(mixb[:, c, :], ztmp[:, c, :], uv[:, c, :], ALU.mult, [R('ztS'), Ruv], [Rmix(c)])
            sprow = salloc([2, 256])
            src = di['state_pool'][l].rearrange("s r f -> (s r) f")
            S.dma('sp', sprow[0:128, 0, :], src[0:128, :], writes=[R('sprow')])
            S.dma('sp', sprow[0:112, 1, :], src[128:240, :], writes=[R('sprow')])
            xpS = salloc([2, 16, 16])
            Rxp = R('xpS')
            for c in range(2):
                pb, Rp = psum()
                tr(pb[:, 0:128], sprow[0:128, 0, c * 128:(c + 1) * 128], [R('sprow')], [Rp])
                tr(pb[:, 128:240], sprow[0:112, 1, c * 128:(c + 1) * 128], [R('sprow')], [Rp])
                cp(xpS[:, c, :, 0:15], pb[:, 0:240].rearrange("p (s r) -> p s r", r=15), [Rp], [Rxp])
                cp(xpS[:, c, :, 15:16], pj[:, 4 + c, :].unsqueeze(2), [Rpj], [Rxp])
            wsum = salloc([2, n])
            for (c, p0, win) in [(0, 0, 2), (0, 64, 4), (1, 0, 8), (1, 64, 16)]:
                S.add('dve', lambda e, c=c, p0=p0, win=win: e.tensor_reduce(
                    wsum[p0:p0 + 64, c, :], xpS[p0:p0 + 64, c, :, 16 - win:16], AX.X, ALU.add), [Rxp], [R('wsumS')])
            dS = salloc([2, n], BF16)
            for c in range(2):
                ts(wsum[:, c, :], wsum[:, c, :], V_(l, 77 + c), None, ALU.mult, None, [R('wsumS'), RC], [R('wsumS')])
                tt(dS[:, c, :], wsum[:, c, :], pj[:, 4 + c, :], ALU.subtract, [R('wsumS'), Rpj], [R('dSS')])
                pb, Rp = psum()
                mm(pb[:, 0:n], wpbd[:, l, c, :], dS[:, c, :], True, True, [RC, R('dSS')], [Rp])
                act(mixb[:, 2 + c, :], pb[:, 0:n], AF.Identity, [Rp, RC], [Rmix(2 + c)], scale=V_(l, 13 + c))
            S.dma('sp', do['pool_sample'][l, :, 0:14, :], di['state_pool'][l, :, 1:15, :])
            rows_out([pj[:, 4, :], pj[:, 5, :]], [Rpj], do['pool_sample'][l, :, 14, :], n, 'pl')
            cvrow = salloc([1, 256])
            S.dma('sp', cvrow[0:32, 0, :], di['state_conv'][l].rearrange("s j f -> (s j) f"), writes=[R('cvrow')])
            zc2 = salloc([2, 16, 2])
            zz = salloc([2, n])
            yy = salloc([2, n])
            for c in range(2):
                pb, Rp = psum()
                tr(pb[:, 0:32], cvrow[0:32, 0, c * 128:(c + 1) * 128], [R('cvrow')], [Rp])
                cp(zc2[:, c, :, :], pb[:, 0:32].rearrange("p (s j) -> p s j", j=2), [Rp], [R('zc2S')])
                tt(zz[:, c, :], pj[:, 8 + c, :], pj[:, 10 + c, :], ALU.mult, [Rpj], [R('zzS')])
                ts(yy[:, c, :], zc2[:, c, :, 0], V_(l, 7 + c), None, ALU.mult, None, [R('zc2S'), RC], [R('yyS')])
                stt(yy[:, c, :], zc2[:, c, :, 1], V_(l, 9 + c), yy[:, c, :], ALU.mult, ALU.add, [R('zc2S'), R('yyS'), RC], [R('yyS')])
                stt(yy[:, c, :], zz[:, c, :], V_(l, 11 + c), yy[:, c, :], ALU.mult, ALU.add, [R('zzS'), R('yyS'), RC], [R('yyS')])
                tt(mixb[:, 4 + c, :], yy[:, c, :], pj[:, 6 + c, :], ALU.mult, [R('yyS'), Rpj], [Rmix(4 + c)])
            S.dma('sp', do['conv_sample'][l, :, 0, :], di['state_conv'][l, :, 1, :])
            rows_out([zz[:, 0, :], zz[:, 1, :]], [R('zzS')], do['conv_sample'][l, :, 1, :], n, 'cvo')
            shrow = salloc([1, 896])
            S.dma('sp', shrow[0:16, 0, :], di['state_shift'][l], writes=[R('shrow')])
            prev = salloc([7, n])
            pb, Rp = psum()
            for cc in range(7):
                tr(pb[:, cc * 16:(cc + 1) * 16], shrow[0:16, 0, cc * 128:(cc + 1) * 128], [R('shrow')], [Rp])
            cp(prev[:, :, :], pb[:, 0:112].rearrange("p (c s) -> p c s", s=16), [Rp], [R('prevS')])
            xs = salloc([7, n])
            Rxs = R('xsS')
            for cc in range(7):
                tt(prev[:, cc, :], prev[:, cc, :], pj[:, 12 + cc, :], ALU.subtract, [R('prevS'), Rpj], [R('prevS')])
                stt(xs[:, cc, :], prev[:, cc, :], V_(l, cc), pj[:, 12 + cc, :], ALU.mult, ALU.add, [R('prevS'), Rpj, RC], [Rxs])
            rows_out([pj[:, 12 + i, :] for i in range(4)], [Rpj], do['shift_sample'][l][:, 0:512], n, 'sh0')
            rows_out([pj[:, 16 + i, :] for i in range(3)], [Rpj], do['shift_sample'][l][:, 512:896], n, 'sh1')
            Gs = [salloc([2, n]) for _ in range(8)]
            RGs_ = [(R('GS', i, 0), R('GS', i, 1)) for i in range(8)]
            xr, xk, xv = xs[:, 0:2, :], xs[:, 2:4, :], xs[:, 4:6, :]
            xl = xs[:, 6:7, :]
            pr = wkv_prep_and_scan(l, n, xr, xk, xv, xl, Rxs, Rxs, Rxs, Rxs, Gs, RGs_, None, None, None, None, False)
            lw, kk, bvv, bonus, gv = pr['lw'], pr['kk'], pr['b'], pr['bonus'], pr['g']
            dec = salloc([2, n])
            act(dec[:, :, :], lw[:, :, 0:n], AF.Exp, [pr['Rlw']], [R('decS')])
            tm5 = salloc([5, 256])
            Rtm5 = R('tm5S')
            qsrc = [(kk, pr['Rkk']), (dec, R('decS')), (bvv, pr['Rb']), (xk, Rxs), (xr, Rxs)]
            for qi, (qa, qR) in enumerate(qsrc):
                pb, Rp = psum()
                for c in range(2):
                    tr(pb[0:16, c * 128:(c + 1) * 128], qa[:, c, 0:n], [qR], [Rp])
                cp(tm5[0:16, qi, :], pb[0:16, 0:256], [Rp], [Rtm5], eng=('act' if qi % 2 else 'dve'))
            esel = salloc([16, 128])
            S.dma('sp', esel[0:16, :, :], di['esel'].rearrange("p (a b) -> p a b", b=128), writes=[R('eselS')])
            S2 = salloc([2, 16, 64])
            RS2 = R('S2S')
            for c in range(2):
                S.dma('sp', S2[:, c, :, :], di['state_wkv'][l, :, 2 * c:2 * c + 2].rearrange("s hh v k -> (hh v) s k"), writes=[RS2])
            HS = 8
            bc5 = salloc([5, 2, HS * 64])
            Rbc = R('bc5S')
            tm5v = tm5[0:16, :, :].rearrange("p q (c hh k) -> p q c hh k", hh=2, k=64)
            tS = salloc([2, HS * 64])
            RtS = R('tSS')
            sa = salloc([2, HS])
            osm = salloc([2, n])
            t4 = tS[:, :, :].rearrange("p c (s k) -> p c s k", k=64)
            B4 = lambda q_: bc5[:, q_, :, :].rearrange("p c (s k) -> p c s k", k=64)
            for half in range(2):
                for si in range(HS):
                    s_ = half * HS + si
                    pA, RpA = psum()
                    pB, RpB = psum()
                    for hh in range(2):
                        mm(pA[hh * 64:(hh + 1) * 64, 0:384].rearrange("p (q c k) -> p q c k", c=2, k=64),
                           esel[0:16, s_, 0:64], tm5v[:, 0:3, :, hh, :], True, True, [R('eselS'), Rtm5], [RpA])
                        mm(pB[hh * 64:(hh + 1) * 64, 0:256].rearrange("p (q c k) -> p q c k", c=2, k=64),
                           esel[0:16, s_, 0:64], tm5v[:, 3:5, :, hh, :], True, True, [R('eselS'), Rtm5], [RpB])
                    cp(bc5[:, 0:3, :, si * 64:(si + 1) * 64], pA[:, 0:384].rearrange("p (q c k) -> p q c k", c=2, k=64),
                       [RpA], [Rbc], eng='act')
                    cp(bc5[:, 3:5, :, si * 64:(si + 1) * 64], pB[:, 0:256].rearrange("p (q c k) -> p q c k", c=2, k=64),
                       [RpB], [Rbc])
                ss = slice(half * HS, (half + 1) * HS)
                S4 = S2[:, :, ss, :]
                tt(t4, S4, B4(0), ALU.mult, [RS2, Rbc], [RtS])
                S.add('dve', lambda e: e.tensor_reduce(sa[:, :, :], t4, AX.X, ALU.add), [RtS], [R('saS')])
                tt(S4, S4, B4(1), ALU.mult, [RS2, Rbc], [RS2])
                tt(t4, B4(2), sa[:, :, :].unsqueeze(3).broadcast_to([128, 2, HS, 64]), ALU.mult, [Rbc, R('saS')], [RtS])
                tt(S4, S4, t4, ALU.subtract, [RS2, RtS], [RS2])
                tt(t4, B4(3), xv[:, :, ss].unsqueeze(3).broadcast_to([128, 2, HS, 64]), ALU.mult, [Rbc, Rxs], [RtS])
                tt(S4, S4, t4, ALU.add, [RS2, RtS], [RS2])
                tt(t4, S4, B4(4), ALU.mult, [RS2, Rbc], [RtS])
                S.add('dve', lambda e, ss=ss: e.tensor_reduce(osm[:, :, ss], t4, AX.X, ALU.add), [RtS], [R('osmS')])
            for c in range(2):
                S.dma('sp', do['wkv_sample'][l, :, 2 * c:2 * c + 2].rearrange("s hh v k -> (hh v) s k"), S2[:, c, :, :], reads=[RS2])
            osq = salloc([2, n])
            tt(osq[:, :, :], osm[:, :, :], osm[:, :, :], ALU.mult, [R('osmS')], [R('osqS')])
            gst = salloc([4, n])
            Rgs = R('gstS')
            for c in range(2):
                pm, Rpm = psum()
                mm(pm[:, 0:n], cmat[:, 1, :], osm[:, c, :], True, True, [RC, R('osmS')], [Rpm])
                ts(gst[:, c, :], pm[:, 0:n], 1.0 / 64, None, ALU.mult, None, [Rpm], [Rgs])
                pq, Rpq = psum()
                mm(pq[:, 0:n], cmat[:, 1, :], osq[:, c, :], True, True, [RC, R('osqS')], [Rpq])
                ts(gst[:, 2 + c, :], pq[:, 0:n], 1.0 / 64, None, ALU.mult, None, [Rpq], [Rgs])
            tt(osq[:, :, :], gst[:, 0:2, :], gst[:, 0:2, :], ALU.mult, [Rgs], [R('osqS')])
            tt(gst[:, 2:4, :], gst[:, 2:4, :], osq[:, :, :], ALU.subtract, [Rgs, R('osqS')], [Rgs])
            act(gst[:, 2:4, :], gst[:, 2:4, :], AF.Sqrt, [Rgs], [Rgs], bias=GN_EPS, scale=1.0)
            S.add('dve', lambda e: e.reciprocal(gst[:, 2:4, :], gst[:, 2:4, :]), [Rgs], [Rgs])
            tt(osm[:, :, :], osm[:, :, :], gst[:, 0:2, :], ALU.subtract, [R('osmS'), Rgs], [R('osmS')])
            tt(osm[:, :, :], osm[:, :, :], gst[:, 2:4, :], ALU.mult, [R('osmS'), Rgs], [R('osmS')])
            for c in range(2):
                act(osm[:, c, :], osm[:, c, :], AF.Identity, [R('osmS'), RC], [R('osmS')], bias=V_(l, 27 + c), scale=V_(l, 25 + c))
                tt(osm[:, c, :], osm[:, c, :], bonus[:, c, 0:n], ALU.add, [R('osmS'), pr['Rbonus']], [R('osmS')])
                tt(mixb[:, 6 + c, :], osm[:, c, :], gv[:, c, 0:n], ALU.mult, [R('osmS'), pr['Rg']], [Rmix(6 + c)])
            return

        def attn_sample_body(l, qT, ob_full):
            n = NS
            qtm = salloc([1, D])
            for g in range(2):
                pb, Rp = psum()
                for j in range(4):
                    tr(pb[0:16, j * 128:(j + 1) * 128], qT[:, g * 4 + j, :], [R('qTS')], [Rp])
                cp(qtm[0:16, 0, g * 512:(g + 1) * 512], pb[0:16, :], [Rp], [R('qtmS')], eng=('act' if g else 'dve'))
            esel = salloc([16, 128])
            S.dma('sp', esel[0:16, :, :], di['esel'].rearrange("p (a b) -> p a b", b=128), writes=[R('eselA')])
            Kb = salloc([3, 2, D])
            Vb = Kb
            prod = salloc([1, D])
            scT = salloc([2, 64])
            for s_ in range(NS):
                q = s_ % 3
                for mt in range(2):
                    S.dma('sp', Kb[:, q, mt, :], di['cache_mem_k'][l, s_, mt * 128:(mt + 1) * 128, :], writes=[R('Kb', q, mt)])
                pbs = []
                for g in range(2):
                    pb, Rp = psum()
                    mm(pb[:, :], esel[0:16, s_, :], qtm[0:16, 0, g * 512:(g + 1) * 512], True, True, [R('eselA'), R('qtmS')], [Rp])
                    pbs.append((pb, Rp))
                for mt in range(2):
                    for g in range(2):
                        pb, Rp = pbs[g]
                        tt(prod[:, 0, g * 512:(g + 1) * 512], Kb[:, q, mt, g * 512:(g + 1) * 512], pb[:, :], ALU.mult,
                           [R('Kb', q, mt), Rp], [R('prod')])
                    S.add('dve', lambda e, mt=mt, s_=s_: e.tensor_reduce(
                        scT[:, mt, s_ * 4:(s_ + 1) * 4], prod[:, 0, :].rearrange("p (h d) -> p h d", d=256), AX.X, ALU.add),
                        [R('prod')], [R('scT')])
            psc, Rpsc = psum()
            for mt in range(2):
                tr(psc[0:64, mt * 128:(mt + 1) * 128], scT[:, mt, :], [R('scT')], [Rpsc])
            sm = salloc([1, 256])
            mxs = salloc([1, 2])
            Rsm = R('smS')
            cp(sm[0:64, 0, :], psc[0:64, 0:256], [Rpsc], [Rsm])
            S.add('dve', lambda e: e.tensor_reduce(mxs[0:64, 0, 0:1], sm[0:64, 0, :], AX.X, ALU.max), [Rsm], [R('mxsS')])
            ts(mxs[0:64, 0, 0:1], mxs[0:64, 0, 0:1], -1.0, None, ALU.mult, None, [R('mxsS')], [R('mxsS')])
            memset(mxs[0:64, 0, 1:2], 0.0, [R('mxsS')])
            act(sm[0:64, 0, :], sm[0:64, 0, :], AF.Exp, [Rsm, R('mxsS')], [Rsm, R('mxsS')], bias=mxs[0:64, 0, 0:1], scale=1.0,
                accum_out=mxs[0:64, 0, 1:2])
            S.add('dve', lambda e: e.reciprocal(mxs[0:64, 0, 1:2], mxs[0:64, 0, 1:2]), [R('mxsS')], [R('mxsS')])
            ts(sm[0:64, 0, :], sm[0:64, 0, :], mxs[0:64, 0, 1:2], None, ALU.mult, None, [Rsm, R('mxsS')], [Rsm])
            PTs = salloc([2, 64])
            for mt in range(2):
                pb, Rp = psum()
                tr(pb[:, 0:64], sm[0:64, 0, mt * 128:(mt + 1) * 128], [Rsm], [Rp])
                cp(PTs[:, mt, :], pb[:, 0:64], [Rp], [R('PTsS')], eng=('act' if mt else 'dve'))
            po, Rpo = ps[6], R('ps', 6)
            for s_ in range(NS):
                q = s_ % 3
                for mt in range(2):
                    S.dma('sp', Vb[:, q, mt, :], di['cache_mem_v'][l, s_, mt * 128:(mt + 1) * 128, :], writes=[R('Kb', q, mt)])
                for c in range(8):
                    h_ = c // 2
                    for mt in range(2):
                        mm(po[:, c * 16 + s_:c * 16 + s_ + 1], Vb[:, q, mt, c * 128:(c + 1) * 128],
                           PTs[:, mt, s_ * 4 + h_:s_ * 4 + h_ + 1], mt == 0, mt == 1, [R('Kb', q, mt), R('PTsS')], [Rpo])
            cp(ob_full[:, :, NT:NTX], po[:, 0:128].rearrange("p (c s) -> p c s", s=16), [Rpo], [R('ob', c) for c in range(8)])
            return

        if stop != 'nosetup':
            setup_layers()
        for ti in range(ntile):
            ext = (ti == 0) and do_sample
            phase()
            load_tokens(di['x_prompt'][ti * NT:(ti + 1) * NT, :], NT)
            if ext:
                load_tokens(di['x_sample'], NS, col0=NT)
            for l in range(nlayer):
                phase()
                mixb, Rmix = mixer_prompt(l, ti, ext)
                if ext:
                    S.barrier()
                    scr_off[0] = (8 * NTX + 1) // 2 + 2
                    mixer_sample_body(l, mixb, Rmix)
                dense_res_ln(l, 'w_out', mixb, Rmix, 8, NT, 29, 37, ext)
                phase()
                ob_, qTS = attn_prompt(l, NT, ext)
                if ext:
                    attn_sample_body(l, qTS, ob_)
                S.barrier()
                scr_off[0] = attn_mark[0]
                dense_res_ln(l, 'w_xo', ob_, lambda k: R('ob', k), 8, NT, 45, 53, ext)
                hid = ffn(l, NT, ext)
                dense_res_ln(l, 'ffn_w2', hid, lambda k: R('hid', k), 22, NT, 61, 69, ext)
            phase()
            store_tokens(do['y_prompt'][ti * NT:(ti + 1) * NT, :], NT)
            if ext:
                store_tokens(do['y_sample'], NS, col0=NT)
        build_program.scr_max = scr_max[0]
        S.emit()
    return nc


def _host_pack(inp):
    f = lambda k: np.asarray(inp[k], dtype=np.float32)
    vec = np.zeros((128, L * NVEC), np.float32)
    bvec = np.zeros((128, L * 768), np.float32)
    pc = lambda v: v.reshape(-1, 128).T
    for l in range(L):
        b = l * NVEC
        vec[:, b + 0:b + 7] = pc(f('mu_d')[l])
        cw = f('conv_w')[l]
        for j in range(3):
            vec[:, b + 7 + 2 * j:b + 9 + 2 * j] = pc(cw[j])
        for off, nm in ((13, 'pool_scale'), (15, 'rwkv_w0'), (17, 'rwkv_a0'), (19, 'rwkv_k_k'), (21, 'rwkv_k_a'),
                        (25, 'rwkv_lnx_g'), (27, 'rwkv_lnx_b')):
            vec[:, b + off:b + off + 2] = pc(f(nm)[l])
        vec[:, b + 23:b + 25] = pc(f('rwkv_r_k')[l].reshape(-1))
        for off, nm in ((29, 'ln1_g'), (37, 'ln1_b'), (45, 'ln2_g'), (53, 'ln2_b'), (61, 'ln3_g'), (69, 'ln3_b')):
            vec[:, b + off:b + off + 8] = pc(f(nm)[l])
        vec[:, b + 80:b + 82] = pc(f('ln_v_g')[l])
        vec[:, b + 82:b + 84] = pc(f('ln_v_b')[l])
        wsl0 = f('ws_chunk')[l][:, 0, 0]
        bsl0 = f('b_chunk')[l][:, 0]
        for c in range(2):
            vec[0:64, b + 84 + c] = wsl0[2 * c]
            vec[64:128, b + 84 + c] = wsl0[2 * c + 1]
            vec[0:64, b + 86 + c] = bsl0[2 * c]
            vec[64:128, b + 86 + c] = bsl0[2 * c + 1]
        bb = l * 768
        bvec[:, bb:bb + 256] = f('ln_v_g')[l][None, :]
        bvec[:, bb + 256:bb + 512] = f('ln_v_b')[l][None, :]
        bc = f('b_chunk')[l]
        for c in range(2):
            bvec[0:64, bb + 512 + c * 128:bb + 512 + (c + 1) * 128] = bc[2 * c][None, :]
            bvec[64:128, bb + 512 + c * 128:bb + 512 + (c + 1) * 128] = bc[2 * c + 1][None, :]
    wins = np.zeros((128, 2), np.float32)
    wins[0:64, 0], wins[64:128, 0], wins[0:64, 1], wins[64:128, 1] = 2, 4, 8, 16
    for l in range(L):
        vec[:, l * NVEC + 77:l * NVEC + 79] = 1.0 / wins
    t = np.arange(16, dtype=np.float32)[None, None, :]
    invcnt0 = (1.0 / np.minimum(wins[:, :, None], t + 1)).astype(np.float32).reshape(128, 32)
    ii = np.arange(128)
    ident = np.eye(128, dtype=np.float32)
    su = (ii[:, None] < ii[None, :]).astype(np.float32)
    iu = (ii[:, None] <= ii[None, :]).astype(np.float32)
    sl_ = (ii[:, None] > ii[None, :]).astype(np.float32)
    cmask = np.concatenate([ident, su, iu, sl_], axis=1)
    blk = np.zeros((128, 128), np.float32)
    blk[0:64, 0:64] = 1
    blk[64:128, 64:128] = 1
    cmat = np.concatenate([np.full((128, 128), 1.0 / 1024, np.float32), blk], axis=1)
    wsT = np.ascontiguousarray(np.swapaxes(f('ws_chunk'), 2, 3))
    wp = f('w_pool')
    wpbd = np.zeros((L, 2, 128, 128), np.float32)
    for l in range(L):
        for g in range(4):
            c, hh = g // 2, g % 2
            wpbd[l, c, hh * 64:(hh + 1) * 64, hh * 64:(hh + 1) * 64] = wp[l, g]
    lora = np.concatenate([f('rwkv_w2'), f('rwkv_a2'), f('rwkv_g2')], axis=1)
    esel = np.zeros((16, 16, 128), np.float32)
    for s_ in range(16):
        esel[s_, s_, :] = 1.0
    esel = esel.reshape(16, 16 * 128)
    return dict(esel=esel, vec=vec, bvec=bvec, invcnt0=invcnt0, cmask=cmask, cmat=cmat, wsT=wsT, wpbd=wpbd,
                lora_w=np.ascontiguousarray(lora))


_NC_CACHE = {}


def kernel(**inp):
    f = lambda k: np.ascontiguousarray(np.asarray(inp[k], dtype=np.float32))
    packed = _host_pack(inp)
    shared = {k: f(k) for k in ('w_in', 'w_out', 'w_xq', 'w_xk', 'w_xv', 'w_xo', 'ffn_w1', 'ffn_w3', 'ffn_w2')}
    shared.update(packed)
    if 'nc' not in _NC_CACHE:
        _NC_CACHE['nc'] = build_program()
    nc = _NC_CACHE['nc']
    in_maps = []
    for b in range(8):
        sl = slice(b * NS, (b + 1) * NS)
        m = dict(shared)
        m['x_prompt'] = f('x_prompt')[b]
        m['x_sample'] = f('x_sample')[sl, 0]
        m['mem_prompt'] = f('mem_prompt')[b]
        m['cache_mem_k'] = np.ascontiguousarray(f('cache_mem_k')[:, sl].reshape(L, NS, 256, D))
        m['cache_mem_v'] = np.ascontiguousarray(f('cache_mem_v')[:, sl].reshape(L, NS, 256, D))
        m['state_pool'] = np.ascontiguousarray(f('state_pool')[:, sl])
        m['state_conv'] = np.ascontiguousarray(f('state_conv')[:, sl])
        m['state_shift'] = np.ascontiguousarray(f('state_shift')[:, sl, 0])
        m['state_wkv'] = np.ascontiguousarray(f('state_wkv')[:, sl])
        in_maps.append(m)
    res = run_bass_kernel_spmd(nc, in_maps, core_ids=list(range(8)))
    rs = res.results
    g = lambda k: [np.asarray(r[k]) for r in rs]
    y_prompt = np.stack(g('y_prompt'), 0)
    y_sample = np.concatenate(g('y_sample'), 0)[:, None, :]
    stk1 = lambda k: np.stack(g(k), 1)
    cat1 = lambda k: np.concatenate(g(k), 1)
    outs = (
        y_prompt, y_sample,
        stk1('chunk_v_prompt'), stk1('pool_prompt'), stk1('conv_prompt'), stk1('shift_prompt')[:, :, None, :],
        stk1('wkv_prompt'),
        stk1('mem_k_prompt').reshape(L, 8, 256, 4, 256), stk1('mem_v_prompt').reshape(L, 8, 256, 4, 256),
        cat1('chunk_v_sample')[:, :, None, :], cat1('pool_sample'), cat1('conv_sample'),
        cat1('shift_sample')[:, :, None, :], cat1('wkv_sample'),
    )
    return tuple(np.ascontiguousarray(o, dtype=np.float32) for o in outs)
```

```python
import contextlib
import numpy as np
import concourse.bass as bass
import concourse.mybir as mybir
from concourse.bass_utils import run_bass_kernel_spmd

F32 = mybir.dt.float32
BF16 = mybir.dt.bfloat16
ALU = mybir.AluOpType
AF = mybir.ActivationFunctionType
AX = mybir.AxisListType

ENGS = ['pe', 'act', 'dve', 'pool', 'sp']
SAME_ENG_SYNC = True

D = 1024
SEQ = 2048
NT = 512
NTILE = SEQ // NT
L = 2
PROJ = 2432
DFF = 2816
NS = 16
NTX = NT + NS
ALPHA = float(4 ** 0.25)
LN_EPS = 1e-5
GN_EPS = 64e-5
NVEC = 88


class _Stop(Exception):
    pass


class Res:
    __slots__ = ('name', 'w', 'r')

    def __init__(self, name):
        self.name = name
        self.w = None
        self.r = {}


class Sched:
    def __init__(self, nc, stack, n_dma_sems=None):
        self.nc = nc
        self.ops = {e: [] for e in ENGS}
        self.cnt = {e: 0 for e in ENGS}
        self.waited = {e: {} for e in ENGS}
        self.sem = {}
        for e in ENGS:
            self.sem[e] = stack.enter_context(nc.semaphore('s_' + e))
        n_dma_sems = n_dma_sems or {'sp': 24, 'pool': 16}
        self.dsem = {}
        self.dsem_cnt = {}
        self.dsem_rr = {}
        for q, n in n_dma_sems.items():
            self.dsem[q] = []
            for i in range(n):
                k = 'd_%s_%d' % (q, i)
                self.sem[k] = stack.enter_context(nc.semaphore(k))
                self.dsem[q].append(k)
                self.dsem_cnt[k] = 0
            self.dsem_rr[q] = 0
        self.res = {}

    def R(self, *key):
        r = self.res.get(key)
        if r is None:
            r = Res(key)
            self.res[key] = r
        return r

    def add(self, eng, fn, reads=(), writes=(), dma=False, inc=1):
        def _flat(xs):
            out = []
            for x in xs:
                if isinstance(x, (tuple, list)):
                    out.extend(_flat(x))
                else:
                    out.append(x)
            return out
        reads = _flat(reads)
        writes = _flat(writes)
        writes = writes + [r for r in reads if r.name[0] == 'ps']
        reads = [r for r in reads if r.name[0] != 'ps']
        deps = {}

        def need(d, same_ok):
            if d is None:
                return
            k, v = d
            if k == eng and not same_ok:
                return
            if deps.get(k, 0) < v:
                deps[k] = v

        same_raw = SAME_ENG_SYNC and eng != 'pe'
        for r in reads:
            need(r.w, same_raw)
        for w in writes:
            need(w.w, same_raw)
            for k, v in w.r.items():
                need((k, v), same_raw)
        if dma:
            q = eng
            sk = self.dsem[q][self.dsem_rr[q] % len(self.dsem[q])]
            self.dsem_rr[q] += 1
            if self.dsem_cnt[sk] > 0:
                need((sk, self.dsem_cnt[sk]), True)
            self.dsem_cnt[sk] += 16
            done = (sk, self.dsem_cnt[sk])
        else:
            self.cnt[eng] += inc
            done = (eng, self.cnt[eng])
        waits = []
        wd = self.waited[eng]
        for k, v in deps.items():
            if wd.get(k, 0) >= v:
                continue
            wd[k] = v
            waits.append((k, v))
        self.ops[eng].append((waits, fn, done, dma, inc))
        for r in reads:
            if r.r.get(done[0], 0) < done[1]:
                r.r[done[0]] = done[1]
        for w in writes:
            w.w = done
            w.r = {}
        return done

    def dma(self, q, out, in_, reads=(), writes=(), **kw):
        return self.add(q, lambda e: e.dma_start(out=out, in_=in_, **kw), reads, writes, dma=True)

    def barrier(self):
        part = ['pe', 'act', 'dve', 'sp']
        snap = {e: self.cnt[e] for e in ('pe', 'act', 'dve')}
        dsn = {k: self.dsem_cnt[k] for k in self.dsem['sp'] if self.dsem_cnt[k] > 0}
        for e in part:
            waits = []
            wd = self.waited[e]
            for k, v in list(snap.items()) + list(dsn.items()):
                if k == e or v == 0:
                    continue
                if wd.get(k, 0) >= v:
                    continue
                wd[k] = v
                waits.append((k, v))
            if waits:
                self.ops[e].append((waits, None, None, False, 0))

    def emit(self):
        nc = self.nc
        hmap = {'pe': 'tensor', 'act': 'scalar', 'dve': 'vector', 'pool': 'gpsimd', 'sp': 'sync'}
        final = [(k, v) for k, v in self.dsem_cnt.items() if v > 0]
        with nc.Block() as block:
            for eng in ENGS:
                def body(e, eng=eng):
                    for waits, fn, done, dma, inc in self.ops[eng]:
                        for k, v in waits:
                            e.wait_ge(self.sem[k], v)
                        if fn is None:
                            continue
                        ins = fn(e)
                        ins.then_inc(self.sem[done[0]], 16 if dma else inc)
                    if eng == 'sp':
                        for k, v in final:
                            e.wait_ge(self.sem[k], v)
                        for en2 in ENGS:
                            if en2 != 'sp' and self.cnt[en2] > 0:
                                e.wait_ge(self.sem[en2], self.cnt[en2])
                getattr(block, hmap[eng])(body)


class Prog:
    def __init__(self):
        self.nc = bass.Bass("TRN2", target_bir_lowering=False)
        self.dram = {}

    def din(self, name, shape):
        t = self.nc.dram_tensor(name, list(shape), F32, kind="ExternalInput").ap()
        self.dram[name] = t
        return t

    def dout(self, name, shape):
        t = self.nc.dram_tensor(name, list(shape), F32, kind="ExternalOutput").ap()
        self.dram[name] = t
        return t


OUT_SHAPES = {
    'y_prompt': [SEQ, D], 'y_sample': [NS, D],
    'chunk_v_prompt': [L, 128, 256], 'pool_prompt': [L, 15, 256], 'conv_prompt': [L, 2, 256],
    'shift_prompt': [L, 896], 'wkv_prompt': [L, 4, 64, 64],
    'mem_k_prompt': [L, 256, D], 'mem_v_prompt': [L, 256, D],
    'chunk_v_sample': [L, NS, 256], 'pool_sample': [L, NS, 15, 256], 'conv_sample': [L, NS, 2, 256],
    'shift_sample': [L, NS, 896], 'wkv_sample': [L, NS, 4, 64, 64],
}
IN_SHAPES = {
    'x_prompt': [SEQ, D], 'x_sample': [NS, D], 'mem_prompt': [256, D],
    'cache_mem_k': [L, NS, 256, D], 'cache_mem_v': [L, NS, 256, D],
    'state_pool': [L, NS, 15, 256], 'state_conv': [L, NS, 2, 256], 'state_shift': [L, NS, 896],
    'state_wkv': [L, NS, 4, 64, 64],
    'w_in': [L, D, PROJ], 'w_out': [L, D, D], 'w_xq': [L, D, D], 'w_xk': [L, D, D], 'w_xv': [L, D, D],
    'w_xo': [L, D, D], 'ffn_w1': [L, D, DFF], 'ffn_w3': [L, D, DFF], 'ffn_w2': [L, DFF, D],
    'wsT': [L, 4, 128, 128], 'wpbd': [L, 2, 128, 128], 'lora_w': [L, 128, 256],
    'vec': [128, L * NVEC], 'bvec': [128, L * 768], 'cmask': [128, 4 * 128], 'cmat': [128, 2 * 128],
    'invcnt0': [128, 32], 'esel': [16, 16 * 128],
}


def build_program(do_sample=True, ntile=NTILE, nlayer=L, stop=None):
    P = Prog()
    nc = P.nc
    di = {k: P.din(k, v) for k, v in IN_SHAPES.items()}
    do = {k: P.dout(k, v) for k, v in OUT_SHAPES.items()}
    st = contextlib.ExitStack()
    with st:
        S = Sched(nc, st)
        R = S.R

        def sb(name, shape, dt=F32):
            return st.enter_context(nc.sbuf_tensor('sb_' + name, list(shape), dt))

        h32 = sb('h32', [128, 8, NTX])
        hb = sb('hb', [128, 8, NTX], BF16)
        pre32 = sb('pre32', [128, 8, NTX])
        lnt = sb('lnt', [128, 3, 2, NTX], BF16)
        pjS = sb('pjS', [128, 19, NS])
        accS = sb('accS', [128, 2, NS])
        NW = 5
        wsl = [sb('wsl%d' % i, [128, 4096], BF16) for i in range(NW)]
        cmask = sb('cmask', [128, 4, 128])
        cmat = sb('cmat', [128, 2, 128])
        onesD = sb('onesD', [128, 128], BF16)
        identb = sb('identb', [128, 128], BF16)
        vec = sb('vec', [128, L * NVEC])
        omka = sb('omka', [128, L, 2])
        bvec = sb('bvec', [128, L, 768])
        invcnt0 = sb('invcnt0', [128, 2, 16])
        wsTb = sb('wsTb', [128, L, 4, 128], BF16)
        wpbd = sb('wpbd', [128, L, 2, 128], BF16)
        lorab = sb('lorab', [128, L, 256], BF16)
        KT = sb('KT', [128, L, 8, 256], BF16)
        Vm = sb('Vm', [128, L, 2, D], BF16)
        hpool = sb('hpool', [128, L, 2, 15])
        hconv = sb('hconv', [128, L, 2, 2])
        hpd = sb('hpd', [128, L, 7])
        Gst = sb('Gst', [128, L, 2, 64])
        lnm = sb('lnm', [128, NTX])
        lnr = sb('lnr', [128, NTX])
        lnn = sb('lnn', [128, NTX])
        SCRN = 20760
        scr = sb('scr', [128, SCRN])

        ps = [st.enter_context(nc.psum_tensor('ps%d' % i, [128, 512], F32)) for i in range(8)]
        ps_rr = [0]

        def psum():
            i = ps_rr[0] % 6
            ps_rr[0] += 1
            return ps[i], R('ps', i)

        scr_off = [0]
        scr_ph = [0]
        attn_mark = [0]
        scr_max = [0]

        def phase():
            S.barrier()
            scr_off[0] = 0
            scr_ph[0] += 1

        def salloc(shape, dt=F32):
            n = int(np.prod(shape))
            words = n if dt == F32 else (n + 1) // 2
            words = (words + 1) // 2 * 2
            o = scr_off[0]
            assert o + words <= SCRN, ('scratch overflow', o, words)
            scr_off[0] += words
            scr_max[0] = max(scr_max[0], scr_off[0])
            v = scr[:, o:o + words]
            if dt != F32:
                v = v.bitcast(dt)
            v = v[:, 0:n]
            if len(shape) == 2:
                v = v.rearrange("p (a b) -> p a b", b=shape[1])
            elif len(shape) == 3:
                v = v.rearrange("p (a b c) -> p a b c", b=shape[1], c=shape[2])
            return v

        def mm(out, lhsT, rhs, start, stop, reads, writes):
            S.add('pe', lambda e: e.matmul(out, lhsT, rhs, start=start, stop=stop), reads, writes)

        def mmr(out, lhsT, rhs, start, stop, reads, writes):
            F32R = mybir.dt.float32r
            S.add('pe', lambda e: e.matmul(out, lhsT.bitcast(F32R), rhs.bitcast(F32R), start=start, stop=stop), reads, writes)

        def tr(out, in_, reads, writes):
            n = in_.shape[0]
            S.add('pe', lambda e: e.transpose(out, in_, cmask[0:n, 0, 0:n]), list(reads) + [R('const')], writes)

        def trb(out, in_, reads, writes):
            n = in_.shape[0]
            S.add('pe', lambda e: e.transpose(out, in_, identb[0:n, 0:n]), list(reads) + [R('const')], writes)

        def act(out, in_, func, reads, writes, bias=None, scale=None, accum_out=None):
            kw = {}
            if bias is not None:
                kw['bias'] = bias
            if scale is not None:
                kw['scale'] = scale
            if accum_out is not None:
                kw['accum_out'] = accum_out
            S.add('act', lambda e: e.activation(out, in_, func, **kw), reads, writes)

        def tt(out, a, b, op, reads, writes, eng='dve'):
            S.add(eng, lambda e: e.tensor_tensor(out, a, b, op), reads, writes)

        def ts(out, a, s1, s2, op0, op1, reads, writes, eng='dve'):
            if s2 is None:
                S.add(eng, lambda e: e.tensor_scalar(out, a, s1, None, op0), reads, writes)
            else:
                S.add(eng, lambda e: e.tensor_scalar(out, a, s1, s2, op0, op1), reads, writes)

        def stt(out, in0, scalar, in1, op0, op1, reads, writes, eng='dve'):
            S.add(eng, lambda e: e.scalar_tensor_tensor(out, in0, scalar, in1, op0, op1), reads, writes)

        def cp(out, in_, reads, writes, eng='dve'):
            if eng == 'act':
                S.add('act', lambda e: e.activation(out, in_, AF.Copy), reads, writes)
            else:
                S.add(eng, lambda e: e.tensor_copy(out, in_), reads, writes)

        def memset(ap, val, writes, eng='dve'):
            S.add(eng, lambda e: e.memset(ap, val), (), writes)

        w_rr = [0]

        def wload(src, kc, cols):
            i = w_rr[0] % NW
            w_rr[0] += 1
            v = wsl[i][:, 0:kc * cols].rearrange("p (k c) -> p k c", c=cols)
            S.dma('pool', v, src, writes=[R('wsl', i)])
            return v, R('wsl', i)

        def wsrc(name, l, c0, cols, kc=8):
            return di[name][l, :, c0:c0 + cols].rearrange("(k p) c -> p k c", p=128)

        RC = R('const')

        S.dma('sp', cmask[:], di['cmask'].rearrange("p (a b) -> p a b", b=128), writes=[RC])
        S.dma('sp', cmat[:], di['cmat'].rearrange("p (a b) -> p a b", b=128), writes=[RC])
        S.dma('sp', vec[:], di['vec'], writes=[RC])
        S.dma('sp', bvec[:], di['bvec'].rearrange("p (l n) -> p l n", n=768), writes=[RC])
        S.dma('sp', invcnt0[:], di['invcnt0'].rearrange("p (a b) -> p a b", b=16), writes=[RC])
        cp(onesD[:], cmat[:, 0, :], [RC], [RC], eng='act')
        cp(identb[:], cmask[:, 0, :], [RC], [RC], eng='act')
        S.dma('pool', wpbd[:], di['wpbd'].rearrange("l c p n -> p l c n"), writes=[RC])
        S.dma('pool', lorab[:], di['lora_w'].rearrange("l p n -> p l n"), writes=[RC])
        for l in range(L):
            vb = l * NVEC
            ts(omka[:, l, :], vec[:, vb + 21:vb + 23], -1.0, 1.0, ALU.mult, ALU.add, [RC], [RC])

        def V_(l, off, n=1):
            return vec[:, l * NVEC + off:l * NVEC + off + n]

        ln_i = [0]

        def ln_accum(c, n, ext=False):
            nx = n + NS if ext else n
            k = ln_i[0] % 3
            ln_i[0] += 1
            Rt = R('lnt', k)
            act(lnt[:, k, 0, 0:nx], pre32[:, c, 0:nx], AF.Copy, [R('pre', c)], [Rt])
            act(lnt[:, k, 1, 0:nx], pre32[:, c, 0:nx], AF.Square, [R('pre', c)], [Rt])
            def pe_part():
                mm(ps[6][:, 0:n], onesD[:], lnt[:, k, 0, 0:n], c == 0, c == 7, [Rt, RC], [R('ps', 6)])
                mm(ps[7][:, 0:n], onesD[:], lnt[:, k, 1, 0:n], c == 0, c == 7, [Rt, RC], [R('ps', 7)])
                if ext:
                    pS, RpS = psum()
                    mm(pS[:, 0:NS], onesD[:], lnt[:, k, 0, n:nx], True, True, [Rt, RC], [RpS])
                    mm(pS[:, NS:2 * NS], onesD[:], lnt[:, k, 1, n:nx], True, True, [Rt, RC], [RpS])
                    av_ = accS[:, :, :].rearrange("p a b -> p (a b)")
                    if c == 0:
                        cp(av_, pS[:, 0:2 * NS], [RpS], [R('accS')])
                    else:
                        tt(av_, av_, pS[:, 0:2 * NS], ALU.add, [RpS, R('accS')], [R('accS')])
            return pe_part

        def ln_finish(l, goff, boff, n, ext=False):
            nx = n + NS if ext else n
            Rl = R('lnsm')
            cp(lnm[:, 0:n], ps[6][:, 0:n], [R('ps', 6)], [Rl], eng='act')
            if ext:
                cp(lnm[:, n:nx], accS[:, 0, :], [R('accS')], [Rl])
            tt(lnr[:, 0:nx], lnm[:, 0:nx], lnm[:, 0:nx], ALU.mult, [Rl], [Rl])
            if ext:
                tt(lnr[:, n:nx], accS[:, 1, :], lnr[:, n:nx], ALU.subtract, [R('accS'), Rl], [Rl])
            tt(lnr[:, 0:n], ps[7][:, 0:n], lnr[:, 0:n], ALU.subtract, [R('ps', 7), Rl], [Rl])
            act(lnr[:, 0:nx], lnr[:, 0:nx], AF.Sqrt, [Rl], [Rl], bias=LN_EPS, scale=1.0)
            S.add('dve', lambda e: e.reciprocal(lnr[:, 0:nx], lnr[:, 0:nx]), [Rl], [Rl])
            stt(lnn[:, 0:nx], lnm[:, 0:nx], -1.0, lnr[:, 0:nx], ALU.mult, ALU.mult, [Rl], [Rl])
            for c in range(8):
                tt(pre32[:, c, 0:nx], pre32[:, c, 0:nx], lnr[:, 0:nx], ALU.mult, [R('pre', c), Rl], [R('pre', c)])
                tt(pre32[:, c, 0:nx], pre32[:, c, 0:nx], lnn[:, 0:nx], ALU.add, [R('pre', c), Rl], [R('pre', c)])
                act(hb[:, c, 0:nx], pre32[:, c, 0:nx], AF.Identity, [R('pre', c), RC], [R('hb', c)],
                    bias=V_(l, boff + c), scale=V_(l, goff + c))
                act(h32[:, c, 0:nx], pre32[:, c, 0:nx], AF.Identity, [R('pre', c), RC], [R('h32', c)],
                    bias=V_(l, boff + c), scale=V_(l, goff + c))

        def dense_res_ln(l, wname, src, src_res, nk, n, goff, boff, ext=False):
            nx = n + NS if ext else n
            pend = []
            if nk == 8:
                groups = [(g * 512, 512) for g in range(2)]
            else:
                groups = [(g * 128, 128) for g in range(8)]
            for (c0, cols) in groups:
                wv, Rw = wload(di[wname][l, :, c0:c0 + cols].rearrange("(k p) c -> p k c", p=128), nk, cols)
                for j in range(cols // 128):
                    c = (c0 + j * 128) // 128
                    pb, Rp = psum()
                    for k in range(nk):
                        mm(pb[:, 0:n], wv[:, k, j * 128:(j + 1) * 128], src[:, k, 0:n], k == 0, k == nk - 1,
                           [Rw, src_res(k)], [Rp])
                    if ext:
                        pS, RpS = psum()
                        for k in range(nk):
                            mm(pS[:, 0:NS], wv[:, k, j * 128:(j + 1) * 128], src[:, k, n:nx], k == 0, k == nk - 1,
                               [Rw, src_res(k)], [RpS])
                    stt(pre32[:, c, 0:n], h32[:, c, 0:n], ALPHA, pb[:, 0:n], ALU.mult, ALU.add,
                        [R('h32', c), Rp], [R('pre', c)])
                    if ext:
                        stt(pre32[:, c, n:nx], h32[:, c, n:nx], ALPHA, pS[:, 0:NS], ALU.mult, ALU.add,
                            [R('h32', c), RpS], [R('pre', c)])
                    pend.append(ln_accum(c, n, ext))
                    if len(pend) > 2:
                        pend.pop(0)()
            for f_ in pend:
                f_()
            ln_finish(l, goff, boff, n, ext)

        def load_tokens(src_rows, n, col0=0):
            nsub = (n + 127) // 128
            xt = salloc([4, D])
            for s_ in range(nsub):
                r = min(128, n - s_ * 128)
                S.dma('sp', xt[0:r, s_, :], src_rows[s_ * 128:s_ * 128 + r, :], writes=[R('xt', s_)])
            for c in range(8):
                pb, Rp = psum()
                for s_ in range(nsub):
                    r = min(128, n - s_ * 128)
                    tr(pb[:, s_ * 128:s_ * 128 + r], xt[0:r, s_, c * 128:(c + 1) * 128], [R('xt', s_)], [Rp])
                cp(h32[:, c, col0:col0 + n], pb[:, 0:n], [Rp], [R('h32', c)], eng='act')
                cp(hb[:, c, col0:col0 + n], pb[:, 0:n], [Rp], [R('hb', c)], eng='act')

        def store_tokens(dst_rows, n, col0=0):
            nsub = (n + 127) // 128
            yt = salloc([4, D])
            for s_ in range(nsub):
                r = min(128, n - s_ * 128)
                for g in range(2):
                    pb, Rp = psum()
                    for j in range(4):
                        c = g * 4 + j
                        tr(pb[0:r, j * 128:(j + 1) * 128], h32[:, c, col0 + s_ * 128:col0 + s_ * 128 + r], [R('h32', c)], [Rp])
                    cp(yt[0:r, s_, g * 512:(g + 1) * 512], pb[0:r, :], [Rp], [R('yt', s_)], eng=('act' if g else 'dve'))
                S.dma('sp', dst_rows[s_ * 128:s_ * 128 + r, :], yt[0:r, s_, :], reads=[R('yt', s_)])

        def setup_layers():
            memT = salloc([8, 256], BF16)
            mt = salloc([2, D])
            for s_ in range(2):
                S.dma('sp', mt[:, s_, :], di['mem_prompt'][s_ * 128:(s_ + 1) * 128, :], writes=[R('mt', s_)])
            for c in range(8):
                pb, Rp = psum()
                for s_ in range(2):
                    tr(pb[:, s_ * 128:(s_ + 1) * 128], mt[:, s_, c * 128:(c + 1) * 128], [R('mt', s_)], [Rp])
                cp(memT[:, c, :], pb[:, 0:256], [Rp], [R('memT')], eng=('act' if c % 2 else 'dve'))
            wst = salloc([4, 128])
            kvo = salloc([2, 512])
            if stop == 'su1':
                return
            for l in range(L):
                S.dma('sp', wst[:], di['wsT'][l].rearrange("h s t -> s h t"), writes=[R('wst')])
                tt(wsTb[:, l, :, :], wst[:], cmask[:, 2:3, :].broadcast_to([128, 4, 128]), ALU.mult,
                   [R('wst'), RC], [RC])
                if stop == 'su2':
                    continue
                for wi, wname in enumerate(['w_xk', 'w_xv']):
                    oname = 'mem_k_prompt' if wi == 0 else 'mem_v_prompt'
                    for g in range(2):
                        wv, Rw = wload(wsrc(wname, l, g * 512, 512), 8, 512)
                        if stop == 'su3':
                            continue
                        if wi == 0:
                            for j in range(4):
                                pb, Rp = psum()
                                for k in range(8):
                                    mm(pb[:, 0:256], wv[:, k, j * 128:(j + 1) * 128], memT[:, k, :], k == 0, k == 7,
                                       [Rw, R('memT')], [Rp])
                                cp(KT[:, l, g * 4 + j, :], pb[:, 0:256], [Rp], [R('KT', l)], eng='act')
                        if stop == 'su4':
                            continue
                        for s_ in range(2):
                            pb, Rp = psum()
                            for k in range(8):
                                mm(pb[:, :], memT[:, k, s_ * 128:(s_ + 1) * 128], wv[:, k, :], k == 0, k == 7,
                                   [Rw, R('memT')], [Rp])
                            if stop == 'su7':
                                continue
                            if stop != 'su9':
                                cp(kvo[:, s_, :], pb[:, :], [Rp], [R('kvo', s_)], eng='act')
                            if stop == 'su8':
                                continue
                            if wi == 1:
                                cp(Vm[:, l, s_, g * 512:(g + 1) * 512], pb[:, :], [Rp], [R('Vm', l)],
                                   eng=('act' if stop == 'su10' else 'dve'))
                            if stop in ('su9', 'su10'):
                                continue
                            if stop != 'su6':
                                S.dma('sp', do[oname][l, s_ * 128:(s_ + 1) * 128, g * 512:(g + 1) * 512], kvo[:, s_, :],
                                      reads=[R('kvo', s_)])
            for t_, nm in ((hpool, 'hpool'), (hconv, 'hconv'), (hpd, 'hpd')):
                memset(t_[:], 0.0, [R(nm)])
            memset(Gst[:], 0.0, [R('Gh', h_) for h_ in range(4)])

        def mixer_prompt(l, ti, ext=False):
            n = NT
            last = (ti == ntile - 1)
            mixb = salloc([8, NTX], BF16)

            def sproj(wv, Rw, col_lo, chunk):
                if not ext:
                    return
                pS, RpS = psum()
                for k in range(8):
                    mm(pS[:, 0:NS], wv[:, k, col_lo:col_lo + 128], hb[:, k, NT:NTX], k == 0, k == 7, [Rw, R('hb', k)], [RpS])
                cp(pjS[:, chunk, :], pS[:, 0:NS], [RpS], [R('pjS')], eng='act')

            Rmix = lambda k: R('mixb', k)
            G = [salloc([2, 528]) for _ in range(8)]
            RG = [(R('G', i, 0), R('G', i, 1)) for i in range(8)]
            X2 = [salloc([2, NT]) for _ in range(2)]
            RX2 = [(R('X2', i, 0), R('X2', i, 1)) for i in range(2)]
            pre = dict(KR=salloc([2, 4, 256], BF16), kt=salloc([2, NT], BF16), bt=salloc([2, NT], BF16), gC=salloc([2, 4]))
            pre['mark'] = scr_off[0]
            wv, Rw = wload(wsrc('w_in', l, 0, 512), 8, 512)
            ug = G[0]
            for c in range(2):
                pb, Rp = psum()
                for k in range(8):
                    mm(pb[:, :], wv[:, k, c * 128:(c + 1) * 128], hb[:, k, 0:NT], k == 0, k == 7, [Rw, R('hb', k)], [Rp])
                act(ug[:, c, 0:NT], pb[:, :], AF.Gelu_apprx_tanh, [Rp], [RG[0]])
            for cs_ in range(4):
                sproj(wv, Rw, cs_ * 128, cs_)
            va = salloc([4, 256])
            vs6 = salloc([4, 6])
            vmv = salloc([4, 2])
            vlnb = salloc([4, 256], BF16)
            bsb = bvec[:, l, 512:768].rearrange("p (c t) -> p c t", t=128)
            for s_ in range(4):
                q = s_
                Rv = R('va', q)
                pb, Rp = psum()
                for k in range(8):
                    mm(pb[:, 0:256], hb[:, k, s_ * 128:(s_ + 1) * 128], wv[:, k, 256:512], k == 0, k == 7,
                       [Rw, R('hb', k)], [Rp])
                act(va[:, q, :], pb[:, 0:256], AF.Gelu_apprx_tanh, [Rp], [Rv])
            for q in range(4):
                Rv = R('va', q)
                S.add('dve', lambda e, q=q: e.bn_stats(vs6[:, q, :], va[:, q, :]), [Rv], [Rv])
                S.add('dve', lambda e, q=q: e.bn_aggr(vmv[:, q, :], vs6[:, q, :]), [Rv], [Rv])
            for q in range(4):
                Rv = R('va', q)
                act(vmv[:, q, 1:2], vmv[:, q, 1:2], AF.Sqrt, [Rv], [Rv], bias=LN_EPS, scale=1.0)
            for q in range(4):
                Rv = R('va', q)
                S.add('dve', lambda e, q=q: e.reciprocal(vmv[:, q, 1:2], vmv[:, q, 1:2]), [Rv], [Rv])
                ts(va[:, q, :], va[:, q, :], vmv[:, q, 0:1], vmv[:, q, 1:2], ALU.subtract, ALU.mult, [Rv], [Rv])
                tt(va[:, q, :], va[:, q, :], bvec[:, l, 0:256], ALU.mult, [Rv, RC], [Rv])
                tt(va[:, q, :], va[:, q, :], bvec[:, l, 256:512], ALU.add, [Rv, RC], [Rv])
                if last and q == 3:
                    S.dma('sp', do['chunk_v_prompt'][l], va[:, q, :], reads=[Rv])
                cp(vlnb[:, q, :], va[:, q, :], [Rv], [R('vlnb', q)], eng='act')
            zt = salloc([2, NT])
            for s_ in range(4):
                q = s_
                zps = []
                for c in range(2):
                    pb2, Rp2 = psum()
                    for hh in range(2):
                        h_ = 2 * c + hh
                        mm(pb2[hh * 64:(hh + 1) * 64, 0:128], vlnb[:, q, h_ * 64:(h_ + 1) * 64], wsTb[:, l, h_, :],
                           True, True, [R('vlnb', q), RC], [Rp2])
                    zps.append((pb2, Rp2))
                for c in range(2):
                    pb2, Rp2 = zps[c]
                    tt(zt[:, c, s_ * 128:(s_ + 1) * 128], pb2[:, 0:128], bsb[:, c, :], ALU.add, [Rp2, RC], [R('zt', s_, c)])
                    tt(mixb[:, c, s_ * 128:(s_ + 1) * 128], zt[:, c, s_ * 128:(s_ + 1) * 128],
                       ug[:, c, s_ * 128:(s_ + 1) * 128], ALU.mult, [R('zt', s_, c), RG[0]], [Rmix(c)])
            if stop == 'A':
                raise _Stop()
            wv, Rw = wload(wsrc('w_in', l, 512, 512), 8, 512)
            xp, s2, s4, s8 = G[1], G[2], G[3], G[4]
            for c in range(2):
                pb, Rp = psum()
                for k in range(8):
                    mm(pb[:, :], wv[:, k, c * 128:(c + 1) * 128], hb[:, k, 0:NT], k == 0, k == 7, [Rw, R('hb', k)], [Rp])
                cp(xp[:, c, 15:15 + NT], pb[:, :], [Rp], [RG[1]], eng='act')
            for cs_ in range(4):
                sproj(wv, Rw, cs_ * 128, 4 + cs_)
            cp(xp[:, :, 0:15], hpool[:, l, :, :], [R('hpool')], [RG[1]])
            cp(hpool[:, l, :, :], xp[:, :, NT:NT + 15], [RG[1]], [R('hpool')])
            W_ = 15 + NT
            tt(s2[:, :, 1:W_], xp[:, :, 1:W_], xp[:, :, 0:W_ - 1], ALU.add, [RG[1]], [RG[2]])
            tt(s4[:, :, 3:W_], s2[:, :, 3:W_], s2[:, :, 1:W_ - 2], ALU.add, [RG[2]], [RG[3]])
            tt(s8[:, 0, 7:W_], s4[:, 1, 7:W_], s4[:, 1, 3:W_ - 4], ALU.add, [RG[3]], [RG[4]])
            tt(s8[:, 1, 15:W_], s8[:, 0, 15:W_], s8[:, 0, 7:W_ - 8], ALU.add, [RG[4]], [RG[4]])
            mean = G[5]
            srcs = [(0, 0, s2[0:64, 0, 15:W_]), (0, 64, s4[64:128, 0, 15:W_]),
                    (1, 0, s8[0:64, 0, 15:W_]), (1, 64, s8[64:128, 1, 15:W_])]
            for (c, p0, sv) in srcs:
                ts(mean[p0:p0 + 64, c, 0:NT], sv, V_(l, 77 + c)[p0:p0 + 64, :], None, ALU.mult, None,
                   [RG[2], RG[3], RG[4], RC], [RG[5]])
                if ti == 0:
                    tt(mean[p0:p0 + 64, c, 0:16], sv[:, 0:16], invcnt0[p0:p0 + 64, c, :], ALU.mult,
                       [RG[2], RG[3], RG[4], RC], [RG[5]])
            dpl = salloc([2, NT], BF16)
            tt(dpl[:, :, :], mean[:, :, 0:NT], xp[:, :, 15:W_], ALU.subtract, [RG[5], RG[1]], [R('dpl')])
            if last:
                pb, Rp = psum()
                for c in range(2):
                    tr(pb[0:15, c * 128:(c + 1) * 128], xp[:, c, NT:NT + 15], [RG[1]], [Rp])
                ot = salloc([1, 256])
                cp(ot[0:15, 0, :], pb[0:15, 0:256], [Rp], [R('ot_pool')])
                S.dma('sp', do['pool_prompt'][l], ot[0:15, 0, :], reads=[R('ot_pool')])
            bgs = G[6]
            for c in range(2):
                pb, Rp = psum()
                for k in range(8):
                    mm(pb[:, :], wv[:, k, 256 + c * 128:256 + (c + 1) * 128], hb[:, k, 0:NT], k == 0, k == 7,
                       [Rw, R('hb', k)], [Rp])
                cp(bgs[:, c, 0:NT], pb[:, :], [Rp], [RG[6]], eng='act')
            if stop == 'B':
                raise _Stop()
            wv, Rw = wload(wsrc('w_in', l, 1024, 512), 8, 512)
            cgs, zc, yc = G[7], G[2], G[3]
            for c in range(2):
                pb, Rp = psum()
                for k in range(8):
                    mm(pb[:, :], wv[:, k, c * 128:(c + 1) * 128], hb[:, k, 0:NT], k == 0, k == 7, [Rw, R('hb', k)], [Rp])
                cp(cgs[:, c, 0:NT], pb[:, :], [Rp], [RG[7]], eng='act')
            for c in range(2):
                pb, Rp = psum()
                for k in range(8):
                    mm(pb[:, :], wv[:, k, 256 + c * 128:256 + (c + 1) * 128], hb[:, k, 0:NT], k == 0, k == 7,
                       [Rw, R('hb', k)], [Rp])
                tt(zc[:, c, 2:2 + NT], pb[:, :], cgs[:, c, 0:NT], ALU.mult, [Rp, RG[7]], [RG[2]])
            for cs_ in range(4):
                sproj(wv, Rw, cs_ * 128, 8 + cs_)
            for c in range(2):
                pb, Rp = psum()
                mm(pb[:, :], wpbd[:, l, c, :], dpl[:, c, :], True, True, [RC, R('dpl')], [Rp])
                act(mixb[:, 2 + c, 0:NT], pb[:, :], AF.Identity, [Rp, RC], [Rmix(2 + c)], scale=V_(l, 13 + c))
            cp(zc[:, :, 0:2], hconv[:, l, :, :], [R('hconv')], [RG[2]])
            cp(hconv[:, l, :, :], zc[:, :, NT:NT + 2], [RG[2]], [R('hconv')])
            for c in range(2):
                ts(yc[:, c, 0:NT], zc[:, c, 0:NT], V_(l, 7 + 0 * 2 + c), None, ALU.mult, None, [RG[2], RC], [RG[3]])
                stt(yc[:, c, 0:NT], zc[:, c, 1:1 + NT], V_(l, 7 + 1 * 2 + c), yc[:, c, 0:NT], ALU.mult, ALU.add,
                    [RG[2], RG[3], RC], [RG[3]])
                stt(yc[:, c, 0:NT], zc[:, c, 2:2 + NT], V_(l, 7 + 2 * 2 + c), yc[:, c, 0:NT], ALU.mult, ALU.add,
                    [RG[2], RG[3], RC], [RG[3]])
                tt(mixb[:, 4 + c, 0:NT], yc[:, c, 0:NT], bgs[:, c, 0:NT], ALU.mult, [RG[3], RG[6]], [Rmix(4 + c)])
            if last:
                pb, Rp = psum()
                for c in range(2):
                    tr(pb[0:2, c * 128:(c + 1) * 128], zc[:, c, NT:NT + 2], [RG[2]], [Rp])
                ot2 = salloc([1, 256])
                cp(ot2[0:2, 0, :], pb[0:2, 0:256], [Rp], [R('ot_conv')])
                S.dma('sp', do['conv_prompt'][l], ot2[0:2, 0, :], reads=[R('ot_conv')])
            if stop == 'C':
                raise _Stop()
            xr, xk, xv = G[1], G[4], G[5]
            xl = salloc([1, NT])
            pdc = salloc([2, NT + 1])
            dif = salloc([1, NT])
            if last:
                shf = salloc([1, 8])
            dsts = [(xr, 0, RG[1]), (xr, 1, RG[1]), (xk, 0, RG[4]), (xk, 1, RG[4]), (xv, 0, RG[5]), (xv, 1, RG[5]),
                    (xl, 0, R('xl'))]
            for gi, (c0, ncol) in enumerate([(1536, 512), (2048, 384)]):
                wv, Rw = wload(wsrc('w_in', l, c0, ncol), 8, ncol)
                for j in range(ncol // 128):
                    cc = gi * 4 + j
                    q = cc % 2
                    Rq = R('pdc', q)
                    pb, Rp = psum()
                    for k in range(8):
                        mm(pb[:, :], wv[:, k, j * 128:(j + 1) * 128], hb[:, k, 0:NT], k == 0, k == 7, [Rw, R('hb', k)], [Rp])
                    cp(pdc[:, q, 1:NT + 1], pb[:, :], [Rp], [Rq], eng='act')
                    sproj(wv, Rw, j * 128, 12 + cc)
                    cp(pdc[:, q, 0:1], hpd[:, l, cc:cc + 1], [R('hpd')], [Rq])
                    cp(hpd[:, l, cc:cc + 1], pdc[:, q, NT:NT + 1], [Rq], [R('hpd')])
                    if last:
                        cp(shf[:, 0, cc:cc + 1], pdc[:, q, NT:NT + 1], [Rq], [R('shf')])
                    tt(dif[:, 0, :], pdc[:, q, 0:NT], pdc[:, q, 1:NT + 1], ALU.subtract, [Rq], [R('dif')])
                    dt_, dc_, dR = dsts[cc]
                    stt(dt_[:, dc_, 0:NT], dif[:, 0, :], V_(l, cc), pdc[:, q, 1:NT + 1], ALU.mult, ALU.add,
                        [R('dif'), Rq, RC], [dR])
            if last:
                pb, Rp = psum()
                tr(pb[0:7, 0:128], shf[:, 0, 0:7], [R('shf')], [Rp])
                ot3 = salloc([1, 128])
                cp(ot3[0:7, 0, :], pb[0:7, 0:128], [Rp], [R('ot_sh')])
                S.dma('sp', do['shift_prompt'][l].rearrange("(c p) -> c p", p=128), ot3[0:7, 0, :], reads=[R('ot_sh')])
            if stop == 'Dproj':
                raise _Stop()
            wkv_prep_and_scan(l, n, xr, xk, xv, xl, RG[1], RG[4], RG[5], R('xl'), G, RG, X2, RX2, mixb, Rmix, last, pre)
            return mixb, Rmix

        def wkv_prep_and_scan(l, n, xr, xk, xv, xl, Rr, Rk, Rv, Rl, G, RG, X2, RX2, mixb, Rmix, last, pre=None):
            lact = salloc([1, n], BF16)
            act(lact[0:32, 0, 0:n], xl[0:32, 0, 0:n], AF.Tanh, [Rl], [R('lact')])
            cp(lact[32:64, 0, 0:n], xl[32:64, 0, 0:n], [Rl], [R('lact')])
            act(lact[64:128, 0, 0:n], xl[64:128, 0, 0:n], AF.Sigmoid, [Rl], [R('lact')])
            lw, av, gv = G[0], G[2], G[3]
            for c in range(2):
                pb, Rp = psum()
                mm(pb[:, 0:n], lorab[0:32, l, c * 128:(c + 1) * 128], lact[0:32, 0, 0:n], True, True, [RC, R('lact')], [Rp])
                act(lw[:, c, 0:n], pb[:, 0:n], AF.Sigmoid, [Rp, RC], [RG[0]], bias=V_(l, 15 + c), scale=1.0)
                pb, Rp = psum()
                mm(pb[:, 0:n], lorab[32:64, l, c * 128:(c + 1) * 128], lact[32:64, 0, 0:n], True, True, [RC, R('lact')], [Rp])
                act(av[:, c, 0:n], pb[:, 0:n], AF.Sigmoid, [Rp, RC], [RG[2]], bias=V_(l, 17 + c), scale=1.0)
                pb, Rp = psum()
                mm(pb[:, 0:n], lorab[64:128, l, c * 128:(c + 1) * 128], lact[64:128, 0, 0:n], True, True, [RC, R('lact')], [Rp])
                cp(gv[:, c, 0:n], pb[:, 0:n], [Rp], [RG[3]], eng='act')
            ts(lw[:, :, 0:n], lw[:, :, 0:n], -0.6065306597126334, None, ALU.mult, None, [RG[0]], [RG[0]])
            kk, tmp = G[6], G[7]
            C2 = range(2)
            pc_ = lambda Rx, c: (Rx[c] if isinstance(Rx, tuple) else Rx)
            EN = lambda c: 'dve'
            for c in C2:
                ts(kk[:, c, 0:n], xk[:, c, 0:n], V_(l, 19 + c), None, ALU.mult, None, [pc_(Rk, c), RC], [RG[6][c]], eng=EN(c))
            for c in C2:
                tt(tmp[:, c, 0:n], kk[:, c, 0:n], kk[:, c, 0:n], ALU.mult, [RG[6][c]], [RG[7][c]], eng=EN(c))
            pbs = []
            for c in C2:
                pb, Rp = psum()
                mm(pb[:, 0:n], cmat[:, 1, :], tmp[:, c, 0:n], True, True, [RC, RG[7][c]], [Rp])
                pbs.append((pb, Rp))
            for c in C2:
                pb, Rp = pbs[c]
                ts(tmp[:, c, 0:n], pb[:, 0:n], 1e-12, None, ALU.max, None, [Rp], [RG[7][c]])
            for c in C2:
                act(tmp[:, c, 0:n], tmp[:, c, 0:n], AF.Sqrt, [RG[7][c]], [RG[7][c]])
            for c in C2:
                S.add('dve', lambda e, c=c: e.reciprocal(tmp[:, c, 0:n], tmp[:, c, 0:n]), [RG[7][c]], [RG[7][c]])
            for c in C2:
                tt(kk[:, c, 0:n], kk[:, c, 0:n], tmp[:, c, 0:n], ALU.mult, [RG[6][c], RG[7][c]], [RG[6][c]], eng=EN(c))
            for c in C2:
                ts(tmp[:, c, 0:n], av[:, c, 0:n], V_(l, 21 + c), omka[:, l, c:c + 1], ALU.mult, ALU.add,
                   [RG[2][c], RC], [RG[7][c]], eng=EN(c))
            for c in C2:
                tt(xk[:, c, 0:n], xk[:, c, 0:n], tmp[:, c, 0:n], ALU.mult, [pc_(Rk, c), RG[7][c]], [pc_(Rk, c)], eng=EN(c))
            for c in C2:
                tt(av[:, c, 0:n], av[:, c, 0:n], kk[:, c, 0:n], ALU.mult, [RG[2][c], RG[6][c]], [RG[2][c]], eng=EN(c))
            for c in C2:
                stt(tmp[:, c, 0:n], xr[:, c, 0:n], V_(l, 23 + c), xk[:, c, 0:n], ALU.mult, ALU.mult,
                    [pc_(Rr, c), pc_(Rk, c), RC], [RG[7][c]])
            pbs = []
            for c in C2:
                pb, Rp = psum()
                mm(pb[:, 0:n], cmat[:, 1, :], tmp[:, c, 0:n], True, True, [RC, RG[7][c]], [Rp])
                pbs.append((pb, Rp))
            for c in C2:
                pb, Rp = pbs[c]
                tt(tmp[:, c, 0:n], pb[:, 0:n], xv[:, c, 0:n], ALU.mult, [Rp, pc_(Rv, c)], [RG[7][c]])
            bv, bonus = av, tmp
            Rb, Rbonus = RG[2], RG[7]
            if X2 is None:
                return dict(lw=lw, kk=kk, b=bv, bonus=bonus, g=gv, Rlw=RG[0], Rkk=RG[6], Rb=Rb, Rbonus=Rbonus, Rg=RG[3])
            Lw = X2[0]
            ones = X2[1]
            memset(ones[:, 0, 0:128], 1.0, [RX2[1]])
            for c in range(2):
                for j in range(4):
                    sl = slice(j * 128, (j + 1) * 128)
                    S.add('dve', lambda e, c=c, sl=sl: e.tensor_tensor_scan(Lw[:, c, sl], ones[:, 0, 0:128], lw[:, c, sl],
                                                                            0.0, ALU.mult, ALU.add),
                          [RG[0][c], RX2[1]], [RX2[0][c]])
            KR, kt, bt, gC = pre['KR'], pre['kt'], pre['bt'], pre['gC']
            ex = X2[1]
            J = lambda ap: ap.rearrange("p (j t) -> p j t", t=128)
            RKR = (R('KR', 0), R('KR', 1))
            Rkt = (R('kt', 0), R('kt', 1))
            Rbt = (R('bt', 0), R('bt', 1))
            RgC = (R('gC', 0), R('gC', 1))
            for c in C2:
                act(ex[:, c, 0:NT], Lw[:, c, 0:NT], AF.Exp, [RX2[0][c]], [RX2[1][c]])
            for c in C2:
                tt(KR[:, c, :, 128:256], J(xr[:, c, 0:NT]), J(ex[:, c, 0:NT]), ALU.mult, [Rr[c], RX2[1][c]], [RKR[c]], eng=EN(c))
                cp(gC[:, c, :], J(ex[:, c, 0:NT])[:, :, 127], [RX2[1][c]], [RgC[c]], eng=EN(c))
            for c in C2:
                tt(ex[:, c, 0:NT], Lw[:, c, 0:NT], lw[:, c, 0:NT], ALU.subtract, [RX2[0][c], RG[0][c]], [RX2[1][c]], eng=EN(c))
            for c in C2:
                act(ex[:, c, 0:NT], ex[:, c, 0:NT], AF.Exp, [RX2[1][c]], [RX2[1][c]])
            for c in C2:
                tt(KR[:, c, :, 0:128], J(kk[:, c, 0:NT]), J(ex[:, c, 0:NT]), ALU.mult, [RG[6][c], RX2[1][c]], [RKR[c]], eng=EN(c))
            for c in C2:
                act(ex[:, c, 0:NT], Lw[:, c, 0:NT], AF.Exp, [RX2[0][c], RX2[1][c]], [RX2[1][c]], scale=-1.0)
            for c in C2:
                tt(kt[:, c, :], xk[:, c, 0:NT], ex[:, c, 0:NT], ALU.mult, [Rk[c], RX2[1][c]], [Rkt[c]], eng=EN(c))
                tt(bt[:, c, :], bv[:, c, 0:NT], ex[:, c, 0:NT], ALU.mult, [Rb[c], RX2[1][c]], [Rbt[c]], eng=EN(c))
            if stop == 'Dprep':
                raise _Stop()
            S.barrier()
            scr_off[0] = pre['mark']
            tm = salloc([4, 256], BF16)
            NMh = [salloc([6, 128]) for _ in range(4)]
            A3h = [salloc([3, 128], BF16) for _ in range(4)]
            Ttb = salloc([4, 128], BF16)
            WT = salloc([2, 128], BF16)
            AkV = salloc([4, 64], BF16)
            Un = salloc([4, 64], BF16)
            Gb = salloc([2, 64], BF16)
            Ob = salloc([2, 256])
            gst = salloc([1, 8])
            od = G[0]
            Rtm = R('tm')
            cp(Gb[:, :, :], Gst[:, l, :, :], [R('Gh', h_) for h_ in range(4)], [R('Gbh', h_) for h_ in range(4)])
            for j in range(4):
                sl = slice(j * 128, (j + 1) * 128)
                srcs = [xv, None, kt, bt]
                for qi in range(4):
                    for c in range(2):
                        pb, Rp = psum()
                        if qi == 0:
                            tr(pb[:, 0:128], xv[:, c, sl], [Rv], [Rp])
                            cp(tm[:, qi, c * 128:(c + 1) * 128], pb[:, 0:128], [Rp], [Rtm], eng=('act' if c else 'dve'))
                        else:
                            if qi == 1:
                                src = KR[:, c, j, 0:128]
                                rs = [RKR]
                            else:
                                src = srcs[qi][:, c, sl]
                                rs = [Rkt, Rbt]
                            pbv = pb[:, 0:64].bitcast(BF16)
                            trb(pbv, src, rs, [Rp])
                            cp(tm[:, qi, c * 128:(c + 1) * 128], pbv, [Rp], [Rtm], eng=('act' if c else 'dve'))
                pO, RpO = ps[7], R('ps', 7)
                hd = []
                for h_ in range(4):
                    c, hh = h_ // 2, h_ % 2
                    P0 = hh * 64
                    Ps = slice(P0, P0 + 64)
                    hd.append(dict(c=c, hh=hh, P0=P0, Ps=Ps, kr=KR[Ps, c, j, :], ktj=kt[Ps, c, sl], btj=bt[Ps, c, sl],
                                   Vt=tm[:, 0, h_ * 64:(h_ + 1) * 64], kat=tm[:, 1, h_ * 64:(h_ + 1) * 64],
                                   ktt=tm[:, 2, h_ * 64:(h_ + 1) * 64], btt=tm[:, 3, h_ * 64:(h_ + 1) * 64],
                                   NM=NMh[h_], A3=A3h[h_], RNM=R('NM', h_), RA3=R('A3', h_), ni=0, mi=2, ri=4))
                for h_, d_ in enumerate(hd):
                    NM, A3, RNM, RA3 = d_['NM'], d_['A3'], d_['RNM'], d_['RA3']
                    p1, Rp1 = psum()
                    mm(p1[:, 0:256], d_['btj'], d_['kr'], True, True, [Rbt, RKR], [Rp1])
                    mm(p1[:, 256:384], d_['kr'][:, 0:128], d_['btj'], True, True, [RKR, Rbt], [Rp1])
                    p2, Rp2 = psum()
                    mm(p2[:, 0:256], d_['ktj'], d_['kr'], True, True, [Rkt, RKR], [Rp2])
                    tt(NM[:, 0, :], p1[:, 0:128], cmask[:, 1, :], ALU.mult, [Rp1, RC], [RNM])
                    tt(NM[:, 2, :], p1[:, 256:384], cmask[:, 3, :], ALU.mult, [Rp1, RC], [RNM])
                    tt(A3[:, 2, :], p1[:, 128:256], cmask[:, 2, :], ALU.mult, [Rp1, RC], [RA3])
                    tt(NM[:, 4, :], cmask[:, 0, :], NM[:, 0, :], ALU.subtract, [RNM, RC], [RNM])
                    tt(A3[:, 0:2, :], p2[:, 0:256].rearrange("p (a b) -> p a b", b=128), cmask[:, 1:3, :], ALU.mult,
                       [Rp2, RC], [RA3])
                for lev in range(6):
                    lastlev = (lev == 5)
                    pms = []
                    for d_ in hd:
                        NM, RNM = d_['NM'], d_['RNM']
                        ni, mi = d_['ni'], d_['mi']
                        pm, Rpm = psum()
                        mm(pm[:, 0:128], NM[:, ni, :], NM[:, mi, :], True, True, [RNM], [Rpm])
                        if not lastlev:
                            mm(pm[:, 128:256], NM[:, mi, :], NM[:, ni, :], True, True, [RNM], [Rpm])
                        pms.append((pm, Rpm))
                    for d_, (pm, Rpm) in zip(hd, pms):
                        NM, RNM = d_['NM'], d_['RNM']
                        no, mo = 1 - d_['ni'], 5 - d_['mi']
                        cp(NM[:, mo, :], pm[:, 0:128], [Rpm], [RNM], eng='act')
                        if not lastlev:
                            cp(NM[:, no, :], pm[:, 128:256], [Rpm], [RNM], eng='act')
                        d_['ni'], d_['mi'] = no, mo
                    prs = []
                    for d_ in hd:
                        NM, RNM = d_['NM'], d_['RNM']
                        pr, Rpr = psum()
                        mm(pr[:, 0:128], NM[:, d_['mi'], :], NM[:, d_['ri'], :], True, True, [RNM], [Rpr])
                        prs.append((pr, Rpr))
                    for h_, (d_, (pr, Rpr)) in enumerate(zip(hd, prs)):
                        NM, RNM = d_['NM'], d_['RNM']
                        ro = 9 - d_['ri']
                        if lastlev:
                            tt(Ttb[:, h_, :], pr[:, 0:128], NM[:, d_['ri'], :], ALU.add, [Rpr, RNM], [R('Ttb', h_)])
                        else:
                            tt(NM[:, ro, :], pr[:, 0:128], NM[:, d_['ri'], :], ALU.add, [Rpr, RNM], [RNM])
                        d_['ri'] = ro
                pws = []
                for h_, d_ in enumerate(hd):
                    Tt = Ttb[:, h_, :]
                    P0, Ps, hh = d_['P0'], d_['Ps'], d_['hh']
                    pw, Rpw = psum()
                    mm(pw[P0:P0 + 64, 0:128], d_['kat'], Tt, True, True, [Rtm, R('Ttb', h_)], [Rpw])
                    mm(pw[:, 128:192], d_['A3'][:, 0, :], d_['Vt'], True, True, [d_['RA3'], Rtm], [Rpw])
                    pws.append((pw, Rpw))
                for h_, (d_, (pw, Rpw)) in enumerate(zip(hd, pws)):
                    Ps, hh, c = d_['Ps'], d_['hh'], d_['c']
                    cp(WT[Ps, c, :], pw[Ps, 0:128], [Rpw], [R('WT', h_)], eng='act')
                    cp(AkV[:, h_, :], pw[:, 128:192], [Rpw], [R('AkV', h_)], eng='act')
                pus = []
                for h_, d_ in enumerate(hd):
                    Ps, c = d_['Ps'], d_['c']
                    pu, Rpu = psum()
                    mm(pu[:, 0:64], Ttb[:, h_, :], AkV[:, h_, :], True, False, [R('Ttb', h_), R('AkV', h_)], [Rpu])
                    mm(pu[:, 0:64], WT[Ps, c, :], Gb[Ps, c, :], False, True, [R('WT', h_), R('Gbh', h_)], [Rpu])
                    pus.append((pu, Rpu))
                for h_, (d_, (pu, Rpu)) in enumerate(zip(hd, pus)):
                    ts(Un[:, h_, :], pu[:, 0:64], -1.0, None, ALU.mult, None, [Rpu], [R('Un', h_)])
                pgs = []
                for h_, d_ in enumerate(hd):
                    Ps, c, P0 = d_['Ps'], d_['c'], d_['P0']
                    oo = pO[:, h_ * 64:(h_ + 1) * 64]
                    mm(oo, d_['kr'][:, 128:256], Gb[Ps, c, :], True, False, [RKR, R('Gbh', h_)], [RpO])
                    mm(oo, d_['A3'][:, 1, :], d_['Vt'], False, False, [d_['RA3'], Rtm], [RpO])
                    mm(oo, d_['A3'][:, 2, :], Un[:, h_, :], False, True, [d_['RA3'], R('Un', h_)], [RpO])
                    pg, Rpg = psum()
                    mm(pg[P0:P0 + 64, 0:64], d_['ktt'], d_['Vt'], True, False, [Rtm], [Rpg])
                    mm(pg[P0:P0 + 64, 0:64], d_['btt'], Un[:, h_, :], False, True, [Rtm, R('Un', h_)], [Rpg])
                    pgs.append((pg, Rpg))
                for h_, (d_, (pg, Rpg)) in enumerate(zip(hd, pgs)):
                    Ps, c = d_['Ps'], d_['c']
                    tt(Gst[Ps, l, c, :], pg[Ps, 0:64], Gst[Ps, l, c, :], ALU.add, [Rpg, R('Gh', h_)], [R('Gh', h_)])
                    ts(Gst[Ps, l, c, :], Gst[Ps, l, c, :], gC[Ps, c, j:j + 1], None, ALU.mult, None,
                       [R('Gh', h_), RgC], [R('Gh', h_)])
                    cp(Gb[Ps, c, :], Gst[Ps, l, c, :], [R('Gh', h_)], [R('Gbh', h_)], eng='act')
                ob = Ob[:, 0, :]
                on = Ob[:, 1, :]
                Rob = R('Ob')
                cp(ob, pO[:, 0:256], [RpO], [Rob], eng='act')
                sm = gst[:, 0, 0:4]
                sq = gst[:, 0, 4:8]
                ob3 = ob.rearrange("p (h v) -> p h v", v=64)
                on3 = on.rearrange("p (h v) -> p h v", v=64)
                S.add('dve', lambda e, sm=sm, ob3=ob3: e.tensor_reduce(sm, ob3, AX.X, ALU.add), [Rob], [R('gst')])
                ts(sm, sm, 1.0 / 64, None, ALU.mult, None, [R('gst')], [R('gst')])
                tt(on3, ob3, sm.unsqueeze(2).broadcast_to([128, 4, 64]), ALU.subtract, [Rob, R('gst')], [Rob])
                tt(ob3, on3, on3, ALU.mult, [Rob], [Rob])
                S.add('dve', lambda e, sq=sq, ob3=ob3: e.tensor_reduce(sq, ob3, AX.X, ALU.add), [Rob], [R('gst')])
                ts(sq, sq, 1.0 / 64, GN_EPS, ALU.mult, ALU.add, [R('gst')], [R('gst')])
                act(sq, sq, AF.Sqrt, [R('gst')], [R('gst')])
                S.add('dve', lambda e, sq=sq: e.reciprocal(sq, sq), [R('gst')], [R('gst')])
                tt(on3, on3, sq.unsqueeze(2).broadcast_to([128, 4, 64]), ALU.mult, [Rob, R('gst')], [Rob])
                for c in range(2):
                    pb, Rp = psum()
                    tr(pb[:, 0:128], on[:, c * 128:(c + 1) * 128], [Rob], [Rp])
                    act(od[:, c, sl], pb[:, 0:128], AF.Identity, [Rp, RC], [RG[0]], bias=V_(l, 27 + c), scale=V_(l, 25 + c))
            for c in range(2):
                tt(od[:, c, 0:NT], od[:, c, 0:NT], bonus[:, c, 0:NT], ALU.add, [RG[0], Rbonus], [RG[0]])
                tt(mixb[:, 6 + c, 0:NT], od[:, c, 0:NT], G[3][:, c, 0:NT], ALU.mult, [RG[0], RG[3]], [Rmix(6 + c)])
            if last:
                wo = salloc([2, 128])
                for c in range(2):
                    pb, Rp = psum()
                    tr(pb[0:64, 0:128], Gst[:, l, c, :], [R('Gh', 2 * c), R('Gh', 2 * c + 1)], [Rp])
                    cp(wo[0:64, c, :], pb[0:64, 0:128], [Rp], [R('wo')])
                    S.dma('sp', do['wkv_prompt'][l, 2 * c:2 * c + 2].rearrange("h v k -> v h k"),
                          wo[0:64, c, :].rearrange("v (h k) -> v h k", k=64), reads=[R('wo')])

        def attn_prompt(l, n, ext=False):
            ob_ = salloc([8, NTX], BF16)
            attn_mark[0] = scr_off[0]
            qT = salloc([8, NT], BF16)
            qTS = salloc([8, NS]) if ext else None
            for g in range(2):
                wv, Rw = wload(wsrc('w_xq', l, g * 512, 512), 8, 512)
                for j in range(4):
                    c = g * 4 + j
                    pb, Rp = psum()
                    for k in range(8):
                        mm(pb[:, :], wv[:, k, j * 128:(j + 1) * 128], hb[:, k, 0:NT], k == 0, k == 7, [Rw, R('hb', k)], [Rp])
                    act(qT[:, c, :], pb[:, :], AF.Copy, [Rp], [R('qT', c)], scale=1.0 / 16)
                    if ext:
                        pS, RpS = psum()
                        for k in range(8):
                            mm(pS[:, 0:NS], wv[:, k, j * 128:(j + 1) * 128], hb[:, k, NT:NTX], k == 0, k == 7, [Rw, R('hb', k)], [RpS])
                        act(qTS[:, c, :], pS[:, 0:NS], AF.Copy, [RpS], [R('qTS')], scale=1.0 / 16)
            Pf = salloc([2, 4, 256])
            PT = salloc([2, 4, NT], BF16)
            mx = salloc([2, 8])
            for s_ in range(4):
                q = s_ % 2
                sl = slice(s_ * 128, (s_ + 1) * 128)
                RP = R('Pf', q)
                pbs = []
                for half in range(2):
                    pb, Rp = psum()
                    pbs.append((pb, Rp))
                    for hh in range(2):
                        h_ = half * 2 + hh
                        for i in range(2):
                            mm(pb[:, hh * 256:(hh + 1) * 256], qT[:, 2 * h_ + i, sl], KT[:, l, 2 * h_ + i, :], i == 0, i == 1,
                               [R('qT', 2 * h_ + i), R('KT', l)], [Rp])
                for half in range(2):
                    pb, Rp = pbs[half]
                    S.add('dve', lambda e, q=q, half=half, pb=pb: e.tensor_reduce(
                        mx[:, q, half * 2:half * 2 + 2], pb[:, :].rearrange("p (h m) -> p h m", m=256), AX.X, ALU.max),
                        [Rp], [R('mx', q)])
                ts(mx[:, q, 0:4], mx[:, q, 0:4], -1.0, None, ALU.mult, None, [R('mx', q)], [R('mx', q)])
                memset(mx[:, q, 4:8], 0.0, [R('mx', q)])
                for h_ in range(4):
                    pb, Rp = pbs[h_ // 2]
                    act(Pf[:, q, h_, :], pb[:, (h_ % 2) * 256:(h_ % 2 + 1) * 256], AF.Exp, [Rp, R('mx', q)], [RP, R('mx', q)],
                        bias=mx[:, q, h_:h_ + 1], scale=1.0, accum_out=mx[:, q, 4 + h_:5 + h_])
                S.add('dve', lambda e, q=q: e.reciprocal(mx[:, q, 4:8], mx[:, q, 4:8]), [R('mx', q)], [R('mx', q)])
                tt(Pf[:, q, :, :], Pf[:, q, :, :], mx[:, q, 4:8].unsqueeze(2).broadcast_to([128, 4, 256]), ALU.mult,
                   [RP, R('mx', q)], [RP])
                for h_ in range(4):
                    pb, Rp = psum()
                    for ms in range(2):
                        tr(pb[:, ms * 128:(ms + 1) * 128], Pf[:, q, h_, ms * 128:(ms + 1) * 128], [RP], [Rp])
                    cp(PT[:, :, h_, sl], pb[:, 0:256].rearrange("p (a b) -> p a b", b=128), [Rp], [R('PT', h_)],
                       eng=('act' if h_ % 2 else 'dve'))
            for c in range(8):
                h_ = c // 2
                pb, Rp = psum()
                for ms in range(2):
                    mm(pb[:, :], Vm[:, l, ms, c * 128:(c + 1) * 128], PT[:, ms, h_, :], ms == 0, ms == 1,
                       [R('Vm', l), R('PT', h_)], [Rp])
                cp(ob_[:, c, 0:NT], pb[:, :], [Rp], [R('ob', c)], eng=('act' if c % 2 else 'dve'))
            return ob_, qTS

        def ffn(l, n, ext=False):
            nx = n + NS if ext else n
            hid = salloc([22, NTX], BF16)
            sg = salloc([2, NTX])
            for (c0, cols) in [(g * 256, 256) for g in range(11)]:
                i_ = w_rr[0] % NW
                w_rr[0] += 1
                wv_ = wsl[i_][:, 0:4096].rearrange("p (k c) -> p k c", c=512)
                Rw1 = Rw3 = R('wsl', i_)
                S.dma('pool', wv_[:, :, 0:256], wsrc('ffn_w1', l, c0, cols), writes=[Rw1])
                S.dma('pool', wv_[:, :, 256:512], wsrc('ffn_w3', l, c0, cols), writes=[Rw1])
                wv1 = wv_[:, :, 0:256]
                wv3 = wv_[:, :, 256:512]
                for j in range(cols // 128):
                    c = (c0 + j * 128) // 128
                    q = c % 2
                    pa, Rpa = psum()
                    for k in range(8):
                        mm(pa[:, 0:n], wv1[:, k, j * 128:(j + 1) * 128], hb[:, k, 0:n], k == 0, k == 7, [Rw1, R('hb', k)], [Rpa])
                    pb, Rpb = psum()
                    for k in range(8):
                        mm(pb[:, 0:n], wv3[:, k, j * 128:(j + 1) * 128], hb[:, k, 0:n], k == 0, k == 7, [Rw3, R('hb', k)], [Rpb])
                    act(sg[:, q, 0:n], pa[:, 0:n], AF.Silu, [Rpa], [R('sg', q)])
                    tt(hid[:, c, 0:n], sg[:, q, 0:n], pb[:, 0:n], ALU.mult, [R('sg', q), Rpb], [R('hid', c)])
                    if ext:
                        pS, RpS = psum()
                        for k in range(8):
                            mm(pS[:, 0:NS], wv1[:, k, j * 128:(j + 1) * 128], hb[:, k, n:nx], k == 0, k == 7, [Rw1, R('hb', k)], [RpS])
                        for k in range(8):
                            mm(pS[:, NS:2 * NS], wv3[:, k, j * 128:(j + 1) * 128], hb[:, k, n:nx], k == 0, k == 7, [Rw3, R('hb', k)], [RpS])
                        act(sg[:, q, n:nx], pS[:, 0:NS], AF.Silu, [RpS], [R('sg', q)])
                        tt(hid[:, c, n:nx], sg[:, q, n:nx], pS[:, NS:2 * NS], ALU.mult, [R('sg', q), RpS], [R('hid', c)])
            return hid

        def rows_out(chunks, rchunks, dst, n, tag):
            pb, Rp = psum()
            for i, ch in enumerate(chunks):
                tr(pb[0:n, i * 128:(i + 1) * 128], ch, rchunks, [Rp])
            w = len(chunks) * 128
            ot = salloc([1, w])
            cp(ot[0:n, 0, :], pb[0:n, 0:w], [Rp], [R('rows', tag)])
            S.dma('sp', dst, ot[0:n, 0, :], reads=[R('rows', tag)])

        def mixer_sample_body(l, mixb_full, Rmix):
            n = NS
            mixb = mixb_full[:, :, NT:NTX]
            pj = pjS
            Rpj = R('pjS')
            uv = salloc([4, n])
            Ruv = R('uvS')
            act(uv[:, :, :], pj[:, 0:4, :], AF.Gelu_apprx_tanh, [Rpj], [Ruv])
            sqv = salloc([2, n])
            tt(sqv[:, :, :], uv[:, 2:4, :], uv[:, 2:4, :], ALU.mult, [Ruv], [R('sqvS')])
            pm, Rpm = psum()
            pq, Rpq = psum()
            for c in range(2):
                mm(pm[:, 0:n], cmat[:, 0, :], uv[:, 2 + c, :], c == 0, c == 1, [RC, Ruv], [Rpm])
                mm(pq[:, 0:n], cmat[:, 0, :], sqv[:, c, :], c == 0, c == 1, [RC, R('sqvS')], [Rpq])
            stv = salloc([3, n])
            Rst = R('stvS')
            ts(stv[:, 0, :], pm[:, 0:n], 4.0, None, ALU.mult, None, [Rpm], [Rst])
            ts(stv[:, 1, :], pq[:, 0:n], 4.0, None, ALU.mult, None, [Rpq], [Rst])
            tt(stv[:, 2, :], stv[:, 0, :], stv[:, 0, :], ALU.mult, [Rst], [Rst])
            tt(stv[:, 1, :], stv[:, 1, :], stv[:, 2, :], ALU.subtract, [Rst], [Rst])
            act(stv[:, 1, :], stv[:, 1, :], AF.Sqrt, [Rst], [Rst], bias=LN_EPS, scale=1.0)
            S.add('dve', lambda e: e.reciprocal(stv[:, 1, :], stv[:, 1, :]), [Rst], [Rst])
            vln = salloc([2, n])
            Rvl = R('vlnS')
            ztmp = salloc([2, n])
            for c in range(2):
                tt(vln[:, c, :], uv[:, 2 + c, :], stv[:, 0, :], ALU.subtract, [Ruv, Rst], [Rvl])
                tt(vln[:, c, :], vln[:, c, :], stv[:, 1, :], ALU.mult, [Rvl, Rst], [Rvl])
                act(vln[:, c, :], vln[:, c, :], AF.Identity, [Rvl, RC], [Rvl], bias=V_(l, 82 + c), scale=V_(l, 80 + c))
            rows_out([vln[:, 0, :], vln[:, 1, :]], [Rvl], do['chunk_v_sample'][l], n, 'cv')
            for c in range(2):
                ts(ztmp[:, c, :], vln[:, c, :], V_(l, 84 + c), V_(l, 86 + c), ALU.mult, ALU.add, [Rvl, RC], [R('ztS')])
                tt(mixb[:, c, :], ztmp[:, c, :], uv[:, c, :], ALU.mult, [R('ztS'), Ruv], [Rmix(c)])
            sprow = salloc([2, 256])
            src = di['state_pool'][l].rearrange("s r f -> (s r) f")
            S.dma('sp', sprow[0:128, 0, :], src[0:128, :], writes=[R('sprow')])
            S.dma('sp', sprow[0:112, 1, :], src[128:240, :], writes=[R('sprow')])
            xpS = salloc([2, 16, 16])
            Rxp = R('xpS')
            for c in range(2):
                pb, Rp = psum()
                tr(pb[:, 0:128], sprow[0:128, 0, c * 128:(c + 1) * 128], [R('sprow')], [Rp])
                tr(pb[:, 128:240], sprow[0:112, 1, c * 128:(c + 1) * 128], [R('sprow')], [Rp])
                cp(xpS[:, c, :, 0:15], pb[:, 0:240].rearrange("p (s r) -> p s r", r=15), [Rp], [Rxp])
                cp(xpS[:, c, :, 15:16], pj[:, 4 + c, :].unsqueeze(2), [Rpj], [Rxp])
            wsum = salloc([2, n])
            for (c, p0, win) in [(0, 0, 2), (0, 64, 4), (1, 0, 8), (1, 64, 16)]:
                S.add('dve', lambda e, c=c, p0=p0, win=win: e.tensor_reduce(
                    wsum[p0:p0 + 64, c, :], xpS[p0:p0 + 64, c, :, 16 - win:16], AX.X, ALU.add), [Rxp], [R('wsumS')])
            dS = salloc([2, n], BF16)
            for c in range(2):
                ts(wsum[:, c, :], wsum[:, c, :], V_(l, 77 + c), None, ALU.mult, None, [R('wsumS'), RC], [R('wsumS')])
                tt(dS[:, c, :], wsum[:, c, :], pj[:, 4 + c, :], ALU.subtract, [R('wsumS'), Rpj], [R('dSS')])
                pb, Rp = psum()
                mm(pb[:, 0:n], wpbd[:, l, c, :], dS[:, c, :], True, True, [RC, R('dSS')], [Rp])
                act(mixb[:, 2 + c, :], pb[:, 0:n], AF.Identity, [Rp, RC], [Rmix(2 + c)], scale=V_(l, 13 + c))
            S.dma('sp', do['pool_sample'][l, :, 0:14, :], di['state_pool'][l, :, 1:15, :])
            rows_out([pj[:, 4, :], pj[:, 5, :]], [Rpj], do['pool_sample'][l, :, 14, :], n, 'pl')
            cvrow = salloc([1, 256])
            S.dma('sp', cvrow[0:32, 0, :], di['state_conv'][l].rearrange("s j f -> (s j) f"), writes=[R('cvrow')])
            zc2 = salloc([2, 16, 2])
            zz = salloc([2, n])
            yy = salloc([2, n])
            for c in range(2):
                pb, Rp = psum()
                tr(pb[:, 0:32], cvrow[0:32, 0, c * 128:(c + 1) * 128], [R('cvrow')], [Rp])
                cp(zc2[:, c, :, :], pb[:, 0:32].rearrange("p (s j) -> p s j", j=2), [Rp], [R('zc2S')])
                tt(zz[:, c, :], pj[:, 8 + c, :], pj[:, 10 + c, :], ALU.mult, [Rpj], [R('zzS')])
                ts(yy[:, c, :], zc2[:, c, :, 0], V_(l, 7 + c), None, ALU.mult, None, [R('zc2S'), RC], [R('yyS')])
                stt(yy[:, c, :], zc2[:, c, :, 1], V_(l, 9 + c), yy[:, c, :], ALU.mult, ALU.add, [R('zc2S'), R('yyS'), RC], [R('yyS')])
                stt(yy[:, c, :], zz[:, c, :], V_(l, 11 + c), yy[:, c, :], ALU.mult, ALU.add, [R('zzS'), R('yyS'), RC], [R('yyS')])
                tt(mixb[:, 4 + c, :], yy[:, c, :], pj[:, 6 + c, :], ALU.mult, [R('yyS'), Rpj], [Rmix(4 + c)])
            S.dma('sp', do['conv_sample'][l, :, 0, :], di['state_conv'][l, :, 1, :])
            rows_out([zz[:, 0, :], zz[:, 1, :]], [R('zzS')], do['conv_sample'][l, :, 1, :], n, 'cvo')
            shrow = salloc([1, 896])
            S.dma('sp', shrow[0:16, 0, :], di['state_shift'][l], writes=[R('shrow')])
            prev = salloc([7, n])
            pb, Rp = psum()
            for cc in range(7):
                tr(pb[:, cc * 16:(cc + 1) * 16], shrow[0:16, 0, cc * 128:(cc + 1) * 128], [R('shrow')], [Rp])
            cp(prev[:, :, :], pb[:, 0:112].rearrange("p (c s) -> p c s", s=16), [Rp], [R('prevS')])
            xs = salloc([7, n])
            Rxs = R('xsS')
            for cc in range(7):
                tt(prev[:, cc, :], prev[:, cc, :], pj[:, 12 + cc, :], ALU.subtract, [R('prevS'), Rpj], [R('prevS')])
                stt(xs[:, cc, :], prev[:, cc, :], V_(l, cc), pj[:, 12 + cc, :], ALU.mult, ALU.add, [R('prevS'), Rpj, RC], [Rxs])
            rows_out([pj[:, 12 + i, :] for i in range(4)], [Rpj], do['shift_sample'][l][:, 0:512], n, 'sh0')
            rows_out([pj[:, 16 + i, :] for i in range(3)], [Rpj], do['shift_sample'][l][:, 512:896], n, 'sh1')
            Gs = [salloc([2, n]) for _ in range(8)]
            RGs_ = [(R('GS', i, 0), R('GS', i, 1)) for i in range(8)]
            xr, xk, xv = xs[:, 0:2, :], xs[:, 2:4, :], xs[:, 4:6, :]
            xl = xs[:, 6:7, :]
            pr = wkv_prep_and_scan(l, n, xr, xk, xv, xl, Rxs, Rxs, Rxs, Rxs, Gs, RGs_, None, None, None, None, False)
            lw, kk, bvv, bonus, gv = pr['lw'], pr['kk'], pr['b'], pr['bonus'], pr['g']
            dec = salloc([2, n])
            act(dec[:, :, :], lw[:, :, 0:n], AF.Exp, [pr['Rlw']], [R('decS')])
            tm5 = salloc([5, 256])
            Rtm5 = R('tm5S')
            qsrc = [(kk, pr['Rkk']), (dec, R('decS')), (bvv, pr['Rb']), (xk, Rxs), (xr, Rxs)]
            for qi, (qa, qR) in enumerate(qsrc):
                pb, Rp = psum()
                for c in range(2):
                    tr(pb[0:16, c * 128:(c + 1) * 128], qa[:, c, 0:n], [qR], [Rp])
                cp(tm5[0:16, qi, :], pb[0:16, 0:256], [Rp], [Rtm5], eng=('act' if qi % 2 else 'dve'))
            esel = salloc([16, 128])
            S.dma('sp', esel[0:16, :, :], di['esel'].rearrange("p (a b) -> p a b", b=128), writes=[R('eselS')])
            S2 = salloc([2, 16, 64])
            RS2 = R('S2S')
            for c in range(2):
                S.dma('sp', S2[:, c, :, :], di['state_wkv'][l, :, 2 * c:2 * c + 2].rearrange("s hh v k -> (hh v) s k"), writes=[RS2])
            HS = 8
            bc5 = salloc([5, 2, HS * 64])
            Rbc = R('bc5S')
            tm5v = tm5[0:16, :, :].rearrange("p q (c hh k) -> p q c hh k", hh=2, k=64)
            tS = salloc([2, HS * 64])
            RtS = R('tSS')
            sa = salloc([2, HS])
            osm = salloc([2, n])
            t4 = tS[:, :, :].rearrange("p c (s k) -> p c s k", k=64)
            B4 = lambda q_: bc5[:, q_, :, :].rearrange("p c (s k) -> p c s k", k=64)
            for half in range(2):
                for si in range(HS):
                    s_ = half * HS + si
                    pA, RpA = psum()
                    pB, RpB = psum()
                    for hh in range(2):
                        mm(pA[hh * 64:(hh + 1) * 64, 0:384].rearrange("p (q c k) -> p q c k", c=2, k=64),
                           esel[0:16, s_, 0:64], tm5v[:, 0:3, :, hh, :], True, True, [R('eselS'), Rtm5], [RpA])
                        mm(pB[hh * 64:(hh + 1) * 64, 0:256].rearrange("p (q c k) -> p q c k", c=2, k=64),
                           esel[0:16, s_, 0:64], tm5v[:, 3:5, :, hh, :], True, True, [R('eselS'), Rtm5], [RpB])
                    cp(bc5[:, 0:3, :, si * 64:(si + 1) * 64], pA[:, 0:384].rearrange("p (q c k) -> p q c k", c=2, k=64),
                       [RpA], [Rbc], eng='act')
                    cp(bc5[:, 3:5, :, si * 64:(si + 1) * 64], pB[:, 0:256].rearrange("p (q c k) -> p q c k", c=2, k=64),
                       [RpB], [Rbc])
                ss = slice(half * HS, (half + 1) * HS)
                S4 = S2[:, :, ss, :]
                tt(t4, S4, B4(0), ALU.mult, [RS2, Rbc], [RtS])
                S.add('dve', lambda e: e.tensor_reduce(sa[:, :, :], t4, AX.X, ALU.add), [RtS], [R('saS')])
                tt(S4, S4, B4(1), ALU.mult, [RS2, Rbc], [RS2])
                tt(t4, B4(2), sa[:, :, :].unsqueeze(3).broadcast_to([128, 2, HS, 64]), ALU.mult, [Rbc, R('saS')], [RtS])
                tt(S4, S4, t4, ALU.subtract, [RS2, RtS], [RS2])
                tt(t4, B4(3), xv[:, :, ss].unsqueeze(3).broadcast_to([128, 2, HS, 64]), ALU.mult, [Rbc, Rxs], [RtS])
                tt(S4, S4, t4, ALU.add, [RS2, RtS], [RS2])
                tt(t4, S4, B4(4), ALU.mult, [RS2, Rbc], [RtS])
                S.add('dve', lambda e, ss=ss: e.tensor_reduce(osm[:, :, ss], t4, AX.X, ALU.add), [RtS], [R('osmS')])
            for c in range(2):
                S.dma('sp', do['wkv_sample'][l, :, 2 * c:2 * c + 2].rearrange("s hh v k -> (hh v) s k"), S2[:, c, :, :], reads=[RS2])
            osq = salloc([2, n])
            tt(osq[:, :, :], osm[:, :, :], osm[:, :, :], ALU.mult, [R('osmS')], [R('osqS')])
            gst = salloc([4, n])
            Rgs = R('gstS')
            for c in range(2):
                pm, Rpm = psum()
                mm(pm[:, 0:n], cmat[:, 1, :], osm[:, c, :], True, True, [RC, R('osmS')], [Rpm])
                ts(gst[:, c, :], pm[:, 0:n], 1.0 / 64, None, ALU.mult, None, [Rpm], [Rgs])
                pq, Rpq = psum()
                mm(pq[:, 0:n], cmat[:, 1, :], osq[:, c, :], True, True, [RC, R('osqS')], [Rpq])
                ts(gst[:, 2 + c, :], pq[:, 0:n], 1.0 / 64, None, ALU.mult, None, [Rpq], [Rgs])
            tt(osq[:, :, :], gst[:, 0:2, :], gst[:, 0:2, :], ALU.mult, [Rgs], [R('osqS')])
            tt(gst[:, 2:4, :], gst[:, 2:4, :], osq[:, :, :], ALU.subtract, [Rgs, R('osqS')], [Rgs])
            act(gst[:, 2:4, :], gst[:, 2:4, :], AF.Sqrt, [Rgs], [Rgs], bias=GN_EPS, scale=1.0)
            S.add('dve', lambda e: e.reciprocal(gst[:, 2:4, :], gst[:, 2:4, :]), [Rgs], [Rgs])
            tt(osm[:, :, :], osm[:, :, :], gst[:, 0:2, :], ALU.subtract, [R('osmS'), Rgs], [R('osmS')])
            tt(osm[:, :, :], osm[:, :, :], gst[:, 2:4, :], ALU.mult, [R('osmS'), Rgs], [R('osmS')])
            for c in range(2):
                act(osm[:, c, :], osm[:, c, :], AF.Identity, [R('osmS'), RC], [R('osmS')], bias=V_(l, 27 + c), scale=V_(l, 25 + c))
                tt(osm[:, c, :], osm[:, c, :], bonus[:, c, 0:n], ALU.add, [R('osmS'), pr['Rbonus']], [R('osmS')])
                tt(mixb[:, 6 + c, :], osm[:, c, :], gv[:, c, 0:n], ALU.mult, [R('osmS'), pr['Rg']], [Rmix(6 + c)])
            return

        def attn_sample_body(l, qT, ob_full):
            n = NS
            qtm = salloc([1, D])
            for g in range(2):
                pb, Rp = psum()
                for j in range(4):
                    tr(pb[0:16, j * 128:(j + 1) * 128], qT[:, g * 4 + j, :], [R('qTS')], [Rp])
                cp(qtm[0:16, 0, g * 512:(g + 1) * 512], pb[0:16, :], [Rp], [R('qtmS')], eng=('act' if g else 'dve'))
            esel = salloc([16, 128])
            S.dma('sp', esel[0:16, :, :], di['esel'].rearrange("p (a b) -> p a b", b=128), writes=[R('eselA')])
            Kb = salloc([3, 2, D])
            Vb = Kb
            prod = salloc([1, D])
            scT = salloc([2, 64])
            for s_ in range(NS):
                q = s_ % 3
                for mt in range(2):
                    S.dma('sp', Kb[:, q, mt, :], di['cache_mem_k'][l, s_, mt * 128:(mt + 1) * 128, :], writes=[R('Kb', q, mt)])
                pbs = []
                for g in range(2):
                    pb, Rp = psum()
                    mm(pb[:, :], esel[0:16, s_, :], qtm[0:16, 0, g * 512:(g + 1) * 512], True, True, [R('eselA'), R('qtmS')], [Rp])
                    pbs.append((pb, Rp))
                for mt in range(2):
                    for g in range(2):
                        pb, Rp = pbs[g]
                        tt(prod[:, 0, g * 512:(g + 1) * 512], Kb[:, q, mt, g * 512:(g + 1) * 512], pb[:, :], ALU.mult,
                           [R('Kb', q, mt), Rp], [R('prod')])
                    S.add('dve', lambda e, mt=mt, s_=s_: e.tensor_reduce(
                        scT[:, mt, s_ * 4:(s_ + 1) * 4], prod[:, 0, :].rearrange("p (h d) -> p h d", d=256), AX.X, ALU.add),
                        [R('prod')], [R('scT')])
            psc, Rpsc = psum()
            for mt in range(2):
                tr(psc[0:64, mt * 128:(mt + 1) * 128], scT[:, mt, :], [R('scT')], [Rpsc])
            sm = salloc([1, 256])
            mxs = salloc([1, 2])
            Rsm = R('smS')
            cp(sm[0:64, 0, :], psc[0:64, 0:256], [Rpsc], [Rsm])
            S.add('dve', lambda e: e.tensor_reduce(mxs[0:64, 0, 0:1], sm[0:64, 0, :], AX.X, ALU.max), [Rsm], [R('mxsS')])
            ts(mxs[0:64, 0, 0:1], mxs[0:64, 0, 0:1], -1.0, None, ALU.mult, None, [R('mxsS')], [R('mxsS')])
            memset(mxs[0:64, 0, 1:2], 0.0, [R('mxsS')])
            act(sm[0:64, 0, :], sm[0:64, 0, :], AF.Exp, [Rsm, R('mxsS')], [Rsm, R('mxsS')], bias=mxs[0:64, 0, 0:1], scale=1.0,
                accum_out=mxs[0:64, 0, 1:2])
            S.add('dve', lambda e: e.reciprocal(mxs[0:64, 0, 1:2], mxs[0:64, 0, 1:2]), [R('mxsS')], [R('mxsS')])
            ts(sm[0:64, 0, :], sm[0:64, 0, :], mxs[0:64, 0, 1:2], None, ALU.mult, None, [Rsm, R('mxsS')], [Rsm])
            PTs = salloc([2, 64])
            for mt in range(2):
                pb, Rp = psum()
                tr(pb[:, 0:64], sm[0:64, 0, mt * 128:(mt + 1) * 128], [Rsm], [Rp])
                cp(PTs[:, mt, :], pb[:, 0:64], [Rp], [R('PTsS')], eng=('act' if mt else 'dve'))
            po, Rpo = ps[6], R('ps', 6)
            for s_ in range(NS):
                q = s_ % 3
                for mt in range(2):
                    S.dma('sp', Vb[:, q, mt, :], di['cache_mem_v'][l, s_, mt * 128:(mt + 1) * 128, :], writes=[R('Kb', q, mt)])
                for c in range(8):
                    h_ = c // 2
                    for mt in range(2):
                        mm(po[:, c * 16 + s_:c * 16 + s_ + 1], Vb[:, q, mt, c * 128:(c + 1) * 128],
                           PTs[:, mt, s_ * 4 + h_:s_ * 4 + h_ + 1], mt == 0, mt == 1, [R('Kb', q, mt), R('PTsS')], [Rpo])
            cp(ob_full[:, :, NT:NTX], po[:, 0:128].rearrange("p (c s) -> p c s", s=16), [Rpo], [R('ob', c) for c in range(8)])
            return

        if stop != 'nosetup':
            setup_layers()
        for ti in range(ntile):
            ext = (ti == 0) and do_sample
            phase()
            load_tokens(di['x_prompt'][ti * NT:(ti + 1) * NT, :], NT)
            if ext:
                load_tokens(di['x_sample'], NS, col0=NT)
            for l in range(nlayer):
                phase()
                mixb, Rmix = mixer_prompt(l, ti, ext)
                if ext:
                    S.barrier()
                    scr_off[0] = (8 * NTX + 1) // 2 + 2
                    mixer_sample_body(l, mixb, Rmix)
                dense_res_ln(l, 'w_out', mixb, Rmix, 8, NT, 29, 37, ext)
                phase()
                ob_, qTS = attn_prompt(l, NT, ext)
                if ext:
                    attn_sample_body(l, qTS, ob_)
                S.barrier()
                scr_off[0] = attn_mark[0]
                dense_res_ln(l, 'w_xo', ob_, lambda k: R('ob', k), 8, NT, 45, 53, ext)
                hid = ffn(l, NT, ext)
                dense_res_ln(l, 'ffn_w2', hid, lambda k: R('hid', k), 22, NT, 61, 69, ext)
            phase()
            store_tokens(do['y_prompt'][ti * NT:(ti + 1) * NT, :], NT)
            if ext:
                store_tokens(do['y_sample'], NS, col0=NT)
        build_program.scr_max = scr_max[0]
        S.emit()
    return nc


def _host_pack(inp):
    f = lambda k: np.asarray(inp[k], dtype=np.float32)
    vec = np.zeros((128, L * NVEC), np.float32)
    bvec = np.zeros((128, L * 768), np.float32)
    pc = lambda v: v.reshape(-1, 128).T
    for l in range(L):
        b = l * NVEC
        vec[:, b + 0:b + 7] = pc(f('mu_d')[l])
        cw = f('conv_w')[l]
        for j in range(3):
            vec[:, b + 7 + 2 * j:b + 9 + 2 * j] = pc(cw[j])
        for off, nm in ((13, 'pool_scale'), (15, 'rwkv_w0'), (17, 'rwkv_a0'), (19, 'rwkv_k_k'), (21, 'rwkv_k_a'),
                        (25, 'rwkv_lnx_g'), (27, 'rwkv_lnx_b')):
            vec[:, b + off:b + off + 2] = pc(f(nm)[l])
        vec[:, b + 23:b + 25] = pc(f('rwkv_r_k')[l].reshape(-1))
        for off, nm in ((29, 'ln1_g'), (37, 'ln1_b'), (45, 'ln2_g'), (53, 'ln2_b'), (61, 'ln3_g'), (69, 'ln3_b')):
            vec[:, b + off:b + off + 8] = pc(f(nm)[l])
        vec[:, b + 80:b + 82] = pc(f('ln_v_g')[l])
        vec[:, b + 82:b + 84] = pc(f('ln_v_b')[l])
        wsl0 = f('ws_chunk')[l][:, 0, 0]
        bsl0 = f('b_chunk')[l][:, 0]
        for c in range(2):
            vec[0:64, b + 84 + c] = wsl0[2 * c]
            vec[64:128, b + 84 + c] = wsl0[2 * c + 1]
            vec[0:64, b + 86 + c] = bsl0[2 * c]
            vec[64:128, b + 86 + c] = bsl0[2 * c + 1]
        bb = l * 768
        bvec[:, bb:bb + 256] = f('ln_v_g')[l][None, :]
        bvec[:, bb + 256:bb + 512] = f('ln_v_b')[l][None, :]
        bc = f('b_chunk')[l]
        for c in range(2):
            bvec[0:64, bb + 512 + c * 128:bb + 512 + (c + 1) * 128] = bc[2 * c][None, :]
            bvec[64:128, bb + 512 + c * 128:bb + 512 + (c + 1) * 128] = bc[2 * c + 1][None, :]
    wins = np.zeros((128, 2), np.float32)
    wins[0:64, 0], wins[64:128, 0], wins[0:64, 1], wins[64:128, 1] = 2, 4, 8, 16
    for l in range(L):
        vec[:, l * NVEC + 77:l * NVEC + 79] = 1.0 / wins
    t = np.arange(16, dtype=np.float32)[None, None, :]
    invcnt0 = (1.0 / np.minimum(wins[:, :, None], t + 1)).astype(np.float32).reshape(128, 32)
    ii = np.arange(128)
    ident = np.eye(128, dtype=np.float32)
    su = (ii[:, None] < ii[None, :]).astype(np.float32)
    iu = (ii[:, None] <= ii[None, :]).astype(np.float32)
    sl_ = (ii[:, None] > ii[None, :]).astype(np.float32)
    cmask = np.concatenate([ident, su, iu, sl_], axis=1)
    blk = np.zeros((128, 128), np.float32)
    blk[0:64, 0:64] = 1
    blk[64:128, 64:128] = 1
    cmat = np.concatenate([np.full((128, 128), 1.0 / 1024, np.float32), blk], axis=1)
    wsT = np.ascontiguousarray(np.swapaxes(f('ws_chunk'), 2, 3))
    wp = f('w_pool')
    wpbd = np.zeros((L, 2, 128, 128), np.float32)
    for l in range(L):
        for g in range(4):
            c, hh = g // 2, g % 2
            wpbd[l, c, hh * 64:(hh + 1) * 64, hh * 64:(hh + 1) * 64] = wp[l, g]
    lora = np.concatenate([f('rwkv_w2'), f('rwkv_a2'), f('rwkv_g2')], axis=1)
    esel = np.zeros((16, 16, 128), np.float32)
    for s_ in range(16):
        esel[s_, s_, :] = 1.0
    esel = esel.reshape(16, 16 * 128)
    return dict(esel=esel, vec=vec, bvec=bvec, invcnt0=invcnt0, cmask=cmask, cmat=cmat, wsT=wsT, wpbd=wpbd,
                lora_w=np.ascontiguousarray(lora))


_NC_CACHE = {}


def kernel(**inp):
    f = lambda k: np.ascontiguousarray(np.asarray(inp[k], dtype=np.float32))
    packed = _host_pack(inp)
    shared = {k: f(k) for k in ('w_in', 'w_out', 'w_xq', 'w_xk', 'w_xv', 'w_xo', 'ffn_w1', 'ffn_w3', 'ffn_w2')}
    shared.update(packed)
    if 'nc' not in _NC_CACHE:
        _NC_CACHE['nc'] = build_program()
    nc = _NC_CACHE['nc']
    in_maps = []
    for b in range(8):
        sl = slice(b * NS, (b + 1) * NS)
        m = dict(shared)
        m['x_prompt'] = f('x_prompt')[b]
        m['x_sample'] = f('x_sample')[sl, 0]
        m['mem_prompt'] = f('mem_prompt')[b]
        m['cache_mem_k'] = np.ascontiguousarray(f('cache_mem_k')[:, sl].reshape(L, NS, 256, D))
        m['cache_mem_v'] = np.ascontiguousarray(f('cache_mem_v')[:, sl].reshape(L, NS, 256, D))
        m['state_pool'] = np.ascontiguousarray(f('state_pool')[:, sl])
        m['state_conv'] = np.ascontiguousarray(f('state_conv')[:, sl])
        m['state_shift'] = np.ascontiguousarray(f('state_shift')[:, sl, 0])
        m['state_wkv'] = np.ascontiguousarray(f('state_wkv')[:, sl])
        in_maps.append(m)
    res = run_bass_kernel_spmd(nc, in_maps, core_ids=list(range(8)))
    rs = res.results
    g = lambda k: [np.asarray(r[k]) for r in rs]
    y_prompt = np.stack(g('y_prompt'), 0)
    y_sample = np.concatenate(g('y_sample'), 0)[:, None, :]
    stk1 = lambda k: np.stack(g(k), 1)
    cat1 = lambda k: np.concatenate(g(k), 1)
    outs = (
        y_prompt, y_sample,
        stk1('chunk_v_prompt'), stk1('pool_prompt'), stk1('conv_prompt'), stk1('shift_prompt')[:, :, None, :],
        stk1('wkv_prompt'),
        stk1('mem_k_prompt').reshape(L, 8, 256, 4, 256), stk1('mem_v_prompt').reshape(L, 8, 256, 4, 256),
        cat1('chunk_v_sample')[:, :, None, :], cat1('pool_sample'), cat1('conv_sample'),
        cat1('shift_sample')[:, :, None, :], cat1('wkv_sample'),
    )
    return tuple(np.ascontiguousarray(o, dtype=np.float32) for o in outs)
```
